# Optimizing a Trainium2 kernel written in Bass

```python
import math
import jax, jax.numpy as jnp
from jax import lax
import numpy as np

D_MODEL = 1024
BATCH = 2
SEQ = 8192
DEPTH = 4
DEC_BATCH = 16
DEC_SEQ = 2048
PAST_LEN = 128

HEAD_DIM = 64
A_PATTERNS = ((128, 1), (512, 4), (2048, 16))
A_GROUPS = 3
A_HEADS = 8
A_WIDTH = A_HEADS * HEAD_DIM
B_HEADS = 8
B_KV_HEADS = 2
B_HALF = 128
B_WIDTH = B_HEADS * HEAD_DIM
C_HEADS = 4
C_VDIM = 2 * HEAD_DIM
C_WIDTH = C_HEADS * C_VDIM
N_BRANCH = 3
Q_BLOCK = 128
RMS_EPS = 1e-6
NEG_INF = -1e30
IN_SIZES = (
    A_GROUPS * A_WIDTH, A_GROUPS * A_WIDTH, A_GROUPS * A_WIDTH, A_WIDTH,
    B_WIDTH, B_KV_HEADS * HEAD_DIM, B_KV_HEADS * HEAD_DIM, B_WIDTH,
    2 * C_HEADS * HEAD_DIM, 2 * C_HEADS * HEAD_DIM, C_WIDTH, C_WIDTH,
    N_BRANCH * D_MODEL,
)
D_IN = sum(IN_SIZES)

kernel_name = 'hybrid_dilated_window_diff_encoder'


def rmsnorm(x, g):
    xf = x.astype(jnp.float32)
    y = xf * lax.rsqrt(jnp.mean(xf * xf, axis=-1, keepdims=True) + RMS_EPS)
    return (y * g.astype(jnp.float32)).astype(x.dtype)


def alibi_slopes(n):
    return jnp.asarray([2.0 ** (-8.0 * (i + 1) / n) for i in range(n)], dtype=jnp.float32)


def banded_attention(q, k, v, half, slopes, dist_scale, sink=None):
    n, s_len, h, d = q.shape
    hkv = k.shape[2]
    grp = h // hkv
    blk = half
    nb = -(-s_len // blk)
    pad = nb * blk - s_len
    f32 = jnp.float32
    qp = jnp.pad(q.astype(f32), ((0, 0), (0, pad), (0, 0), (0, 0)))
    kv_pad = ((0, 0), (blk, blk + pad), (0, 0), (0, 0))
    kb = jnp.pad(k.astype(f32), kv_pad).reshape(n, nb + 2, blk, hkv, k.shape[-1])
    vb = jnp.pad(v.astype(f32), kv_pad).reshape(n, nb + 2, blk, hkv, v.shape[-1])
    kw = jnp.concatenate([kb[:, :-2], kb[:, 1:-1], kb[:, 2:]], axis=2)
    vw = jnp.concatenate([vb[:, :-2], vb[:, 1:-1], vb[:, 2:]], axis=2)
    qb = qp.reshape(n, nb, blk, hkv, grp, d)
    s = jnp.einsum('nbqhgd,nbkhd->nbhgqk', qb, kw) * (d ** -0.5)
    koff = jnp.arange(3 * blk) - blk
    rel = koff[None, :] - jnp.arange(blk)[:, None]
    kpos = jnp.arange(nb)[:, None] * blk + koff[None, :]
    mask = (jnp.abs(rel) <= half)[None] & ((kpos >= 0) & (kpos < s_len))[:, None, :]
    bias = -slopes.astype(f32).reshape(hkv, grp, 1, 1) * (jnp.abs(rel).astype(f32) * dist_scale)
    s = jnp.where(mask[None, :, None, None], s + bias, NEG_INF)
    m = jnp.max(s, axis=-1)
    if sink is not None:
        sk = sink.astype(f32).reshape(1, 1, hkv, grp, 1)
        m = jnp.maximum(m, sk)
    e = jnp.exp(s - m[..., None])
    l = jnp.sum(e, axis=-1)
    if sink is not None:
        l = l + jnp.exp(sk - m)
    o = jnp.einsum('nbhgqk,nbkhd->nbqhgd', e, vw) / jnp.moveaxis(l, -1, 2)[..., None]
    o = o.reshape(n, nb * blk, h, v.shape[-1])[:, :s_len]
    lse = jnp.moveaxis(m + jnp.log(l), -1, 2).reshape(n, nb * blk, h)[:, :s_len]
    return o, lse


def dilated_attention(q, k, v):
    b, s_len = q.shape[:2]
    slopes_all = alibi_slopes(A_GROUPS * A_HEADS).reshape(A_GROUPS, A_HEADS)
    outs, lses = [], []
    for gi, (window, dil) in enumerate(A_PATTERNS):
        half = window // (2 * dil)
        sub = s_len // dil

        def to_sub(t):
            t = t[:, :, gi].reshape(b, sub, dil, A_HEADS, t.shape[-1])
            return jnp.swapaxes(t, 1, 2).reshape(b * dil, sub, A_HEADS, t.shape[-1])

        o, lse = banded_attention(to_sub(q), to_sub(k), to_sub(v), half, slopes_all[gi], float(dil))
        o = jnp.swapaxes(o.reshape(b, dil, sub, A_HEADS, HEAD_DIM), 1, 2).reshape(b, s_len, A_HEADS, HEAD_DIM)
        lse = jnp.swapaxes(lse.reshape(b, dil, sub, A_HEADS), 1, 2).reshape(b, s_len, A_HEADS)
        outs.append(o)
        lses.append(lse)
    w = jax.nn.softmax(jnp.stack(lses, 0), axis=0)
    return jnp.einsum('gbsh,gbshd->bshd', w, jnp.stack(outs, 0))


def differential_attention(q, k, v, lam, lam_init, subln_g):
    f32 = jnp.float32
    b, s_len = q.shape[:2]
    nb = s_len // Q_BLOCK
    slopes = alibi_slopes(C_HEADS)
    kpos = jnp.arange(s_len)
    kf = k.astype(f32)
    vf = v.astype(f32)
    qb = jnp.moveaxis(q.astype(f32).reshape(b, nb, Q_BLOCK, C_HEADS, 2, HEAD_DIM), 1, 0)

    def one_block(args):
        qblk, i = args
        qpos = i * Q_BLOCK + jnp.arange(Q_BLOCK)
        bias = -slopes[:, None, None] * jnp.abs(qpos[:, None] - kpos[None, :]).astype(f32)
        s = jnp.einsum('bqhcd,bkhcd->bchqk', qblk, kf) * (HEAD_DIM ** -0.5) + bias
        p = jax.nn.softmax(s, axis=-1)
        a = p[:, 0] - lam * p[:, 1]
        return jnp.einsum('bhqk,bkhd->bqhd', a, vf)

    o = lax.map(one_block, (qb, jnp.arange(nb)))
    o = jnp.moveaxis(o, 0, 1).reshape(b, s_len, C_HEADS, C_VDIM)
    return rmsnorm(o, subln_g) * (1.0 - lam_init)


def hybrid_layer(x, layer_idx, norm_g, w_in, w_oa, w_ob, w_oc, w_out, b_sink,
                 lam_q1, lam_k1, lam_q2, lam_k2, c_subln_g):
    b, s_len, _ = x.shape
    h = rmsnorm(x, norm_g)
    proj = h @ w_in
    splits = [int(c) for c in np.cumsum(IN_SIZES)[:-1]]
    qa, ka, va, ga, qb, kb, vb, gb, qc, kc, vc, gc, gm = jnp.split(proj, splits, axis=-1)
    a5 = lambda t: t.reshape(b, s_len, A_GROUPS, A_HEADS, HEAD_DIM)
    oa = dilated_attention(a5(qa), a5(ka), a5(va)).reshape(b, s_len, A_WIDTH).astype(x.dtype)
    ya = (oa * jax.nn.silu(ga)) @ w_oa
    ob, _ = banded_attention(qb.reshape(b, s_len, B_HEADS, HEAD_DIM),
                             kb.reshape(b, s_len, B_KV_HEADS, HEAD_DIM),
                             vb.reshape(b, s_len, B_KV_HEADS, HEAD_DIM),
                             B_HALF, alibi_slopes(B_HEADS), 1.0, sink=b_sink)
    ob = ob.reshape(b, s_len, B_WIDTH).astype(x.dtype)
    yb = (ob * jax.nn.silu(gb)) @ w_ob
    lam_init = 0.8 - 0.6 * math.exp(-0.3 * layer_idx)
    f32 = jnp.float32
    lam = (jnp.exp(jnp.sum(lam_q1.astype(f32) * lam_k1.astype(f32)))
           - jnp.exp(jnp.sum(lam_q2.astype(f32) * lam_k2.astype(f32))) + lam_init)
    oc = differential_attention(qc.reshape(b, s_len, C_HEADS, 2, HEAD_DIM),
                                kc.reshape(b, s_len, C_HEADS, 2, HEAD_DIM),
                                vc.reshape(b, s_len, C_HEADS, C_VDIM), lam, lam_init, c_subln_g)
    oc = oc.reshape(b, s_len, C_WIDTH).astype(x.dtype)
    yc = (oc * jax.nn.silu(gc)) @ w_oc
    gates = jax.nn.sigmoid(gm.reshape(b, s_len, N_BRANCH, D_MODEL))
    mixed = gates[:, :, 0] * ya + gates[:, :, 1] * yb + gates[:, :, 2] * yc
    return x + mixed @ w_out


def setup_inputs(seed: int = 0) -> dict:
    key = jax.random.key(seed)
    ks = jax.random.split(key, 16)
    f32 = jnp.float32

    def nrm(k, shape, scale):
        return jax.random.normal(k, shape, f32) * scale

    return {
        'x_prompt': nrm(ks[0], (BATCH, SEQ, D_MODEL), 1.0),
        'x_sample': nrm(ks[1], (DEC_BATCH, DEC_SEQ, D_MODEL), 1.0),
        'norm_g': 1.0 + nrm(ks[2], (DEPTH, D_MODEL), 0.02),
        'w_in': nrm(ks[3], (DEPTH, D_MODEL, D_IN), D_MODEL ** -0.5),
        'w_oa': nrm(ks[4], (DEPTH, A_WIDTH, D_MODEL), A_WIDTH ** -0.5),
        'w_ob': nrm(ks[5], (DEPTH, B_WIDTH, D_MODEL), B_WIDTH ** -0.5),
        'w_oc': nrm(ks[6], (DEPTH, C_WIDTH, D_MODEL), C_WIDTH ** -0.5),
        'w_out': nrm(ks[7], (DEPTH, D_MODEL, D_MODEL), D_MODEL ** -0.5),
        'b_sink': nrm(ks[8], (DEPTH, B_HEADS), 0.5),
        'lam_q1': nrm(ks[9], (DEPTH, HEAD_DIM), 0.1),
        'lam_k1': nrm(ks[10], (DEPTH, HEAD_DIM), 0.1),
        'lam_q2': nrm(ks[11], (DEPTH, HEAD_DIM), 0.1),
        'lam_k2': nrm(ks[12], (DEPTH, HEAD_DIM), 0.1),
        'c_subln_g': 1.0 + nrm(ks[13], (DEPTH, C_VDIM), 0.02),
        'final_norm_g': 1.0 + nrm(ks[14], (D_MODEL,), 0.02),
    }


def reference(x_prompt, x_sample, norm_g, w_in, w_oa, w_ob, w_oc, w_out, b_sink,
              lam_q1, lam_k1, lam_q2, lam_k2, c_subln_g, final_norm_g):
    def trunk(x):
        for l in range(DEPTH):
            x = hybrid_layer(x, l, norm_g[l], w_in[l], w_oa[l], w_ob[l], w_oc[l], w_out[l], b_sink[l],
                             lam_q1[l], lam_k1[l], lam_q2[l], lam_k2[l], c_subln_g[l])
        return rmsnorm(x, final_norm_g)

    y_prompt = trunk(x_prompt)
    y_sample = trunk(x_sample)
    return (y_prompt, y_sample)
```

```python
import math
from contextlib import ExitStack

import numpy as np
import ml_dtypes

import concourse.bass as bass
import concourse.mybir as mybir
from concourse.bass_utils import run_bass_kernel_spmd

F32 = mybir.dt.float32
BF16 = mybir.dt.bfloat16
AF = mybir.ActivationFunctionType
ALU = mybir.AluOpType
AX = mybir.AxisListType

NT = 8192
D = 1024
DIN = 11520
DEPTH = 4
NCORES = 8
NEG = -30000.0
EPS = 1e-6
C_SKIP = 144.0
A_PAT = ((128, 1), (512, 4), (2048, 16))

SEGS = [
    ("qa", 0, 1536, "q"), ("ka", 1536, 1536, "k"), ("va", 3072, 1536, "v"), ("ga", 4608, 512, "silu"),
    ("qb", 5120, 512, "q"), ("kb", 5632, 128, "k"), ("vb", 5760, 128, "v"), ("gb", 5888, 512, "silu"),
    ("qc", 6400, 512, "q"), ("kc", 6912, 512, "k"), ("vc", 7424, 512, "v"), ("gc", 7936, 512, "silu"),
    ("gm", 8448, 3072, "sigmoid"),
]


class Sem:
    def __init__(self, h):
        self.h = h
        self.n = 0


class Ring:
    def __init__(self, n):
        self.n = n
        self.i = 0
        self.free = [None] * n

    def next(self):
        i = self.i
        self.i = (i + 1) % self.n
        return i, self.free[i]


class Prog:
    ENG = ("sync", "scalar", "gpsimd", "vector", "tensor")

    def __init__(self, nc, stack):
        self.nc = nc
        self.stack = stack
        self.q = {e: [] for e in self.ENG}
        self.waited = {e: {} for e in self.ENG}
        self.sems = {}
        self.uid = 0
        self.engsem = {}
        self.last = {e: None for e in self.ENG}

    def sem(self, name):
        if name not in self.sems:
            self.sems[name] = Sem(self.stack.enter_context(self.nc.semaphore("s_" + name)))
        return self.sems[name]

    def name(self, base):
        self.uid += 1
        return f"{base}_{self.uid}"

    def emit(self, eng, fn, waits=(), sig=None, amt=1, chain=None, is_dma=False):
        comp = fn is not None and not is_dma and eng in self.engsem
        if comp:
            if sig is None:
                sig = self.engsem[eng]
            if chain is None:
                chain = True
            if chain and self.last[eng] is not None:
                waits = list(waits) + [self.last[eng]]
        ws = []
        for t in waits:
            if t is None:
                continue
            sem, val = t
            if self.waited[eng].get(sem, 0) >= val:
                continue
            self.waited[eng][sem] = val
            ws.append((sem.h, val))
        tok = None
        if sig is not None:
            sig.n += amt
            tok = (sig, sig.n)
        if comp:
            self.last[eng] = tok
        self.q[eng].append((ws, fn, sig.h if sig is not None else None, amt))
        return tok

    def dma(self, eng, out, in_, waits=(), sig=None):
        return self.emit(eng, lambda e: e.dma_start(out=out, in_=in_), waits, sig, 16, is_dma=True)

    def wait(self, eng, toks):
        self.emit(eng, None, toks)

    def wait_sems(self, eng, sems):
        self.emit(eng, None, [(s_, s_.n) for s_ in sems if s_.n > 0])

    def flush(self, scope=None, barrier=True):
        if scope is not None:
            with self.nc.named_scope(scope):
                self._flush(barrier)
        else:
            self._flush(barrier)

    def _flush(self, barrier=True):
        with self.nc.Block() as block:
            for name in self.ENG:
                items = self.q[name]

                def body(e, items=items):
                    for ws, fn, sh, amt in items:
                        for h, v in ws:
                            e.wait_ge(h, v)
                        if fn is not None:
                            ins = fn(e)
                            if sh is not None:
                                ins.then_inc(sh, amt)

                getattr(block, name)(body)
        if barrier:
            self.nc.all_engine_barrier()
        self.q = {e: [] for e in self.ENG}


def MM(out, lhsT, rhs, start=True, stop=True):
    return lambda e: e.matmul(out, lhsT=lhsT, rhs=rhs, start=start, stop=stop)


def TR(out, in_, ident):
    return lambda e: e.transpose(out, in_, ident)


def ACT(out, in_, func, scale=1.0, bias=None, accum_out=None):
    kw = {}
    if bias is not None:
        kw["bias"] = bias
    if accum_out is not None:
        kw["accum_out"] = accum_out
    return lambda e: e.activation(out=out, in_=in_, func=func, scale=scale, **kw)


def TT(out, in0, in1, op):
    return lambda e: e.tensor_tensor(out=out, in0=in0, in1=in1, op=op)


def TS(out, in0, s1, op0, s2=None, op1=None):
    if op1 is None:
        return lambda e: e.tensor_scalar(out=out, in0=in0, scalar1=s1, scalar2=None, op0=op0)
    return lambda e: e.tensor_scalar(out=out, in0=in0, scalar1=s1, scalar2=s2, op0=op0, op1=op1)


def STT(out, in0, scalar, in1, op0, op1):
    return lambda e: e.scalar_tensor_tensor(out=out, in0=in0, scalar=scalar, in1=in1, op0=op0, op1=op1)


def CP(out, in_):
    return lambda e: e.tensor_copy(out=out, in_=in_)


def RCP(out, in_):
    return lambda e: e.reciprocal(out=out, in_=in_)


def MSET(ap, v):
    return lambda e: e.memset(ap, v)


def ssl(start, n, step):
    if step == 1:
        return slice(start, start + n)
    return slice(start, start + step * (n - 1) + 1, step)


def bcast_rows(ap2d_row, nparts):
    a = ap2d_row
    return bass.AP(a.tensor, a.offset, [[0, nparts]] + [list(x) for x in a.ap[1:]])


def bf16(a):
    return np.asarray(a, np.float32).astype(ml_dtypes.bfloat16)


def make_consts():
    c = {}
    kp = np.arange(128)[:, None].astype(np.float64)
    cq = np.arange(384)[None, :].astype(np.float64)
    delta = (cq - 128.0) - kp
    bB = np.empty((128, 8, 384), np.float64)
    for h in range(8):
        slope = 2.0 ** (-(h + 1))
        bB[:, h, :] = np.where(np.abs(delta) <= 128, -slope * np.abs(delta), NEG)
    c["c_bB"] = bB.reshape(128, 8 * 384).astype(np.float32)
    cq = np.arange(256)[None, :]
    cc = cq // 128
    qq = (cq % 128).astype(np.float64)
    delta = qq - kp + np.where(cc == 0, -64.0, 64.0)
    bA = np.empty((128, 24, 256), np.float64)
    for g, (_, dil) in enumerate(A_PAT):
        for h in range(8):
            slope = np.float32(2.0 ** (-8.0 * (8 * g + h + 1) / 24))
            bA[:, g * 8 + h, :] = np.where(np.abs(delta) <= 64, -(np.float64(slope) * dil) * np.abs(delta), NEG)
    c["c_bA"] = bA.reshape(128, 24 * 256).astype(np.float32)
    q1 = np.arange(128)[None, :].astype(np.float64)
    bC = np.empty((128, 4, 128), np.float64)
    for h in range(4):
        m = 2.0 ** (-2.0 * (h + 1))
        bC[:, h, :] = -m * np.abs(q1 - kp)
    c["c_bC"] = bf16(bC.reshape(128, 512))
    c["c_id"] = bf16(np.eye(128))
    return c


def make_aug(seg_ids, masked):
    t = np.arange(NT)
    A = (t // 128).astype(np.float64)
    b = (t % 128).astype(np.float64)
    oh = np.zeros((4, NT), np.float64)
    oh[seg_ids, t] = 1.0
    qa = np.zeros((3, 8, NT), np.float64)
    qa[:, 0:4, :] = oh[None]
    al = np.stack([A, b, np.ones(NT), np.ones(NT)])
    qa[0, 4:8] = al
    qa[1, 4:8] = -al
    ka = np.zeros((4, 8, NT), np.float64)
    if masked:
        ka[:, 0:4, :] = (NEG * (1.0 - oh))[None]
    for h in range(4):
        m = 2.0 ** (-2.0 * (h + 1))
        ka[h, 4] = -128.0 * m
        ka[h, 5] = -m
        ka[h, 6] = 128.0 * m * A
        ka[h, 7] = m * b
    return bf16(qa), bf16(ka)


def build(depth=DEPTH, dbg=None, stop_after=None):
    nc = bass.Bass("TRN2", target_bir_lowering=False)
    dbg = dbg or ()

    def din(name, shape, dt=F32):
        return nc.dram_tensor(name, list(shape), dt, kind="ExternalInput").ap()

    def dscr(name, shape, dt=BF16):
        kind = {"kind": "ExternalOutput"} if name in dbg else {}
        return nc.dram_tensor(name, list(shape), dt, **kind).ap()

    x_in = din("x", [NT, D])
    y_out = nc.dram_tensor("y", [NT, D], F32, kind="ExternalOutput").ap()
    norm_g = din("norm_g", [DEPTH, D])
    w_in = din("w_in", [DEPTH, D, DIN])
    w_oa = din("w_oa", [DEPTH, 512, D])
    w_ob = din("w_ob", [DEPTH, 512, D])
    w_oc = din("w_oc", [DEPTH, 512, D])
    w_out = din("w_out", [DEPTH, D, D])
    b_sink = din("b_sink", [DEPTH, 8])
    lam_q1 = din("lam_q1", [DEPTH, 64])
    lam_k1 = din("lam_k1", [DEPTH, 64])
    lam_q2 = din("lam_q2", [DEPTH, 64])
    lam_k2 = din("lam_k2", [DEPTH, 64])
    subln_g = din("c_subln_g", [DEPTH, 128])
    final_g = din("final_norm_g", [1, D])
    c_bB = din("c_bB", [128, 8 * 384], F32)
    c_bA = din("c_bA", [128, 24 * 256], F32)
    c_bC = din("c_bC", [128, 512], BF16)
    c_id = din("c_id", [128, 128], BF16)
    c_qaug = din("c_qaug", [3, 8, NT], BF16)
    c_kaug = din("c_kaug", [4, 8, NT], BF16)

    wb_in = dscr("wb_in", [depth, D, DIN])
    wb_o = [dscr(f"wb_o{i}", [depth, 512, D]) for i in range(3)]
    wb_out = dscr("wb_out", [depth, D, D])
    XR = dscr("XR", [NT, D], F32)
    QaT = dscr("QaT", [1536, NT]); KaT = dscr("KaT", [1536, NT]); Va = dscr("Va", [NT, 1536])
    QbT = dscr("QbT", [512, NT]); KbT = dscr("KbT", [128, NT]); Vb = dscr("Vb", [NT, 128])
    QcT = dscr("QcT", [512, NT]); KcT = dscr("KcT", [512, NT]); Vc = dscr("Vc", [NT, 512])
    SGa = dscr("SGa", [512, NT]); SGb = dscr("SGb", [512, NT]); SGc = dscr("SGc", [512, NT])
    SGm = dscr("SGm", [3072, NT])
    OGa = dscr("OGa", [512, NT]); OGb = dscr("OGb", [512, NT]); OGc = dscr("OGc", [512, NT])
    DEST = {"qa": QaT, "ka": KaT, "va": Va, "ga": SGa, "qb": QbT, "kb": KbT, "vb": Vb, "gb": SGb,
            "qc": QcT, "kc": KcT, "vc": Vc, "gc": SGc, "gm": SGm}

    with ExitStack() as gst:
        P = Prog(nc, gst)
        S_pe, S_act, S_dve, S_pool = P.sem("pe"), P.sem("act"), P.sem("dve"), P.sem("pool")
        P.engsem = {"scalar": S_act, "vector": S_dve, "gpsimd": S_pool}

        def sbuf(st, base, shape, dt):
            return st.enter_context(nc.sbuf_tensor(P.name(base), list(shape), dt))

        def psum(st, base, shape, dt):
            return st.enter_context(nc.psum_tensor(P.name(base), list(shape), dt))

        cast_tok = {}

        def phase_cast():
            for l in range(depth):
                S = P.sem(f"cast{l}")
                for r in range(8):
                    P.dma("gpsimd", wb_in[l, r * 128:(r + 1) * 128, :], w_in[l, r * 128:(r + 1) * 128, :], sig=S)
                for i, w in enumerate((w_oa, w_ob, w_oc)):
                    P.dma("gpsimd", wb_o[i][l, :, :], w[l, :, :], sig=S)
                P.dma("gpsimd", wb_out[l, :, :], w_out[l, :, :], sig=S)
                cast_tok[l] = (S, S.n)
            P.flush("cast", barrier=False)

        def phase_T(l):
            first = l == 0
            last = l == depth
            Xsrc = x_in if l <= 1 else XR
            with ExitStack() as st:
                ps = psum(st, "psT", [128, 6, 512], F32)
                pt = psum(st, "ptT", [128, 2, 1024], BF16)
                ring = Ring(6)
                ident = sbuf(st, "ident", [128, 128], BF16)
                gb = sbuf(st, "gb", [128, D], F32)
                xts = [sbuf(st, "xt", [128, 4, D], F32) for _ in range(2)]
                junk = sbuf(st, "junk", [128, D], BF16)
                ssq = sbuf(st, "ssq", [128, 32], F32)
                epsb = sbuf(st, "epsb", [128, 1], F32)
                S_c = P.sem("T_const"); S_x = [P.sem("T_x0"), P.sem("T_x1")]; S_xst = P.sem("T_xst")
                P.wait("sync", [cast_tok[k] for k in range(min(l, depth - 1) + 1)])
                P.dma("sync", ident[:], c_id[:, :], sig=S_c)
                gsrc = final_g[0:1, :] if last else norm_g[l:l + 1, :]
                P.dma("sync", gb[:], bcast_rows(gsrc, 128), sig=S_c)
                P.emit("vector", MSET(epsb[:], EPS), sig=S_dve)
                t_eps = (S_dve, S_dve.n)
                if not first:
                    ogs = [sbuf(st, "og", [128, 3, 4, 512], BF16) for _ in range(2)]
                    gm = [sbuf(st, "gm", [128, 3, 512], BF16) for _ in range(2)]
                    tmp = [[sbuf(st, "tmp", [128, 512], F32) for _ in range(3)] for _ in range(2)]
                    mixed = sbuf(st, "mixed", [128, 8, 512], BF16)
                    Wo = [sbuf(st, "Wo", [128, 4, D], BF16) for _ in range(3)]
                    wout = sbuf(st, "wout", [128, 8, D], BF16)
                    for i in range(3):
                        P.dma("sync", Wo[i][:], wb_o[i][l - 1].rearrange("(k p) f -> p k f", p=128), sig=S_c)
                    P.dma("sync", wout[:], wb_out[l - 1].rearrange("(k p) f -> p k f", p=128), sig=S_c)
                    S_og = [P.sem("T_og0"), P.sem("T_og1")]; S_gm = [P.sem("T_gm0"), P.sem("T_gm1")]
                if not last:
                    hT = sbuf(st, "hT", [128, 8, 2048], BF16)
                    hb = sbuf(st, "hb", [128, 4, D], BF16)
                    Wt = [sbuf(st, "Wt", [128, 8, 512], BF16) for _ in range(2)]
                    stF = [sbuf(st, "stF", [128, 2048], BF16) for _ in range(3)]
                    stV = [sbuf(st, "stV", [128, 4, 512], BF16) for _ in range(2)]
                    S_w = [P.sem("T_w0"), P.sem("T_w1")]
                    S_stF = [P.sem(f"T_stF{i}") for i in range(3)]
                    S_stV = [P.sem(f"T_stV{i}") for i in range(2)]
                t_c = (S_c, S_c.n)
                state = dict(og_free=[None, None], mixed_free=None, x_free=[[], []], gm_free=[None, None], hb_free=None,
                             w_free=[None, None], stF_free=[None] * 3, stV_free=[None] * 2, wi=0, fi=0, vi=0,
                             pt_free=[None, None], pti=0, stores=[], ld={}, tmp_free=[None, None], ssq_free=[None, None])

                def T1_load(tt):
                    par = tt % 2
                    xsl = slice(tt * 512, tt * 512 + 512)
                    t_x = P.dma("sync", xts[par][:], Xsrc[xsl, :].rearrange("(s p) f -> p s f", p=128),
                                waits=state["x_free"][par], sig=S_x[par])
                    t_og = None
                    if not first:
                        for b_, src in enumerate((OGa, OGb, OGc)):
                            P.dma("sync", ogs[par][:, b_], src.rearrange("(k p) t -> p k t", p=128)[:, :, xsl],
                                  waits=[state["og_free"][par]], sig=S_og[par])
                        t_og = (S_og[par], S_og[par].n)
                    state["ld"][tt] = (t_x, t_og)

                def T1(tt):
                    tok0 = tt * 512
                    par = tt % 2
                    xt = xts[par]
                    xsl = slice(tok0, tok0 + 512)
                    t_x, t_og = state["ld"].pop(tt)
                    if first and tt + 1 < 16:
                        T1_load(tt + 1)
                    x_ready = t_x
                    if not first:
                        og = ogs[par]
                        gsrc_all = SGm.rearrange("(b f p) t -> p b f t", b=3, p=128)
                        t_mixed = None
                        for fc in range(8):
                            sl = fc % 2
                            t_gm = P.dma("sync", gm[sl][:], gsrc_all[:, :, fc, xsl],
                                         waits=[state["gm_free"][sl]], sig=S_gm[sl])
                            if fc == 3 and tt + 1 < 16:
                                T1_load(tt + 1)
                            bks = []
                            for b in range(3):
                                bi, bfree = ring.next()
                                for kc in range(4):
                                    tk = P.emit("tensor", MM(ps[:, bi, :], Wo[b][:, kc, fc * 128:(fc + 1) * 128],
                                                             og[:, b, kc, :], kc == 0, kc == 3),
                                                waits=[bfree, t_og, t_c], sig=S_pe if kc == 3 else None)
                                bks.append((bi, tk))
                            if fc == 7:
                                state["og_free"][par] = bks[-1][1]
                            for b in range(3):
                                bi, tk = bks[b]
                                td = P.emit("vector", TT(tmp[sl][b][:], ps[:, bi, :], gm[sl][:, b, :], ALU.mult),
                                            waits=[tk, t_gm, state["tmp_free"][sl]], sig=S_dve, chain=False)
                                ring.free[bi] = td
                            state["gm_free"][sl] = td
                            w = [td]
                            if fc == 0:
                                w.append(state["mixed_free"])
                            P.emit("gpsimd", TT(tmp[sl][0][:], tmp[sl][0][:], tmp[sl][1][:], ALU.add), waits=w, sig=S_pool, chain=False)
                            t_mixed = P.emit("gpsimd", TT(mixed[:, fc, :], tmp[sl][0][:], tmp[sl][2][:], ALU.add), sig=S_pool)
                            state["tmp_free"][sl] = t_mixed
                        for sub in range(4):
                            for fh in range(2):
                                bi, bfree = ring.next()
                                for kc in range(8):
                                    tk = P.emit("tensor", MM(ps[:, bi, :], mixed[:, kc, sub * 128:(sub + 1) * 128],
                                                             wout[:, kc, fh * 512:(fh + 1) * 512], kc == 0, kc == 7),
                                                waits=[bfree, t_mixed], sig=S_pe if kc == 7 else None)
                                td = P.emit("vector", TT(xt[:, sub, fh * 512:(fh + 1) * 512], ps[:, bi, :],
                                                         xt[:, sub, fh * 512:(fh + 1) * 512], ALU.add),
                                            waits=[tk, t_x], sig=S_dve, chain=False)
                                ring.free[bi] = td
                        state["mixed_free"] = tk
                        x_ready = td
                    frees = []
                    if not first and not last:
                        t_st = P.dma("gpsimd", XR[xsl, :].rearrange("(s p) f -> p s f", p=128), xt[:],
                                     waits=[x_ready], sig=S_xst)
                        frees.append(t_st)
                    q0 = 16 * par
                    for sub in range(4):
                        t_sq = P.emit("scalar", ACT(junk[:], xt[:, sub, :], AF.Square, accum_out=ssq[:, q0 + sub:q0 + sub + 1]),
                                      waits=[x_ready, state["ssq_free"][par]], sig=S_act, chain=False)
                    P.emit("scalar", ACT(ssq[:, q0 + 4:q0 + 8], ssq[:, q0:q0 + 4], AF.Ln, scale=1.0 / D, bias=epsb[:, 0:1]), waits=[t_eps])
                    t_r = P.emit("scalar", ACT(ssq[:, q0 + 8:q0 + 12], ssq[:, q0 + 4:q0 + 8], AF.Exp, scale=-0.5), sig=S_act)
                    if last:
                        for sub in range(4):
                            td = P.emit("vector", STT(xt[:, sub, :], xt[:, sub, :], ssq[:, q0 + 8 + sub:q0 + 9 + sub], gb[:],
                                                      ALU.mult, ALU.mult), waits=[t_r, t_c], sig=S_dve, chain=False)
                        t_st = P.dma("gpsimd", y_out[xsl, :].rearrange("(s p) f -> p s f", p=128), xt[:],
                                     waits=[td], sig=S_xst)
                        state["x_free"][par] = [t_st]
                        state["ssq_free"][par] = td
                        return
                    for sub in range(4):
                        w = [t_r, t_c]
                        if sub == 0:
                            w.append(state["hb_free"])
                        td = P.emit("vector", STT(hb[:, sub, :], xt[:, sub, :], ssq[:, q0 + 8 + sub:q0 + 9 + sub], gb[:],
                                                  ALU.mult, ALU.mult), waits=w, sig=S_dve, chain=False)
                        pi = state["pti"]; state["pti"] = 1 - pi
                        for kc in range(8):
                            tk = P.emit("tensor", TR(pt[:, pi, kc * 128:(kc + 1) * 128], hb[:, sub, kc * 128:(kc + 1) * 128], ident[:]),
                                        waits=[td, state["pt_free"][pi], t_c], sig=S_pe if kc == 7 else None)
                        off = (tt % 4) * 512 + sub * 128
                        ta = P.emit("scalar", ACT(hT[:, :, off:off + 128], pt[:, pi, :].rearrange("p (k t) -> p k t", k=8), AF.Copy),
                                    waits=[tk], sig=S_act, chain=False)
                        state["pt_free"][pi] = ta
                    state["hb_free"] = tk
                    state["hT_ready"] = ta
                    state["ssq_free"][par] = td
                    frees += [t_sq, td]
                    state["x_free"][par] = frees

                wtiles = [(name, c0, kind, w0, min(512, ncols - w0)) for (name, c0, ncols, kind) in SEGS
                          for w0 in range(0, ncols, 512)]
                NW = len(wtiles)
                w_tok = {}

                def load_w(k):
                    name, c0, kind, w0, wc = wtiles[k % NW]
                    wi = k % 2
                    w_tok[k] = P.dma("sync", Wt[wi][:, :, 0:wc],
                                     wb_in[l].rearrange("(k p) c -> p k c", p=128)[:, :, c0 + w0:c0 + w0 + wc],
                                     waits=[state["w_free"][wi]], sig=S_w[wi])

                def T2(s):
                    tsl = slice(s * 2048, (s + 1) * 2048)
                    t_h = state["hT_ready"]
                    for ti, (name, c0, kind, w0, wc) in enumerate(wtiles):
                        dest = DEST[name]
                        k = s * NW + ti
                        wi = k % 2
                        if k + 1 < 4 * NW:
                            load_w(k + 1)
                        t_w = w_tok[k]
                        if kind == "v":
                            for s16 in range(16):
                                vi = state["vi"]
                                bi, bfree = ring.next()
                                for kc in range(8):
                                    tk = P.emit("tensor", MM(ps[:, bi, 0:wc], hT[:, kc, s16 * 128:(s16 + 1) * 128],
                                                             Wt[wi][:, kc, 0:wc], kc == 0, kc == 7),
                                                waits=[bfree, t_w, t_h], sig=S_pe if kc == 7 else None)
                                w = [tk]
                                if s16 % 4 == 0:
                                    w.append(state["stV_free"][vi])
                                ta = P.emit("scalar", ACT(stV[vi][:, s16 % 4, 0:wc], ps[:, bi, 0:wc], AF.Copy),
                                            waits=w, sig=S_act, chain=False)
                                ring.free[bi] = ta
                                if s16 % 4 == 3:
                                    r0 = s * 2048 + (s16 // 4) * 512
                                    t_st = P.dma("gpsimd", dest[r0:r0 + 512, w0:w0 + wc].rearrange("(s p) c -> p s c", p=128),
                                                 stV[vi][:, :, 0:wc], waits=[ta], sig=S_stV[vi])
                                    state["stV_free"][vi] = t_st
                                    state["vi"] = 1 - vi
                            state["w_free"][wi] = tk
                            continue
                        for sc in range(wc // 128):
                            fi = state["fi"]; state["fi"] = (fi + 1) % 3
                            for i in range(4):
                                bi, bfree = ring.next()
                                for kc in range(8):
                                    tk = P.emit("tensor", MM(ps[:, bi, :], Wt[wi][:, kc, sc * 128:(sc + 1) * 128],
                                                             hT[:, kc, i * 512:(i + 1) * 512], kc == 0, kc == 7),
                                                waits=[bfree, t_w, t_h], sig=S_pe if kc == 7 else None)
                                w = [tk]
                                if i == 0:
                                    w.append(state["stF_free"][fi])
                                o = stF[fi][:, i * 512:(i + 1) * 512]
                                if kind == "q":
                                    te = P.emit("vector", TS(o, ps[:, bi, :], 0.125, ALU.mult), waits=w, sig=S_dve, chain=False)
                                elif kind == "k":
                                    te = P.emit("vector", CP(o, ps[:, bi, :]), waits=w, sig=S_dve, chain=False)
                                elif kind == "silu":
                                    te = P.emit("scalar", ACT(o, ps[:, bi, :], AF.Silu), waits=w, sig=S_act, chain=False)
                                else:
                                    te = P.emit("scalar", ACT(o, ps[:, bi, :], AF.Sigmoid), waits=w, sig=S_act, chain=False)
                                ring.free[bi] = te
                            f0 = w0 + sc * 128
                            t_st = P.dma("gpsimd", dest[f0:f0 + 128, tsl], stF[fi][:], waits=[te], sig=S_stF[fi])
                            state["stF_free"][fi] = t_st
                        state["w_free"][wi] = tk

                if not last:
                    load_w(0)
                T1_load(0)
                for s in range(4):
                    for i in range(4):
                        T1(4 * s + i)
                    if not last:
                        T2(s)
                st_sems = [S_xst]
                if not last:
                    st_sems += S_stF + S_stV
                for e_ in ("sync", "gpsimd", "scalar", "vector", "tensor"):
                    P.wait_sems(e_, st_sems)
                P.flush(f"T{l}")

        def run_banded(res, kbs):
            ps_s, ps_a, PT, SB = res["ps_s"], res["ps_a"], res["PT"], res["SB"]
            NPT = res["NPT"]
            G = 2
            ngroups = (len(kbs) + G - 1) // G
            s_free = res["s_free"]
            sb_free = res["sb_free"]
            pt_free = res["pt_free"]
            a_free = res["a_free"]
            exp_tok = {}
            qk_tok = {}
            bias_tok = {}

            def do_qk(gi):
                sl = gi % 2
                tk = None
                members = kbs[gi * G:(gi + 1) * G]
                for m, kb in enumerate(members):
                    for qi_, (lo, hi, lhsT, rhs) in enumerate(kb["qk"]):
                        islast = (qi_ == len(kb["qk"]) - 1) and (m == len(members) - 1)
                        tk = P.emit("tensor", MM(ps_s[:, sl * G + m, lo:hi], lhsT, rhs, True, True),
                                    waits=[s_free[sl]] + list(kb["waits"]), sig=S_pe if islast else None)
                qk_tok[gi] = tk

            def do_bias(gi):
                sl = gi % 2
                members = kbs[gi * G:(gi + 1) * G]
                cols = members[0]["cols"]
                n = len(members)
                bap = members[0]["bias"]
                bb = bass.AP(bap.tensor, bap.offset, [list(bap.ap[0]), [0, n], list(bap.ap[-1])])
                td = P.emit("vector", TT(SB[:, sl, 0:n, 0:cols], ps_s[:, sl * G:sl * G + n, 0:cols], bb, ALU.add),
                            waits=[qk_tok[gi], sb_free[sl]] + list(members[0]["waits"]), sig=S_dve, chain=False)
                bias_tok[gi] = td
                s_free[sl] = td

            def do_exp(gi):
                sl = gi % 2
                members = kbs[gi * G:(gi + 1) * G]
                cols = members[0]["cols"]
                n = len(members)
                slots = [(gi * G + m) % NPT for m in range(n)]
                w = [bias_tok[gi]] + [pt_free[s_] for s_ in slots]
                ta = P.emit("scalar", ACT(PT[:, slots[0]:slots[0] + n, 0:cols], SB[:, sl, 0:n, 0:cols], AF.Exp),
                            waits=w, sig=S_act, chain=False)
                exp_tok[gi] = ta
                sb_free[sl] = ta

            def do_av(gi):
                members = kbs[gi * G:(gi + 1) * G]
                for m, kb in enumerate(members):
                    for job in kb["av"]:
                        bank, slot = job["bank"], job["slot"]
                        qa, qb = job["qa"], job["qb"]
                        np_ = len(job["parts"])
                        for pi_, (kidx, c, vap) in enumerate(job["parts"]):
                            w = [exp_tok[kidx // G]]
                            if pi_ == 0 and job["bank_first"]:
                                w += list(a_free[bank] or ())
                            w += list(job.get("waits", ()))
                            tk = P.emit("tensor", MM(ps_a[:, bank, slot * 128 + qa:slot * 128 + qb], vap,
                                                     PT[:, kidx % NPT, c * 128 + qa:c * 128 + qb], pi_ == 0, pi_ == np_ - 1),
                                        waits=w, sig=S_pe if (pi_ == np_ - 1) else None)
                            pt_free[kidx % NPT] = tk
                        if job["bank_done"]:
                            a_free[bank] = job["evac"](tk)

            for step in range(ngroups + 2):
                if step < ngroups:
                    do_qk(step)
                    do_bias(step)
                    do_exp(step)
                if step >= 2:
                    do_av(step - 2)

        def phase_B(l):
            with ExitStack() as st:
                ps_s = psum(st, "psBs", [128, 4, 512], F32)
                ps_a = psum(st, "psBa", [128, 2, 512], F32)
                NPT = 8
                PT = sbuf(st, "PT", [128, NPT, 384], BF16)
                ident = sbuf(st, "ident", [128, 128], BF16)
                bias = sbuf(st, "biasB", [128, 8, 384], F32)
                SB = sbuf(st, "SBb", [128, 2, 2, 384], F32)
                QT = [sbuf(st, "QTb", [68, NT], BF16) for _ in range(2)]
                KT = [sbuf(st, "KTb", [68, NT], BF16) for _ in range(2)]
                Vg = sbuf(st, "Vb", [128, 64, 2, 128], BF16)
                sg = [sbuf(st, "sgb", [64, NT], BF16) for _ in range(2)]
                esink = sbuf(st, "esink", [128, 8], F32)
                rec = [sbuf(st, "recB", [64, 512], F32) for _ in range(2)]
                of = [sbuf(st, "ofB", [64, 512], F32) for _ in range(2)]
                rec_free = [None, None]
                stg = [sbuf(st, "stgB", [64, 512], BF16) for _ in range(2)]
                S_c = P.sem("B_c"); S_q = [P.sem("B_q0"), P.sem("B_q1")]; S_st = [P.sem("B_st0"), P.sem("B_st1")]
                P.dma("sync", ident[:], c_id[:, :], sig=S_c)
                P.dma("sync", bias[:], c_bB.rearrange("p (h c) -> p h c", h=8), sig=S_c)
                P.dma("sync", esink[:], bcast_rows(b_sink[l:l + 1, :], 128), sig=S_c)
                for kv in range(2):
                    P.dma("sync", KT[kv][0:64, :], KbT[kv * 64:(kv + 1) * 64, :], sig=S_c)
                    P.dma("sync", KT[kv][64:68, :], c_kaug[0, 0:4, :], sig=S_c)
                    P.dma("sync", QT[kv][64:68, :], c_qaug[2, 0:4, :], sig=S_c)
                P.emit("gpsimd", MSET(Vg[:], 1.0), sig=S_pool)
                t_ms = (S_pool, S_pool.n)
                vsrc = Vb.rearrange("(j p) (k d) -> p j k d", p=128, k=2)
                for q4 in range(4):
                    for kv in range(2):
                        P.dma("sync", Vg[:, q4 * 16:(q4 + 1) * 16, kv, 0:64], vsrc[:, q4 * 16:(q4 + 1) * 16, kv, :],
                              waits=[t_ms], sig=S_c)
                t_c = (S_c, S_c.n)
                t_es = P.emit("scalar", ACT(esink[:], esink[:], AF.Exp), waits=[t_c], sig=S_act)
                res = dict(ps_s=ps_s, ps_a=ps_a, PT=PT, SB=SB, NPT=NPT, s_free=[None, None], sb_free=[None, None],
                           pt_free=[None] * NPT, a_free=[None, None])
                q_free = [None, None]
                st_free = [None, None]
                stores = []
                sti = [0]
                tq = {}

                def load_head(h):
                    qi = h % 2
                    P.dma("sync", QT[qi][0:64, :], QbT[h * 64:(h + 1) * 64, :], waits=[q_free[qi]], sig=S_q[qi])
                    P.dma("sync", sg[qi][:], SGb[h * 64:(h + 1) * 64, :], waits=[q_free[qi]], sig=S_q[qi])
                    tq[h] = (S_q[qi], S_q[qi].n)

                load_head(0)
                for h in range(8):
                    kv = h // 4
                    qi = h % 2
                    if h + 1 < 8:
                        load_head(h + 1)
                    t_q = tq[h]
                    kbs = []
                    last_tok = [None]

                    def mk_evac(bank, q0, h=h, qi=qi):
                        def evac(tk):
                            tsl = slice(q0 * 128, q0 * 128 + 512)
                            ri = sti[0]; sti[0] = 1 - ri
                            ta0 = P.emit("scalar", ACT(rec[ri][:], ps_a[64:128, bank, :], AF.Ln, bias=esink[64:128, h:h + 1]),
                                         waits=[tk, t_es, rec_free[ri]], sig=S_act, chain=False)
                            ta = P.emit("scalar", ACT(rec[ri][:], rec[ri][:], AF.Exp, scale=-1.0), sig=S_act)
                            td = P.emit("vector", TT(of[ri][:], ps_a[0:64, bank, :], sg[qi][:, tsl], ALU.mult),
                                        waits=[tk, t_q, rec_free[ri]], sig=S_dve, chain=False)
                            te = P.emit("gpsimd", TT(stg[ri][:], of[ri][:], rec[ri][:], ALU.mult),
                                        waits=[ta, td, st_free[ri]], sig=S_pool, chain=False)
                            rec_free[ri] = te
                            t_st = P.dma("gpsimd", OGb[h * 64:(h + 1) * 64, tsl], stg[ri][:], waits=[te], sig=S_st[ri])
                            st_free[ri] = t_st
                            stores.append(t_st)
                            last_tok[0] = te
                            return [ta0, td]
                        return evac

                    for j in range(64):
                        qlo = max(j - 1, 0); qhi = min(j + 1, 63)
                        lo = (qlo - (j - 1)) * 128; hi = (qhi - (j - 1) + 1) * 128
                        kb = dict(qk=[(lo, hi, KT[kv][0:68, j * 128:(j + 1) * 128], QT[qi][0:68, qlo * 128:(qhi + 1) * 128])],
                                  bias=bias[:, h, :], cols=384, waits=[t_q, t_c], av=[])
                        done = []
                        if j >= 1:
                            done.append(j - 1)
                        if j == 63:
                            done.append(63)
                        for qb_ in done:
                            parts = [(jj, qb_ - jj + 1, Vg[:, jj, kv, :]) for jj in (qb_ - 1, qb_, qb_ + 1) if 0 <= jj < 64]
                            bank = (qb_ // 4) % 2
                            job = dict(parts=parts, qa=0, qb=128, slot=qb_ % 4, bank=bank, bank_first=(qb_ % 4 == 0),
                                       bank_done=(qb_ % 4 == 3), waits=[t_c])
                            if job["bank_done"]:
                                job["evac"] = mk_evac(bank, qb_ - 3)
                            kb["av"].append(job)
                        kbs.append(kb)
                    run_banded(res, kbs)
                    q_free[qi] = last_tok[0]
                for e_ in ("sync", "gpsimd", "scalar", "vector", "tensor"):
                    P.wait_sems(e_, S_st)
                P.flush(f"B{l}")

        def phase_A(l):
            with ExitStack() as st:
                ps_s = psum(st, "psAs", [128, 4, 512], F32)
                ps_a = psum(st, "psAa", [128, 2, 512], F32)
                NPT = 8
                PT = sbuf(st, "PTa", [128, NPT, 256], BF16)
                ident = sbuf(st, "ident", [128, 128], BF16)
                bA = sbuf(st, "bA", [128, 24, 256], F32)
                SB = sbuf(st, "SBa", [128, 2, 2, 256], F32)
                QT = [sbuf(st, "QTa", [68, NT], BF16) for _ in range(2)]
                KT = [sbuf(st, "KTa", [68, NT], BF16) for _ in range(2)]
                Vg = [sbuf(st, "Va", [128, 64, 128], BF16) for _ in range(2)]
                accs = [sbuf(st, "accA", [128, NT], F32) for _ in range(2)]
                sgt = [sbuf(st, "sga", [64, 512], BF16) for _ in range(2)]
                rec = [sbuf(st, "recA", [64, 512], F32) for _ in range(2)]
                of = [sbuf(st, "ofA", [64, 512], F32) for _ in range(2)]
                rec_free = [None, None]
                stg = [sbuf(st, "stgA", [64, 512], BF16) for _ in range(2)]
                S_c = P.sem("A_c"); S_q = [P.sem("A_q0"), P.sem("A_q1")]
                S_st = [P.sem("A_st0"), P.sem("A_st1")]; S_sg = [P.sem("A_sg0"), P.sem("A_sg1")]
                P.dma("sync", ident[:], c_id[:, :], sig=S_c)
                P.dma("sync", bA[:], c_bA.rearrange("p (h c) -> p h c", h=24), sig=S_c)
                for i in range(2):
                    P.dma("sync", KT[i][64:68, :], c_kaug[0, 0:4, :], sig=S_c)
                    P.dma("sync", QT[i][64:68, :], c_qaug[2, 0:4, :], sig=S_c)
                    P.emit("gpsimd", MSET(Vg[i][:], 1.0), sig=S_pool)
                t_ms = (S_pool, S_pool.n)
                t_c = (S_c, S_c.n)
                res = dict(ps_s=ps_s, ps_a=ps_a, PT=PT, SB=SB, NPT=NPT, s_free=[None, None], sb_free=[None, None],
                           pt_free=[None] * NPT, a_free=[None, None])
                q_free = [None, None]
                st_free = [None, None]
                sg_free = [None, None]
                stores = []
                acc_free = [None, None]
                ui = 0
                tq = {}

                def load_unit(u):
                    h, g = divmod(u, 3)
                    dil = A_PAT[g][1]
                    qi = u % 2
                    f0 = g * 512 + h * 64
                    P.dma("sync", QT[qi][0:64, :], QaT[f0:f0 + 64, :], waits=[q_free[qi]], sig=S_q[qi])
                    P.dma("sync", KT[qi][0:64, :], KaT[f0:f0 + 64, :], waits=[q_free[qi]], sig=S_q[qi])
                    U = NT // dil
                    nb = U // 128
                    vsrc = Va.rearrange("(u d) c -> d u c", d=dil)
                    for r in range(dil):
                        vr = vsrc[r, :, f0:f0 + 64].rearrange("(j p) c -> p j c", p=128)
                        for j0 in range(0, nb, 16):
                            j1 = min(nb, j0 + 16)
                            P.dma("sync", Vg[qi][:, r * nb + j0:r * nb + j1, 0:64], vr[:, j0:j1, :],
                                  waits=[q_free[qi], t_ms], sig=S_q[qi])
                    tq[u] = (S_q[qi], S_q[qi].n)

                load_unit(0)
                for h in range(8):
                    last_acc = None
                    acc = accs[h % 2]
                    for g, (_, dil) in enumerate(A_PAT):
                        qi = ui % 2
                        if ui + 1 < 24:
                            load_unit(ui + 1)
                        t_q = tq[ui]
                        ui += 1
                        f0 = g * 512 + h * 64
                        U = NT // dil
                        nb = U // 128
                        kbs = []
                        last_tok = [None]

                        def mk_evac(bank, lo, hi, t0, n, g=g, dil=dil, acc=acc, h=h):
                            def evac(tk):
                                nonlocal last_acc
                                dst = acc[:, ssl(t0, n, dil)]
                                w = [tk]
                                if g == 0:
                                    w += list(acc_free[h % 2] or ())
                                    td = P.emit("vector", CP(dst, ps_a[:, bank, lo:hi]), waits=w, sig=S_dve, chain=False)
                                else:
                                    td = P.emit("vector", TT(dst, ps_a[:, bank, lo:hi], dst, ALU.add), waits=w, sig=S_dve, chain=False)
                                last_tok[0] = td
                                last_acc = td
                                return [td]
                            return evac

                        kidx = 0
                        for r in range(dil):
                            for j in range(nb):
                                tb = r + dil * 128 * j
                                ulo = max(128 * j - 64, 0); uhi = min(128 * j + 192, U)
                                lo = ulo - (128 * j - 64); hi = uhi - (128 * j - 64)
                                kb = dict(qk=[(lo, hi, KT[qi][0:68, ssl(tb, 128, dil)],
                                               QT[qi][0:68, ssl(r + dil * ulo, uhi - ulo, dil)])],
                                          bias=bA[:, g * 8 + h, :], cols=256, waits=[t_q, t_c], av=[])
                                done = [j - 1]
                                if j == nb - 1:
                                    done.append(nb - 1)
                                for qb_ in done:
                                    parts = []
                                    for jj in (qb_, qb_ + 1):
                                        if 0 <= jj < nb:
                                            parts.append((kidx - (j - jj), qb_ - jj + 1, Vg[qi][:, r * nb + jj, :]))
                                    qa = 64 if qb_ == -1 else 0
                                    qbb = 64 if qb_ == nb - 1 else 128
                                    seq = qb_ + 1
                                    bank = (seq // 4) % 2
                                    slot = seq % 4
                                    bank_done = (slot == 3) or (qb_ == nb - 1)
                                    job = dict(parts=parts, qa=qa, qb=qbb, slot=slot, bank=bank, bank_first=(slot == 0),
                                               bank_done=bank_done, waits=[t_c])
                                    if bank_done:
                                        first_qb = qb_ - slot
                                        c_lo = 64 if first_qb == -1 else 0
                                        c_hi = slot * 128 + qbb
                                        u0 = 128 * first_qb + 64 + c_lo
                                        job["evac"] = mk_evac(bank, c_lo, c_hi, r + dil * u0, c_hi - c_lo)
                                    kb["av"].append(job)
                                kbs.append(kb)
                                kidx += 1
                        run_banded(res, kbs)
                        q_free[qi] = last_tok[0]
                    for c in range(16):
                        tsl = slice(c * 512, (c + 1) * 512)
                        si = c % 2
                        t_sg = P.dma("sync", sgt[si][:], SGa[h * 64:(h + 1) * 64, tsl], waits=[sg_free[si]], sig=S_sg[si])
                        P.emit("scalar", ACT(rec[si][:], acc[64:128, tsl], AF.Ln), waits=[last_acc, rec_free[si]], sig=S_act, chain=False)
                        ta = P.emit("scalar", ACT(rec[si][:], rec[si][:], AF.Exp, scale=-1.0), sig=S_act)
                        tp = P.emit("gpsimd", TT(of[si][:], acc[0:64, tsl], sgt[si][:], ALU.mult),
                                    waits=[last_acc, t_sg, rec_free[si]], sig=S_pool, chain=False)
                        te = P.emit("gpsimd", TT(stg[si][:], of[si][:], rec[si][:], ALU.mult), waits=[ta, st_free[si]], sig=S_pool)
                        sg_free[si] = te
                        rec_free[si] = te
                        t_st = P.dma("gpsimd", OGa[h * 64:(h + 1) * 64, tsl], stg[si][:], waits=[te], sig=S_st[si])
                        st_free[si] = t_st
                        stores.append(t_st)
                    acc_free[h % 2] = [te, ta]
                for e_ in ("sync", "gpsimd", "scalar", "vector", "tensor"):
                    P.wait_sems(e_, S_st)
                P.flush(f"A{l}")

        def phase_C2(l):
            lam_init = 0.8 - 0.6 * math.exp(-0.3 * l)
            with ExitStack() as st:
                ps_s = psum(st, "psCs", [128, 4, 512], F32)
                ps_a = psum(st, "psCa", [128, 4, 512], F32)
                NPT = 6
                PT = [sbuf(st, "PTc", [128, 2, 512], BF16) for _ in range(NPT)]
                LD = [[sbuf(st, "LD", [128, 512], F32) for _ in range(2)] for _ in range(2)]
                LP = [[sbuf(st, "LP", [128, 512], F32) for _ in range(2)] for _ in range(2)]
                Lt = sbuf(st, "Lt", [128, 512], F32)
                Lhi = sbuf(st, "Lhi", [128, 512], BF16)
                Llo = sbuf(st, "Llo", [128, 512], BF16)
                l_free = [None, None]
                ident = sbuf(st, "ident", [128, 128], BF16)
                ones = sbuf(st, "ones", [128, 128], BF16)
                bC = sbuf(st, "bC", [128, 4, 128], BF16)
                KT = [[sbuf(st, "KTc", [72, NT], BF16) for _ in range(2)] for _ in range(2)]
                Vh = [sbuf(st, "Vc", [128, 64, 128], BF16) for _ in range(2)]
                QTt = [[[sbuf(st, "QTc", [72, 512], BF16) for _ in range(3)] for _ in range(2)] for _ in range(2)]
                sgt = [sbuf(st, "sgc", [128, 512], BF16) for _ in range(2)]
                r1 = sbuf(st, "r1", [128, 512], F32)
                o1 = sbuf(st, "o1", [128, 512], F32)
                o2 = sbuf(st, "o2", [128, 512], F32)
                oo = sbuf(st, "oo", [128, 512], F32)
                sq = sbuf(st, "sq", [128, 512], BF16)
                rstd = sbuf(st, "rstd", [128, 512], F32)
                stg = [sbuf(st, "stgC", [128, 512], BF16) for _ in range(2)]
                lam = sbuf(st, "lam", [128, 4, 64], F32)
                lsc = sbuf(st, "lsc", [128, 8], F32)
                coef = sbuf(st, "coef", [128, 1], F32)
                epsb = sbuf(st, "epsbC", [128, 1], F32)
                S_c = P.sem("C_c"); S_k = [P.sem("C_k0"), P.sem("C_k1")]; S_q = [P.sem("C_q0"), P.sem("C_q1")]
                S_st = [P.sem("C_st0"), P.sem("C_st1")]
                P.dma("sync", ident[:], c_id[:, :], sig=S_c)
                P.dma("sync", bC[:], c_bC.rearrange("p (h c) -> p h c", h=4), sig=S_c)
                for i, v in enumerate((lam_q1, lam_k1, lam_q2, lam_k2)):
                    P.dma("sync", lam[:, i, :], bcast_rows(v[l:l + 1, :], 128), sig=S_c)
                P.dma("sync", coef[:], subln_g[l:l + 1, :].rearrange("a d -> d a"), sig=S_c)
                t_c = (S_c, S_c.n)
                P.emit("vector", MSET(ones[:], 1.0))
                P.emit("vector", MSET(epsb[:], EPS))
                P.emit("vector", TT(lam[:, 0, :], lam[:, 0, :], lam[:, 1, :], ALU.mult), waits=[t_c])
                P.emit("vector", TT(lam[:, 2, :], lam[:, 2, :], lam[:, 3, :], ALU.mult))
                P.emit("vector", lambda e: e.reduce_sum(out=lsc[:, 0:1], in_=lam[:, 0, :], axis=AX.X))
                td = P.emit("vector", lambda e: e.reduce_sum(out=lsc[:, 1:2], in_=lam[:, 2, :], axis=AX.X), sig=S_dve)
                ta = P.emit("scalar", ACT(lsc[:, 2:4], lsc[:, 0:2], AF.Exp), waits=[td], sig=S_act)
                P.emit("vector", TT(lsc[:, 4:5], lsc[:, 3:4], lsc[:, 2:3], ALU.subtract), waits=[ta])
                P.emit("vector", TS(lsc[:, 5:6], lsc[:, 4:5], -lam_init, ALU.add))
                t_l = P.emit("vector", TS(coef[:], coef[:], 1.0 - lam_init, ALU.mult), sig=S_dve)
                neglam = lsc[:, 5:6]

                s_free = [None, None]
                pt_free = [None] * NPT
                a_free = [None] * 4
                k_free = [None, None]
                q_free = [None, None]
                sg_free = [None, None]
                st_free = [None, None]
                stores = []

                units = [(h, c) for h in range(4) for c in range(16)]
                loads = {}

                def load_head(h):
                    kb_ = h % 2
                    for m in range(2):
                        f0 = h * 128 + m * 64
                        P.dma("sync", KT[kb_][m][0:64, :], KcT[f0:f0 + 64, :], waits=[k_free[kb_]], sig=S_k[kb_])
                        P.dma("sync", KT[kb_][m][64:72, :], c_kaug[h, :, :], waits=[k_free[kb_]], sig=S_k[kb_])
                    vsrc = Vc[:, h * 128:(h + 1) * 128].rearrange("(j p) d -> p j d", p=128)
                    for q4 in range(4):
                        P.dma("sync", Vh[kb_][:, q4 * 16:(q4 + 1) * 16, :], vsrc[:, q4 * 16:(q4 + 1) * 16, :],
                              waits=[k_free[kb_]], sig=S_k[kb_])
                    loads[("k", h)] = (S_k[kb_], S_k[kb_].n)

                def load_unit(ui):
                    h, c = units[ui]
                    qb_ = ui % 2
                    csl = slice(c * 512, (c + 1) * 512)
                    for m in range(2):
                        f0 = h * 128 + m * 64
                        for ver in range(3):
                            P.dma("sync", QTt[qb_][m][ver][0:64, :], QcT[f0:f0 + 64, csl], waits=[q_free[qb_], sg_free[qb_]], sig=S_q[qb_])
                            P.dma("sync", QTt[qb_][m][ver][64:72, :], c_qaug[ver, :, csl], waits=[q_free[qb_]], sig=S_q[qb_])
                    P.dma("sync", sgt[qb_][:], SGc[h * 128:(h + 1) * 128, csl], waits=[q_free[qb_], sg_free[qb_]], sig=S_q[qb_])
                    loads[("q", ui)] = (S_q[qb_], S_q[qb_].n)

                def unit_range(h, c):
                    m_ = 2.0 ** (-2.0 * (h + 1))
                    js = []
                    for jb in range(64):
                        if jb < 4 * c:
                            dmin = 512 * c - (128 * jb + 127)
                        elif jb > 4 * c + 3:
                            dmin = 128 * jb - (512 * c + 511)
                        else:
                            dmin = 0
                        if m_ * dmin < C_SKIP:
                            js.append(jb)
                    return js[0] // 2, js[-1] // 2

                urange = [unit_range(h, c) for (h, c) in units]
                groups = [(ui, m, g) for ui in range(len(units)) for m in range(2)
                          for g in range(urange[ui][0], urange[ui][1] + 1)]
                NG = len(groups)
                qk_tok = {}
                exp_tok = {}
                pending_ss = []
                unit_state = {}

                def do_qk(G):
                    ui, m, g = groups[G]
                    h, c = units[ui]
                    kb_, qb_ = h % 2, ui % 2
                    sl = G % 2
                    w0 = [s_free[sl], loads[("k", h)], loads[("q", ui)], t_c, t_l]
                    tk = None
                    for mm_ in range(2):
                        jb = 2 * g + mm_
                        out = ps_s[:, sl * 2 + mm_, :]
                        lhsT = KT[kb_][m][0:72, jb * 128:(jb + 1) * 128]
                        Q = QTt[qb_][m]
                        sig = S_pe if mm_ == 1 else None
                        if jb < 4 * c:
                            tk = P.emit("tensor", MM(out, lhsT, Q[0][0:72, :]), waits=w0, sig=sig)
                        elif jb > 4 * c + 3:
                            tk = P.emit("tensor", MM(out, lhsT, Q[1][0:72, :]), waits=w0, sig=sig)
                        else:
                            a = jb - 4 * c
                            if a > 0:
                                P.emit("tensor", MM(ps_s[:, sl * 2 + mm_, 0:a * 128], lhsT, Q[1][0:72, 0:a * 128]), waits=w0)
                            P.emit("tensor", MM(ps_s[:, sl * 2 + mm_, a * 128:(a + 1) * 128], lhsT, Q[2][0:72, a * 128:(a + 1) * 128], True, False), waits=w0)
                            tk = P.emit("tensor", MM(ps_s[:, sl * 2 + mm_, a * 128:(a + 1) * 128], ident[:], bC[:, h, :], False, True),
                                        sig=sig if a == 3 else None)
                            if a < 3:
                                tk = P.emit("tensor", MM(ps_s[:, sl * 2 + mm_, (a + 1) * 128:512], lhsT, Q[0][0:72, (a + 1) * 128:512]),
                                            waits=w0, sig=sig)
                    qk_tok[G] = tk

                def do_exp(G):
                    sl = G % 2
                    pi = G % NPT
                    ta = P.emit("scalar", ACT(PT[pi][:], ps_s[:, sl * 2:sl * 2 + 2, :], AF.Exp),
                                waits=[qk_tok[G]] + list(pt_free[pi] or ()), sig=S_act, chain=False)
                    exp_tok[G] = ta
                    s_free[sl] = ta

                lstate = {}
                pending_L = []

                def do_av(G):
                    ui, m, g = groups[G]
                    h, c = units[ui]
                    kb_, qb_ = h % 2, ui % 2
                    pi = G % NPT
                    bo, bl_ = 2 * m, 2 * m + 1
                    tk = None
                    glo, ghi = urange[ui]
                    jfirst, jlast = 2 * glo, 2 * ghi + 1
                    for mm_ in range(2):
                        jb = 2 * g + mm_
                        w = [exp_tok[G]]
                        if jb == jfirst:
                            w += [a_free[bo]]
                        tk = P.emit("tensor", MM(ps_a[:, bo, :], Vh[kb_][:, jb, :], PT[pi][:, mm_, :], jb == jfirst, jb == jlast),
                                    waits=w, sig=S_pe if mm_ == 1 else None)
                    k = (2 * ui + m) % 2
                    stt = lstate.setdefault((ui, m), dict(tok={"vector": [None, None], "gpsimd": [None, None]},
                                                         cnt={"vector": 0, "gpsimd": 0}))
                    eng = "gpsimd" if (g - glo) % 4 == 3 else "vector"
                    accs_ = (LP if eng == "gpsimd" else LD)[k]
                    sem_ = S_pool if eng == "gpsimd" else S_dve
                    tl = None
                    for mm_ in range(2):
                        n_ = stt["cnt"][eng]
                        par = n_ % 2
                        src = PT[pi][:, mm_, :]
                        if n_ < 2:
                            tl = P.emit(eng, CP(accs_[par][:], src), waits=[exp_tok[G], l_free[k]], sig=sem_, chain=False)
                        else:
                            tl = P.emit(eng, TT(accs_[par][:], src, accs_[par][:], ALU.add),
                                        waits=[exp_tok[G], stt["tok"][eng][par]], sig=sem_, chain=False)
                        stt["tok"][eng][par] = tl
                        stt["cnt"][eng] = n_ + 1
                    pt_free[pi] = [tk, tl]
                    if g == ghi:
                        pending_L.append((ui, m, tk))
                        if m == 1:
                            q_free[qb_] = tk
                            if c == 15:
                                k_free[kb_] = tk

                def do_L():
                    ui, m, tk_o = pending_L.pop(0)
                    h, c = units[ui]
                    k = (2 * ui + m) % 2
                    bo, bl_ = 2 * m, 2 * m + 1
                    stt = lstate.pop((ui, m))
                    parts = []
                    for eng, accs_ in (("vector", LD[k]), ("gpsimd", LP[k])):
                        for par in range(2):
                            if stt["cnt"][eng] > par:
                                parts.append((accs_[par], stt["tok"][eng][par]))
                    toks = [t_ for _, t_ in parts]
                    first_ = True
                    if len(parts) == 1:
                        P.emit("vector", CP(Lt[:], parts[0][0][:]), waits=toks)
                    else:
                        P.emit("vector", TT(Lt[:], parts[0][0][:], parts[1][0][:], ALU.add), waits=toks)
                        for (ap_, _) in parts[2:]:
                            P.emit("vector", TT(Lt[:], Lt[:], ap_[:], ALU.add))
                    P.emit("vector", CP(Lhi[:], Lt[:]))
                    td0 = P.emit("vector", TT(Llo[:], Lt[:], Lhi[:], ALU.subtract), sig=S_dve)
                    l_free[k] = td0
                    P.emit("tensor", MM(ps_a[:, bl_, :], ones[:], Lhi[:], True, False), waits=[td0, a_free[bl_]])
                    tk = P.emit("tensor", MM(ps_a[:, bl_, :], ones[:], Llo[:], False, True), sig=S_pe)
                    if m == 0:
                        P.emit("vector", RCP(r1[:], ps_a[:, 1, :]), waits=[tk, tk_o])
                        td = P.emit("vector", TT(o1[:], ps_a[:, 0, :], r1[:], ALU.mult), sig=S_dve)
                        a_free[0] = td; a_free[1] = td
                    else:
                        P.emit("vector", RCP(r1[:], ps_a[:, 3, :]), waits=[tk, tk_o])
                        P.emit("vector", TT(o2[:], ps_a[:, 2, :], r1[:], ALU.mult))
                        P.emit("vector", STT(oo[:], o2[:], neglam, o1[:], ALU.mult, ALU.add), waits=[t_l])
                        td = P.emit("vector", TT(sq[:], oo[:], oo[:], ALU.mult), sig=S_dve)
                        a_free[3] = td
                        a_free[2] = td
                        pending_ss.append((td, ui))

                def do_ss():
                    td, ui = pending_ss.pop(0)
                    h, c = units[ui]
                    qb_ = ui % 2
                    csl = slice(c * 512, (c + 1) * 512)
                    tk = P.emit("tensor", MM(ps_a[:, 2, :], ones[:], sq[:]), waits=[td], sig=S_pe)
                    P.emit("scalar", ACT(rstd[:], ps_a[:, 2, :], AF.Ln, scale=1.0 / 128, bias=epsb[:, 0:1]), waits=[tk])
                    ta = P.emit("scalar", ACT(rstd[:], rstd[:], AF.Exp, scale=-0.5), sig=S_act)
                    a_free[2] = ta
                    si = ui % 2
                    P.emit("vector", STT(oo[:], oo[:], coef[:, 0:1], rstd[:], ALU.mult, ALU.mult), waits=[ta])
                    te = P.emit("vector", TT(stg[si][:], oo[:], sgt[qb_][:], ALU.mult), waits=[st_free[si], loads[("q", ui)]], sig=S_dve)
                    t_st = P.dma("gpsimd", OGc[h * 128:(h + 1) * 128, csl], stg[si][:], waits=[te], sig=S_st[si])
                    st_free[si] = t_st
                    stores.append(t_st)
                    sg_free[qb_] = te

                load_head(0)
                load_unit(0)
                load_unit(1)
                for step in range(NG + 2):
                    if step < NG:
                        ui, m, g = groups[step]
                        h, c = units[ui]
                        do_qk(step)
                        do_exp(step)
                        glo, ghi = urange[ui]
                        if g - glo == min(3, ghi - glo) and pending_L:
                            do_L()
                        if m == 0 and g - glo == min(6, ghi - glo) and pending_ss:
                            do_ss()
                        if m == 0 and g - glo == min(7, ghi - glo):
                            if ui >= 1 and ui + 1 < len(units):
                                load_unit(ui + 1)
                            if c == 8 and h + 1 < 4:
                                load_head(h + 1)
                    if step >= 2:
                        do_av(step - 2)
                while pending_L:
                    do_L()
                while pending_ss:
                    do_ss()
                for e_ in ("sync", "gpsimd", "scalar", "vector", "tensor"):
                    P.wait_sems(e_, S_st)
                P.flush(f"C{l}")

        phase_cast()
        done = False
        for l in range(depth):
            for nm, ph in (("T", phase_T), ("B", phase_B), ("A", phase_A), ("C", phase_C2)):
                ph(l)
                if stop_after == f"{nm}{l}":
                    done = True
                    break
            if done:
                break
        if not done:
            phase_T(depth)
    return nc


_CACHE = {}
_RUN_KW = {}


def kernel(x_prompt, x_sample, norm_g, w_in, w_oa, w_ob, w_oc, w_out, b_sink,
           lam_q1, lam_k1, lam_q2, lam_k2, c_subln_g, final_norm_g):
    f = lambda a: np.ascontiguousarray(np.asarray(a, dtype=np.float32))
    x_prompt = f(x_prompt); x_sample = f(x_sample)
    consts = make_consts()
    seg_p = np.zeros(NT, np.int64)
    seg_s = np.arange(NT) // 2048
    qa_p, ka_p = make_aug(seg_p, False)
    qa_s, ka_s = make_aug(seg_s, True)
    shared = dict(norm_g=f(norm_g), w_in=f(w_in), w_oa=f(w_oa), w_ob=f(w_ob), w_oc=f(w_oc), w_out=f(w_out),
                  b_sink=f(b_sink), lam_q1=f(lam_q1), lam_k1=f(lam_k1), lam_q2=f(lam_q2), lam_k2=f(lam_k2),
                  c_subln_g=f(c_subln_g), final_norm_g=f(final_norm_g).reshape(1, D), **consts)
    in_maps = []
    for c in range(NCORES):
        if c < 2 or c >= 6:
            xs = x_prompt[c % 2]
            qa, ka = qa_p, ka_p
        else:
            xs = x_sample[4 * (c - 2):4 * (c - 2) + 4].reshape(NT, D)
            qa, ka = qa_s, ka_s
        in_maps.append(dict(x=np.ascontiguousarray(xs), c_qaug=qa, c_kaug=ka, **shared))
    if "nc" not in _CACHE:
        _CACHE["nc"] = build()
    res = run_bass_kernel_spmd(_CACHE["nc"], in_maps, core_ids=list(range(NCORES)), **_RUN_KW)
    _CACHE["res"] = res
    ys = [np.asarray(r["y"], dtype=np.float32) for r in res.results]
    y_prompt = np.stack([ys[0], ys[1]], 0)
    y_sample = np.concatenate([ys[c].reshape(4, 2048, D) for c in range(2, 6)], 0)
    return (y_prompt, y_sample)
```

```python
import math
from contextlib import ExitStack

import numpy as np
import ml_dtypes

import concourse.bass as bass
import concourse.mybir as mybir
from concourse.bass_utils import run_bass_kernel_spmd

F32 = mybir.dt.float32
BF16 = mybir.dt.bfloat16
AF = mybir.ActivationFunctionType
ALU = mybir.AluOpType
AX = mybir.AxisListType

NT = 8192
D = 1024
DIN = 11520
DEPTH = 4
NCORES = 8
NEG = -30000.0
EPS = 1e-6
C_SKIP = 144.0
A_PAT = ((128, 1), (512, 4), (2048, 16))

SEGS = [
    ("qa", 0, 1536, "q"), ("ka", 1536, 1536, "k"), ("va", 3072, 1536, "v"), ("ga", 4608, 512, "silu"),
    ("qb", 5120, 512, "q"), ("kb", 5632, 128, "k"), ("vb", 5760, 128, "v"), ("gb", 5888, 512, "silu"),
    ("qc", 6400, 512, "q"), ("kc", 6912, 512, "k"), ("vc", 7424, 512, "v"), ("gc", 7936, 512, "silu"),
    ("gm", 8448, 3072, "sigmoid"),
]


class Sem:
    def __init__(self, h):
        self.h = h
        self.n = 0


class Ring:
    def __init__(self, n):
        self.n = n
        self.i = 0
        self.free = [None] * n

    def next(self):
        i = self.i
        self.i = (i + 1) % self.n
        return i, self.free[i]


class Prog:
    ENG = ("sync", "scalar", "gpsimd", "vector", "tensor")

    def __init__(self, nc, stack):
        self.nc = nc
        self.stack = stack
        self.q = {e: [] for e in self.ENG}
        self.waited = {e: {} for e in self.ENG}
        self.sems = {}
        self.uid = 0
        self.engsem = {}
        self.last = {e: None for e in self.ENG}

    def sem(self, name):
        if name not in self.sems:
            self.sems[name] = Sem(self.stack.enter_context(self.nc.semaphore("s_" + name)))
        return self.sems[name]

    def name(self, base):
        self.uid += 1
        return f"{base}_{self.uid}"

    def emit(self, eng, fn, waits=(), sig=None, amt=1, chain=None, is_dma=False):
        comp = fn is not None and not is_dma and eng in self.engsem
        if comp:
            if sig is None:
                sig = self.engsem[eng]
            if chain is None:
                chain = True
            if chain and self.last[eng] is not None:
                waits = list(waits) + [self.last[eng]]
        ws = []
        for t in waits:
            if t is None:
                continue
            sem, val = t
            if self.waited[eng].get(sem, 0) >= val:
                continue
            self.waited[eng][sem] = val
            ws.append((sem.h, val))
        tok = None
        if sig is not None:
            sig.n += amt
            tok = (sig, sig.n)
        if comp:
            self.last[eng] = tok
        self.q[eng].append((ws, fn, sig.h if sig is not None else None, amt))
        return tok

    def dma(self, eng, out, in_, waits=(), sig=None):
        return self.emit(eng, lambda e: e.dma_start(out=out, in_=in_), waits, sig, 16, is_dma=True)

    def wait(self, eng, toks):
        self.emit(eng, None, toks)

    def wait_sems(self, eng, sems):
        self.emit(eng, None, [(s_, s_.n) for s_ in sems if s_.n > 0])

    def flush(self, scope=None, barrier=True):
        if scope is not None:
            with self.nc.named_scope(scope):
                self._flush(barrier)
        else:
            self._flush(barrier)

    def _flush(self, barrier=True):
        with self.nc.Block() as block:
            for name in self.ENG:
                items = self.q[name]

                def body(e, items=items):
                    for ws, fn, sh, amt in items:
                        for h, v in ws:
                            e.wait_ge(h, v)
                        if fn is not None:
                            ins = fn(e)
                            if sh is not None:
                                ins.then_inc(sh, amt)

                getattr(block, name)(body)
        if barrier:
            self.nc.all_engine_barrier()
        self.q = {e: [] for e in self.ENG}


def MM(out, lhsT, rhs, start=True, stop=True):
    return lambda e: e.matmul(out, lhsT=lhsT, rhs=rhs, start=start, stop=stop)


def TR(out, in_, ident):
    return lambda e: e.transpose(out, in_, ident)


def ACT(out, in_, func, scale=1.0, bias=None, accum_out=None):
    kw = {}
    if bias is not None:
        kw["bias"] = bias
    if accum_out is not None:
        kw["accum_out"] = accum_out
    return lambda e: e.activation(out=out, in_=in_, func=func, scale=scale, **kw)


def TT(out, in0, in1, op):
    return lambda e: e.tensor_tensor(out=out, in0=in0, in1=in1, op=op)


def TS(out, in0, s1, op0, s2=None, op1=None):
    if op1 is None:
        return lambda e: e.tensor_scalar(out=out, in0=in0, scalar1=s1, scalar2=None, op0=op0)
    return lambda e: e.tensor_scalar(out=out, in0=in0, scalar1=s1, scalar2=s2, op0=op0, op1=op1)


def STT(out, in0, scalar, in1, op0, op1):
    return lambda e: e.scalar_tensor_tensor(out=out, in0=in0, scalar=scalar, in1=in1, op0=op0, op1=op1)


def CP(out, in_):
    return lambda e: e.tensor_copy(out=out, in_=in_)


def RCP(out, in_):
    return lambda e: e.reciprocal(out=out, in_=in_)


def MSET(ap, v):
    return lambda e: e.memset(ap, v)


def ssl(start, n, step):
    if step == 1:
        return slice(start, start + n)
    return slice(start, start + step * (n - 1) + 1, step)


def bcast_rows(ap2d_row, nparts):
    a = ap2d_row
    return bass.AP(a.tensor, a.offset, [[0, nparts]] + [list(x) for x in a.ap[1:]])


def bf16(a):
    return np.asarray(a, np.float32).astype(ml_dtypes.bfloat16)


def make_consts():
    c = {}
    kp = np.arange(128)[:, None].astype(np.float64)
    cq = np.arange(384)[None, :].astype(np.float64)
    delta = (cq - 128.0) - kp
    bB = np.empty((128, 8, 384), np.float64)
    for h in range(8):
        slope = 2.0 ** (-(h + 1))
        bB[:, h, :] = np.where(np.abs(delta) <= 128, -slope * np.abs(delta), NEG)
    c["c_bB"] = bB.reshape(128, 8 * 384).astype(np.float32)
    cq = np.arange(256)[None, :]
    cc = cq // 128
    qq = (cq % 128).astype(np.float64)
    delta = qq - kp + np.where(cc == 0, -64.0, 64.0)
    bA = np.empty((128, 24, 256), np.float64)
    for g, (_, dil) in enumerate(A_PAT):
        for h in range(8):
            slope = np.float32(2.0 ** (-8.0 * (8 * g + h + 1) / 24))
            bA[:, g * 8 + h, :] = np.where(np.abs(delta) <= 64, -(np.float64(slope) * dil) * np.abs(delta), NEG)
    c["c_bA"] = bA.reshape(128, 24 * 256).astype(np.float32)
    q1 = np.arange(128)[None, :].astype(np.float64)
    bC = np.empty((128, 4, 128), np.float64)
    for h in range(4):
        m = 2.0 ** (-2.0 * (h + 1))
        bC[:, h, :] = -m * np.abs(q1 - kp)
    c["c_bC"] = bf16(bC.reshape(128, 512))
    c["c_id"] = bf16(np.eye(128))
    return c


def make_aug(seg_ids, masked):
    t = np.arange(NT)
    A = (t // 128).astype(np.float64)
    b = (t % 128).astype(np.float64)
    oh = np.zeros((4, NT), np.float64)
    oh[seg_ids, t] = 1.0
    qa = np.zeros((3, 8, NT), np.float64)
    qa[:, 0:4, :] = oh[None]
    al = np.stack([A, b, np.ones(NT), np.ones(NT)])
    qa[0, 4:8] = al
    qa[1, 4:8] = -al
    ka = np.zeros((4, 8, NT), np.float64)
    if masked:
        ka[:, 0:4, :] = (NEG * (1.0 - oh))[None]
    for h in range(4):
        m = 2.0 ** (-2.0 * (h + 1))
        ka[h, 4] = -128.0 * m
        ka[h, 5] = -m
        ka[h, 6] = 128.0 * m * A
        ka[h, 7] = m * b
    return bf16(qa), bf16(ka)


def build(depth=DEPTH, dbg=None, stop_after=None):
    nc = bass.Bass("TRN2", target_bir_lowering=False)
    dbg = dbg or ()

    def din(name, shape, dt=F32):
        return nc.dram_tensor(name, list(shape), dt, kind="ExternalInput").ap()

    def dscr(name, shape, dt=BF16):
        kind = {"kind": "ExternalOutput"} if name in dbg else {}
        return nc.dram_tensor(name, list(shape), dt, **kind).ap()

    x_in = din("x", [NT, D])
    y_out = nc.dram_tensor("y", [NT, D], F32, kind="ExternalOutput").ap()
    norm_g = din("norm_g", [DEPTH, D])
    w_in = din("w_in", [DEPTH, D, DIN])
    w_oa = din("w_oa", [DEPTH, 512, D])
    w_ob = din("w_ob", [DEPTH, 512, D])
    w_oc = din("w_oc", [DEPTH, 512, D])
    w_out = din("w_out", [DEPTH, D, D])
    b_sink = din("b_sink", [DEPTH, 8])
    lam_q1 = din("lam_q1", [DEPTH, 64])
    lam_k1 = din("lam_k1", [DEPTH, 64])
    lam_q2 = din("lam_q2", [DEPTH, 64])
    lam_k2 = din("lam_k2", [DEPTH, 64])
    subln_g = din("c_subln_g", [DEPTH, 128])
    final_g = din("final_norm_g", [1, D])
    c_bB = din("c_bB", [128, 8 * 384], F32)
    c_bA = din("c_bA", [128, 24 * 256], F32)
    c_bC = din("c_bC", [128, 512], BF16)
    c_id = din("c_id", [128, 128], BF16)
    c_qaug = din("c_qaug", [3, 8, NT], BF16)
    c_kaug = din("c_kaug", [4, 8, NT], BF16)

    wb_in = dscr("wb_in", [depth, D, DIN])
    wb_o = [dscr(f"wb_o{i}", [depth, 512, D]) for i in range(3)]
    wb_out = dscr("wb_out", [depth, D, D])
    XR = dscr("XR", [NT, D], F32)
    QaT = dscr("QaT", [1536, NT]); KaT = dscr("KaT", [1536, NT]); Va = dscr("Va", [NT, 1536])
    QbT = dscr("QbT", [512, NT]); KbT = dscr("KbT", [128, NT]); Vb = dscr("Vb", [NT, 128])
    QcT = dscr("QcT", [512, NT]); KcT = dscr("KcT", [512, NT]); Vc = dscr("Vc", [NT, 512])
    SGa = dscr("SGa", [512, NT]); SGb = dscr("SGb", [512, NT]); SGc = dscr("SGc", [512, NT])
    SGm = dscr("SGm", [3072, NT])
    OGa = dscr("OGa", [512, NT]); OGb = dscr("OGb", [512, NT]); OGc = dscr("OGc", [512, NT])
    DEST = {"qa": QaT, "ka": KaT, "va": Va, "ga": SGa, "qb": QbT, "kb": KbT, "vb": Vb, "gb": SGb,
            "qc": QcT, "kc": KcT, "vc": Vc, "gc": SGc, "gm": SGm}

    with ExitStack() as gst:
        P = Prog(nc, gst)
        S_pe, S_act, S_dve, S_pool = P.sem("pe"), P.sem("act"), P.sem("dve"), P.sem("pool")
        P.engsem = {"scalar": S_act, "vector": S_dve, "gpsimd": S_pool}

        def sbuf(st, base, shape, dt):
            return st.enter_context(nc.sbuf_tensor(P.name(base), list(shape), dt))

        def psum(st, base, shape, dt):
            return st.enter_context(nc.psum_tensor(P.name(base), list(shape), dt))

        cast_tok = {}

        def phase_cast():
            for l in range(depth):
                S = P.sem(f"cast{l}")
                for r in range(8):
                    P.dma("gpsimd", wb_in[l, r * 128:(r + 1) * 128, :], w_in[l, r * 128:(r + 1) * 128, :], sig=S)
                for i, w in enumerate((w_oa, w_ob, w_oc)):
                    P.dma("gpsimd", wb_o[i][l, :, :], w[l, :, :], sig=S)
                P.dma("gpsimd", wb_out[l, :, :], w_out[l, :, :], sig=S)
                cast_tok[l] = (S, S.n)
            P.flush("cast", barrier=False)

        def phase_T(l):
            first = l == 0
            last = l == depth
            Xsrc = x_in if l <= 1 else XR
            with ExitStack() as st:
                ps = psum(st, "psT", [128, 6, 512], F32)
                pt = psum(st, "ptT", [128, 2, 1024], BF16)
                ring = Ring(6)
                ident = sbuf(st, "ident", [128, 128], BF16)
                gb = sbuf(st, "gb", [128, D], F32)
                xts = [sbuf(st, "xt", [128, 4, D], F32) for _ in range(2)]
                junk = sbuf(st, "junk", [128, D], BF16)
                ssq = sbuf(st, "ssq", [128, 32], F32)
                epsb = sbuf(st, "epsb", [128, 1], F32)
                S_c = P.sem("T_const"); S_x = [P.sem("T_x0"), P.sem("T_x1")]; S_xst = P.sem("T_xst")
                P.wait("sync", [cast_tok[k] for k in range(min(l, depth - 1) + 1)])
                P.dma("sync", ident[:], c_id[:, :], sig=S_c)
                gsrc = final_g[0:1, :] if last else norm_g[l:l + 1, :]
                P.dma("sync", gb[:], bcast_rows(gsrc, 128), sig=S_c)
                P.emit("vector", MSET(epsb[:], EPS), sig=S_dve)
                t_eps = (S_dve, S_dve.n)
                if not first:
                    ogs = [sbuf(st, "og", [128, 3, 4, 512], BF16) for _ in range(2)]
                    gm = [sbuf(st, "gm", [128, 3, 512], BF16) for _ in range(2)]
                    tmp = [[sbuf(st, "tmp", [128, 512], F32) for _ in range(3)] for _ in range(2)]
                    mixed = sbuf(st, "mixed", [128, 8, 512], BF16)
                    Wo = [sbuf(st, "Wo", [128, 4, D], BF16) for _ in range(3)]
                    wout = sbuf(st, "wout", [128, 8, D], BF16)
                    for i in range(3):
                        P.dma("sync", Wo[i][:], wb_o[i][l - 1].rearrange("(k p) f -> p k f", p=128), sig=S_c)
                    P.dma("sync", wout[:], wb_out[l - 1].rearrange("(k p) f -> p k f", p=128), sig=S_c)
                    S_og = [P.sem("T_og0"), P.sem("T_og1")]; S_gm = [P.sem("T_gm0"), P.sem("T_gm1")]
                if not last:
                    hT = sbuf(st, "hT", [128, 8, 2048], BF16)
                    hb = sbuf(st, "hb", [128, 4, D], BF16)
                    Wt = [sbuf(st, "Wt", [128, 8, 512], BF16) for _ in range(2)]
                    stF = [sbuf(st, "stF", [128, 2048], BF16) for _ in range(3)]
                    stV = [sbuf(st, "stV", [128, 4, 512], BF16) for _ in range(2)]
                    S_w = [P.sem("T_w0"), P.sem("T_w1")]
                    S_stF = [P.sem(f"T_stF{i}") for i in range(3)]
                    S_stV = [P.sem(f"T_stV{i}") for i in range(2)]
                t_c = (S_c, S_c.n)
                state = dict(og_free=[None, None], mixed_free=None, x_free=[[], []], gm_free=[None, None], hb_free=None,
                             w_free=[None, None], stF_free=[None] * 3, stV_free=[None] * 2, wi=0, fi=0, vi=0,
                             pt_free=[None, None], pti=0, stores=[], ld={}, tmp_free=[None, None], ssq_free=[None, None])

                def T1_load(tt):
                    par = tt % 2
                    xsl = slice(tt * 512, tt * 512 + 512)
                    t_x = P.dma("sync", xts[par][:], Xsrc[xsl, :].rearrange("(s p) f -> p s f", p=128),
                                waits=state["x_free"][par], sig=S_x[par])
                    t_og = None
                    if not first:
                        for b_, src in enumerate((OGa, OGb, OGc)):
                            P.dma("sync", ogs[par][:, b_], src.rearrange("(k p) t -> p k t", p=128)[:, :, xsl],
                                  waits=[state["og_free"][par]], sig=S_og[par])
                        t_og = (S_og[par], S_og[par].n)
                    state["ld"][tt] = (t_x, t_og)

                def T1(tt):
                    tok0 = tt * 512
                    par = tt % 2
                    xt = xts[par]
                    xsl = slice(tok0, tok0 + 512)
                    t_x, t_og = state["ld"].pop(tt)
                    if first and tt + 1 < 16:
                        T1_load(tt + 1)
                    x_ready = t_x
                    if not first:
                        og = ogs[par]
                        gsrc_all = SGm.rearrange("(b f p) t -> p b f t", b=3, p=128)
                        t_mixed = None
                        for fc in range(8):
                            sl = fc % 2
                            t_gm = P.dma("sync", gm[sl][:], gsrc_all[:, :, fc, xsl],
                                         waits=[state["gm_free"][sl]], sig=S_gm[sl])
                            if fc == 3 and tt + 1 < 16:
                                T1_load(tt + 1)
                            bks = []
                            for b in range(3):
                                bi, bfree = ring.next()
                                for kc in range(4):
                                    tk = P.emit("tensor", MM(ps[:, bi, :], Wo[b][:, kc, fc * 128:(fc + 1) * 128],
                                                             og[:, b, kc, :], kc == 0, kc == 3),
                                                waits=[bfree, t_og, t_c], sig=S_pe if kc == 3 else None)
                                bks.append((bi, tk))
                            if fc == 7:
                                state["og_free"][par] = bks[-1][1]
                            for b in range(3):
                                bi, tk = bks[b]
                                td = P.emit("vector", TT(tmp[sl][b][:], ps[:, bi, :], gm[sl][:, b, :], ALU.mult),
                                            waits=[tk, t_gm, state["tmp_free"][sl]], sig=S_dve, chain=False)
                                ring.free[bi] = td
                            state["gm_free"][sl] = td
                            w = [td]
                            if fc == 0:
                                w.append(state["mixed_free"])
                            P.emit("gpsimd", TT(tmp[sl][0][:], tmp[sl][0][:], tmp[sl][1][:], ALU.add), waits=w, sig=S_pool, chain=False)
                            t_mixed = P.emit("gpsimd", TT(mixed[:, fc, :], tmp[sl][0][:], tmp[sl][2][:], ALU.add), sig=S_pool)
                            state["tmp_free"][sl] = t_mixed
                        for sub in range(4):
                            for fh in range(2):
                                bi, bfree = ring.next()
                                for kc in range(8):
                                    tk = P.emit("tensor", MM(ps[:, bi, :], mixed[:, kc, sub * 128:(sub + 1) * 128],
                                                             wout[:, kc, fh * 512:(fh + 1) * 512], kc == 0, kc == 7),
                                                waits=[bfree, t_mixed], sig=S_pe if kc == 7 else None)
                                td = P.emit("vector", TT(xt[:, sub, fh * 512:(fh + 1) * 512], ps[:, bi, :],
                                                         xt[:, sub, fh * 512:(fh + 1) * 512], ALU.add),
                                            waits=[tk, t_x], sig=S_dve, chain=False)
                                ring.free[bi] = td
                        state["mixed_free"] = tk
                        x_ready = td
                    frees = []
                    if not first and not last:
                        t_st = P.dma("gpsimd", XR[xsl, :].rearrange("(s p) f -> p s f", p=128), xt[:],
                                     waits=[x_ready], sig=S_xst)
                        frees.append(t_st)
                    q0 = 16 * par
                    for sub in range(4):
                        t_sq = P.emit("scalar", ACT(junk[:], xt[:, sub, :], AF.Square, accum_out=ssq[:, q0 + sub:q0 + sub + 1]),
                                      waits=[x_ready, state["ssq_free"][par]], sig=S_act)
                    P.emit("scalar", ACT(ssq[:, q0 + 4:q0 + 8], ssq[:, q0:q0 + 4], AF.Ln, scale=1.0 / D, bias=epsb[:, 0:1]), waits=[t_eps])
                    t_r = P.emit("scalar", ACT(ssq[:, q0 + 8:q0 + 12], ssq[:, q0 + 4:q0 + 8], AF.Exp, scale=-0.5), sig=S_act)
                    if last:
                        for sub in range(4):
                            td = P.emit("vector", STT(xt[:, sub, :], xt[:, sub, :], ssq[:, q0 + 8 + sub:q0 + 9 + sub], gb[:],
                                                      ALU.mult, ALU.mult), waits=[t_r, t_c], sig=S_dve, chain=False)
                        t_st = P.dma("gpsimd", y_out[xsl, :].rearrange("(s p) f -> p s f", p=128), xt[:],
                                     waits=[td], sig=S_xst)
                        state["x_free"][par] = [t_st]
                        state["ssq_free"][par] = td
                        return
                    for sub in range(4):
                        w = [t_r, t_c]
                        if sub == 0:
                            w.append(state["hb_free"])
                        td = P.emit("vector", STT(hb[:, sub, :], xt[:, sub, :], ssq[:, q0 + 8 + sub:q0 + 9 + sub], gb[:],
                                                  ALU.mult, ALU.mult), waits=w, sig=S_dve, chain=False)
                        pi = state["pti"]; state["pti"] = 1 - pi
                        for kc in range(8):
                            tk = P.emit("tensor", TR(pt[:, pi, kc * 128:(kc + 1) * 128], hb[:, sub, kc * 128:(kc + 1) * 128], ident[:]),
                                        waits=[td, state["pt_free"][pi], t_c], sig=S_pe if kc == 7 else None)
                        off = (tt % 4) * 512 + sub * 128
                        ta = P.emit("scalar", ACT(hT[:, :, off:off + 128], pt[:, pi, :].rearrange("p (k t) -> p k t", k=8), AF.Copy),
                                    waits=[tk], sig=S_act, chain=False)
                        state["pt_free"][pi] = ta
                    state["hb_free"] = tk
                    state["hT_ready"] = ta
                    state["ssq_free"][par] = td
                    frees += [t_sq, td]
                    state["x_free"][par] = frees

                wtiles = [(name, c0, kind, w0, min(512, ncols - w0)) for (name, c0, ncols, kind) in SEGS
                          for w0 in range(0, ncols, 512)]
                NW = len(wtiles)
                w_tok = {}

                def load_w(k):
                    name, c0, kind, w0, wc = wtiles[k % NW]
                    wi = k % 2
                    w_tok[k] = P.dma("sync", Wt[wi][:, :, 0:wc],
                                     wb_in[l].rearrange("(k p) c -> p k c", p=128)[:, :, c0 + w0:c0 + w0 + wc],
                                     waits=[state["w_free"][wi]], sig=S_w[wi])

                def T2(s):
                    tsl = slice(s * 2048, (s + 1) * 2048)
                    t_h = state["hT_ready"]
                    for ti, (name, c0, kind, w0, wc) in enumerate(wtiles):
                        dest = DEST[name]
                        k = s * NW + ti
                        wi = k % 2
                        if k + 1 < 4 * NW:
                            load_w(k + 1)
                        t_w = w_tok[k]
                        if kind == "v":
                            for s16 in range(16):
                                vi = state["vi"]
                                bi, bfree = ring.next()
                                for kc in range(8):
                                    tk = P.emit("tensor", MM(ps[:, bi, 0:wc], hT[:, kc, s16 * 128:(s16 + 1) * 128],
                                                             Wt[wi][:, kc, 0:wc], kc == 0, kc == 7),
                                                waits=[bfree, t_w, t_h], sig=S_pe if kc == 7 else None)
                                w = [tk]
                                if s16 % 4 == 0:
                                    w.append(state["stV_free"][vi])
                                ta = P.emit("scalar", ACT(stV[vi][:, s16 % 4, 0:wc], ps[:, bi, 0:wc], AF.Copy),
                                            waits=w, sig=S_act, chain=False)
                                ring.free[bi] = ta
                                if s16 % 4 == 3:
                                    r0 = s * 2048 + (s16 // 4) * 512
                                    t_st = P.dma("gpsimd", dest[r0:r0 + 512, w0:w0 + wc].rearrange("(s p) c -> p s c", p=128),
                                                 stV[vi][:, :, 0:wc], waits=[ta], sig=S_stV[vi])
                                    state["stV_free"][vi] = t_st
                                    state["vi"] = 1 - vi
                            state["w_free"][wi] = tk
                            continue
                        for sc in range(wc // 128):
                            fi = state["fi"]; state["fi"] = (fi + 1) % 3
                            for i in range(4):
                                bi, bfree = ring.next()
                                for kc in range(8):
                                    tk = P.emit("tensor", MM(ps[:, bi, :], Wt[wi][:, kc, sc * 128:(sc + 1) * 128],
                                                             hT[:, kc, i * 512:(i + 1) * 512], kc == 0, kc == 7),
                                                waits=[bfree, t_w, t_h], sig=S_pe if kc == 7 else None)
                                w = [tk]
                                if i == 0:
                                    w.append(state["stF_free"][fi])
                                o = stF[fi][:, i * 512:(i + 1) * 512]
                                if kind == "q":
                                    te = P.emit("vector", TS(o, ps[:, bi, :], 0.125, ALU.mult), waits=w, sig=S_dve, chain=False)
                                elif kind == "k":
                                    te = P.emit("vector", CP(o, ps[:, bi, :]), waits=w, sig=S_dve, chain=False)
                                elif kind == "silu":
                                    te = P.emit("scalar", ACT(o, ps[:, bi, :], AF.Silu), waits=w, sig=S_act, chain=False)
                                else:
                                    te = P.emit("scalar", ACT(o, ps[:, bi, :], AF.Sigmoid), waits=w, sig=S_act, chain=False)
                                ring.free[bi] = te
                            f0 = w0 + sc * 128
                            t_st = P.dma("gpsimd", dest[f0:f0 + 128, tsl], stF[fi][:], waits=[te], sig=S_stF[fi])
                            state["stF_free"][fi] = t_st
                        state["w_free"][wi] = tk

                if not last:
                    load_w(0)
                T1_load(0)
                for s in range(4):
                    for i in range(4):
                        T1(4 * s + i)
                    if not last:
                        T2(s)
                st_sems = [S_xst]
                if not last:
                    st_sems += S_stF + S_stV
                for e_ in ("sync", "gpsimd", "scalar", "vector", "tensor"):
                    P.wait_sems(e_, st_sems)
                P.flush(f"T{l}")

        def run_banded(res, kbs):
            ps_s, ps_a, PT, SB = res["ps_s"], res["ps_a"], res["PT"], res["SB"]
            NPT = res["NPT"]
            G = 2
            ngroups = (len(kbs) + G - 1) // G
            s_free = res["s_free"]
            sb_free = res["sb_free"]
            pt_free = res["pt_free"]
            a_free = res["a_free"]
            exp_tok = {}
            qk_tok = {}
            bias_tok = {}

            def do_qk(gi):
                sl = gi % 2
                tk = None
                members = kbs[gi * G:(gi + 1) * G]
                for m, kb in enumerate(members):
                    for qi_, (lo, hi, lhsT, rhs) in enumerate(kb["qk"]):
                        islast = (qi_ == len(kb["qk"]) - 1) and (m == len(members) - 1)
                        tk = P.emit("tensor", MM(ps_s[:, sl * G + m, lo:hi], lhsT, rhs, True, True),
                                    waits=[s_free[sl]] + list(kb["waits"]), sig=S_pe if islast else None)
                qk_tok[gi] = tk

            def do_bias(gi):
                sl = gi % 2
                members = kbs[gi * G:(gi + 1) * G]
                cols = members[0]["cols"]
                n = len(members)
                bap = members[0]["bias"]
                bb = bass.AP(bap.tensor, bap.offset, [list(bap.ap[0]), [0, n], list(bap.ap[-1])])
                td = P.emit("vector", TT(SB[:, sl, 0:n, 0:cols], ps_s[:, sl * G:sl * G + n, 0:cols], bb, ALU.add),
                            waits=[qk_tok[gi], sb_free[sl]] + list(members[0]["waits"]), sig=S_dve, chain=False)
                bias_tok[gi] = td
                s_free[sl] = td

            def do_exp(gi):
                sl = gi % 2
                members = kbs[gi * G:(gi + 1) * G]
                cols = members[0]["cols"]
                n = len(members)
                slots = [(gi * G + m) % NPT for m in range(n)]
                w = [bias_tok[gi]] + [pt_free[s_] for s_ in slots]
                ta = P.emit("scalar", ACT(PT[:, slots[0]:slots[0] + n, 0:cols], SB[:, sl, 0:n, 0:cols], AF.Exp),
                            waits=w, sig=S_act, chain=False)
                exp_tok[gi] = ta
                sb_free[sl] = ta

            def do_av(gi):
                members = kbs[gi * G:(gi + 1) * G]
                for m, kb in enumerate(members):
                    for job in kb["av"]:
                        bank, slot = job["bank"], job["slot"]
                        qa, qb = job["qa"], job["qb"]
                        np_ = len(job["parts"])
                        for pi_, (kidx, c, vap) in enumerate(job["parts"]):
                            w = [exp_tok[kidx // G]]
                            if pi_ == 0 and job["bank_first"]:
                                w += list(a_free[bank] or ())
                            w += list(job.get("waits", ()))
                            tk = P.emit("tensor", MM(ps_a[:, bank, slot * 128 + qa:slot * 128 + qb], vap,
                                                     PT[:, kidx % NPT, c * 128 + qa:c * 128 + qb], pi_ == 0, pi_ == np_ - 1),
                                        waits=w, sig=S_pe if (pi_ == np_ - 1) else None)
                            pt_free[kidx % NPT] = tk
                        if job["bank_done"]:
                            a_free[bank] = job["evac"](tk)

            for step in range(ngroups + 2):
                if step < ngroups:
                    do_qk(step)
                    do_bias(step)
                    do_exp(step)
                if step >= 2:
                    do_av(step - 2)

        def phase_B(l):
            with ExitStack() as st:
                ps_s = psum(st, "psBs", [128, 4, 512], F32)
                ps_a = psum(st, "psBa", [128, 2, 512], F32)
                NPT = 8
                PT = sbuf(st, "PT", [128, NPT, 384], BF16)
                ident = sbuf(st, "ident", [128, 128], BF16)
                bias = sbuf(st, "biasB", [128, 8, 384], F32)
                SB = sbuf(st, "SBb", [128, 2, 2, 384], F32)
                QT = [sbuf(st, "QTb", [68, NT], BF16) for _ in range(2)]
                KT = [sbuf(st, "KTb", [68, NT], BF16) for _ in range(2)]
                Vg = sbuf(st, "Vb", [128, 64, 2, 128], BF16)
                sg = [sbuf(st, "sgb", [64, NT], BF16) for _ in range(2)]
                esink = sbuf(st, "esink", [128, 8], F32)
                rec = [sbuf(st, "recB", [64, 512], F32) for _ in range(2)]
                of = [sbuf(st, "ofB", [64, 512], F32) for _ in range(2)]
                rec_free = [None, None]
                stg = [sbuf(st, "stgB", [64, 512], BF16) for _ in range(2)]
                S_c = P.sem("B_c"); S_q = [P.sem("B_q0"), P.sem("B_q1")]; S_st = [P.sem("B_st0"), P.sem("B_st1")]
                P.dma("sync", ident[:], c_id[:, :], sig=S_c)
                P.dma("sync", bias[:], c_bB.rearrange("p (h c) -> p h c", h=8), sig=S_c)
                P.dma("sync", esink[:], bcast_rows(b_sink[l:l + 1, :], 128), sig=S_c)
                for kv in range(2):
                    P.dma("sync", KT[kv][0:64, :], KbT[kv * 64:(kv + 1) * 64, :], sig=S_c)
                    P.dma("sync", KT[kv][64:68, :], c_kaug[0, 0:4, :], sig=S_c)
                    P.dma("sync", QT[kv][64:68, :], c_qaug[2, 0:4, :], sig=S_c)
                P.emit("gpsimd", MSET(Vg[:], 1.0), sig=S_pool)
                t_ms = (S_pool, S_pool.n)
                vsrc = Vb.rearrange("(j p) (k d) -> p j k d", p=128, k=2)
                for q4 in range(4):
                    for kv in range(2):
                        P.dma("sync", Vg[:, q4 * 16:(q4 + 1) * 16, kv, 0:64], vsrc[:, q4 * 16:(q4 + 1) * 16, kv, :],
                              waits=[t_ms], sig=S_c)
                t_c = (S_c, S_c.n)
                t_es = P.emit("scalar", ACT(esink[:], esink[:], AF.Exp), waits=[t_c], sig=S_act)
                res = dict(ps_s=ps_s, ps_a=ps_a, PT=PT, SB=SB, NPT=NPT, s_free=[None, None], sb_free=[None, None],
                           pt_free=[None] * NPT, a_free=[None, None])
                q_free = [None, None]
                st_free = [None, None]
                stores = []
                sti = [0]
                tq = {}

                def load_head(h):
                    qi = h % 2
                    P.dma("sync", QT[qi][0:64, :], QbT[h * 64:(h + 1) * 64, :], waits=[q_free[qi]], sig=S_q[qi])
                    P.dma("sync", sg[qi][:], SGb[h * 64:(h + 1) * 64, :], waits=[q_free[qi]], sig=S_q[qi])
                    tq[h] = (S_q[qi], S_q[qi].n)

                load_head(0)
                for h in range(8):
                    kv = h // 4
                    qi = h % 2
                    if h + 1 < 8:
                        load_head(h + 1)
                    t_q = tq[h]
                    kbs = []
                    last_tok = [None]

                    def mk_evac(bank, q0, h=h, qi=qi):
                        def evac(tk):
                            tsl = slice(q0 * 128, q0 * 128 + 512)
                            ri = sti[0]; sti[0] = 1 - ri
                            ta0 = P.emit("scalar", ACT(rec[ri][:], ps_a[64:128, bank, :], AF.Ln, bias=esink[64:128, h:h + 1]),
                                         waits=[tk, t_es, rec_free[ri]], sig=S_act, chain=False)
                            ta = P.emit("scalar", ACT(rec[ri][:], rec[ri][:], AF.Exp, scale=-1.0), sig=S_act)
                            td = P.emit("vector", TT(of[ri][:], ps_a[0:64, bank, :], sg[qi][:, tsl], ALU.mult),
                                        waits=[tk, t_q, rec_free[ri]], sig=S_dve, chain=False)
                            te = P.emit("gpsimd", TT(stg[ri][:], of[ri][:], rec[ri][:], ALU.mult),
                                        waits=[ta, td, st_free[ri]], sig=S_pool, chain=False)
                            rec_free[ri] = te
                            t_st = P.dma("gpsimd", OGb[h * 64:(h + 1) * 64, tsl], stg[ri][:], waits=[te], sig=S_st[ri])
                            st_free[ri] = t_st
                            stores.append(t_st)
                            last_tok[0] = te
                            return [ta0, td]
                        return evac

                    for j in range(64):
                        qlo = max(j - 1, 0); qhi = min(j + 1, 63)
                        lo = (qlo - (j - 1)) * 128; hi = (qhi - (j - 1) + 1) * 128
                        kb = dict(qk=[(lo, hi, KT[kv][0:68, j * 128:(j + 1) * 128], QT[qi][0:68, qlo * 128:(qhi + 1) * 128])],
                                  bias=bias[:, h, :], cols=384, waits=[t_q, t_c], av=[])
                        done = []
                        if j >= 1:
                            done.append(j - 1)
                        if j == 63:
                            done.append(63)
                        for qb_ in done:
                            parts = [(jj, qb_ - jj + 1, Vg[:, jj, kv, :]) for jj in (qb_ - 1, qb_, qb_ + 1) if 0 <= jj < 64]
                            bank = (qb_ // 4) % 2
                            job = dict(parts=parts, qa=0, qb=128, slot=qb_ % 4, bank=bank, bank_first=(qb_ % 4 == 0),
                                       bank_done=(qb_ % 4 == 3), waits=[t_c])
                            if job["bank_done"]:
                                job["evac"] = mk_evac(bank, qb_ - 3)
                            kb["av"].append(job)
                        kbs.append(kb)
                    run_banded(res, kbs)
                    q_free[qi] = last_tok[0]
                for e_ in ("sync", "gpsimd", "scalar", "vector", "tensor"):
                    P.wait_sems(e_, S_st)
                P.flush(f"B{l}")

        def phase_A(l):
            with ExitStack() as st:
                ps_s = psum(st, "psAs", [128, 4, 512], F32)
                ps_a = psum(st, "psAa", [128, 2, 512], F32)
                NPT = 8
                PT = sbuf(st, "PTa", [128, NPT, 256], BF16)
                ident = sbuf(st, "ident", [128, 128], BF16)
                bA = sbuf(st, "bA", [128, 24, 256], F32)
                SB = sbuf(st, "SBa", [128, 2, 2, 256], F32)
                QT = [sbuf(st, "QTa", [68, NT], BF16) for _ in range(2)]
                KT = [sbuf(st, "KTa", [68, NT], BF16) for _ in range(2)]
                Vg = [sbuf(st, "Va", [128, 64, 128], BF16) for _ in range(2)]
                accs = [sbuf(st, "accA", [128, NT], F32) for _ in range(2)]
                sgt = [sbuf(st, "sga", [64, 512], BF16) for _ in range(2)]
                rec = [sbuf(st, "recA", [64, 512], F32) for _ in range(2)]
                of = [sbuf(st, "ofA", [64, 512], F32) for _ in range(2)]
                rec_free = [None, None]
                stg = [sbuf(st, "stgA", [64, 512], BF16) for _ in range(2)]
                S_c = P.sem("A_c"); S_q = [P.sem("A_q0"), P.sem("A_q1")]
                S_st = [P.sem("A_st0"), P.sem("A_st1")]; S_sg = [P.sem("A_sg0"), P.sem("A_sg1")]
                P.dma("sync", ident[:], c_id[:, :], sig=S_c)
                P.dma("sync", bA[:], c_bA.rearrange("p (h c) -> p h c", h=24), sig=S_c)
                for i in range(2):
                    P.dma("sync", KT[i][64:68, :], c_kaug[0, 0:4, :], sig=S_c)
                    P.dma("sync", QT[i][64:68, :], c_qaug[2, 0:4, :], sig=S_c)
                    P.emit("gpsimd", MSET(Vg[i][:], 1.0), sig=S_pool)
                t_ms = (S_pool, S_pool.n)
                t_c = (S_c, S_c.n)
                res = dict(ps_s=ps_s, ps_a=ps_a, PT=PT, SB=SB, NPT=NPT, s_free=[None, None], sb_free=[None, None],
                           pt_free=[None] * NPT, a_free=[None, None])
                q_free = [None, None]
                st_free = [None, None]
                sg_free = [None, None]
                stores = []
                acc_free = [None, None]
                ui = 0
                tq = {}

                def load_unit(u):
                    h, g = divmod(u, 3)
                    dil = A_PAT[g][1]
                    qi = u % 2
                    f0 = g * 512 + h * 64
                    P.dma("sync", QT[qi][0:64, :], QaT[f0:f0 + 64, :], waits=[q_free[qi]], sig=S_q[qi])
                    P.dma("sync", KT[qi][0:64, :], KaT[f0:f0 + 64, :], waits=[q_free[qi]], sig=S_q[qi])
                    U = NT // dil
                    nb = U // 128
                    vsrc = Va.rearrange("(u d) c -> d u c", d=dil)
                    for r in range(dil):
                        vr = vsrc[r, :, f0:f0 + 64].rearrange("(j p) c -> p j c", p=128)
                        for j0 in range(0, nb, 16):
                            j1 = min(nb, j0 + 16)
                            P.dma("sync", Vg[qi][:, r * nb + j0:r * nb + j1, 0:64], vr[:, j0:j1, :],
                                  waits=[q_free[qi], t_ms], sig=S_q[qi])
                    tq[u] = (S_q[qi], S_q[qi].n)

                load_unit(0)
                for h in range(8):
                    last_acc = None
                    acc = accs[h % 2]
                    for g, (_, dil) in enumerate(A_PAT):
                        qi = ui % 2
                        if ui + 1 < 24:
                            load_unit(ui + 1)
                        t_q = tq[ui]
                        ui += 1
                        f0 = g * 512 + h * 64
                        U = NT // dil
                        nb = U // 128
                        kbs = []
                        last_tok = [None]

                        def mk_evac(bank, lo, hi, t0, n, g=g, dil=dil, acc=acc, h=h):
                            def evac(tk):
                                nonlocal last_acc
                                dst = acc[:, ssl(t0, n, dil)]
                                w = [tk]
                                if g == 0:
                                    w += list(acc_free[h % 2] or ())
                                    td = P.emit("vector", CP(dst, ps_a[:, bank, lo:hi]), waits=w, sig=S_dve, chain=False)
                                else:
                                    td = P.emit("vector", TT(dst, ps_a[:, bank, lo:hi], dst, ALU.add), waits=w, sig=S_dve)
                                last_tok[0] = td
                                last_acc = td
                                return [td]
                            return evac

                        kidx = 0
                        for r in range(dil):
                            for j in range(nb):
                                tb = r + dil * 128 * j
                                ulo = max(128 * j - 64, 0); uhi = min(128 * j + 192, U)
                                lo = ulo - (128 * j - 64); hi = uhi - (128 * j - 64)
                                kb = dict(qk=[(lo, hi, KT[qi][0:68, ssl(tb, 128, dil)],
                                               QT[qi][0:68, ssl(r + dil * ulo, uhi - ulo, dil)])],
                                          bias=bA[:, g * 8 + h, :], cols=256, waits=[t_q, t_c], av=[])
                                done = [j - 1]
                                if j == nb - 1:
                                    done.append(nb - 1)
                                for qb_ in done:
                                    parts = []
                                    for jj in (qb_, qb_ + 1):
                                        if 0 <= jj < nb:
                                            parts.append((kidx - (j - jj), qb_ - jj + 1, Vg[qi][:, r * nb + jj, :]))
                                    qa = 64 if qb_ == -1 else 0
                                    qbb = 64 if qb_ == nb - 1 else 128
                                    seq = qb_ + 1
                                    bank = (seq // 4) % 2
                                    slot = seq % 4
                                    bank_done = (slot == 3) or (qb_ == nb - 1)
                                    job = dict(parts=parts, qa=qa, qb=qbb, slot=slot, bank=bank, bank_first=(slot == 0),
                                               bank_done=bank_done, waits=[t_c])
                                    if bank_done:
                                        first_qb = qb_ - slot
                                        c_lo = 64 if first_qb == -1 else 0
                                        c_hi = slot * 128 + qbb
                                        u0 = 128 * first_qb + 64 + c_lo
                                        job["evac"] = mk_evac(bank, c_lo, c_hi, r + dil * u0, c_hi - c_lo)
                                    kb["av"].append(job)
                                kbs.append(kb)
                                kidx += 1
                        run_banded(res, kbs)
                        q_free[qi] = last_tok[0]
                    for c in range(16):
                        tsl = slice(c * 512, (c + 1) * 512)
                        si = c % 2
                        t_sg = P.dma("sync", sgt[si][:], SGa[h * 64:(h + 1) * 64, tsl], waits=[sg_free[si]], sig=S_sg[si])
                        P.emit("scalar", ACT(rec[si][:], acc[64:128, tsl], AF.Ln), waits=[last_acc, rec_free[si]], sig=S_act, chain=False)
                        ta = P.emit("scalar", ACT(rec[si][:], rec[si][:], AF.Exp, scale=-1.0), sig=S_act)
                        tp = P.emit("gpsimd", TT(of[si][:], acc[0:64, tsl], sgt[si][:], ALU.mult),
                                    waits=[last_acc, t_sg, rec_free[si]], sig=S_pool, chain=False)
                        te = P.emit("gpsimd", TT(stg[si][:], of[si][:], rec[si][:], ALU.mult), waits=[ta, st_free[si]], sig=S_pool)
                        sg_free[si] = te
                        rec_free[si] = te
                        t_st = P.dma("gpsimd", OGa[h * 64:(h + 1) * 64, tsl], stg[si][:], waits=[te], sig=S_st[si])
                        st_free[si] = t_st
                        stores.append(t_st)
                    acc_free[h % 2] = [te, ta]
                for e_ in ("sync", "gpsimd", "scalar", "vector", "tensor"):
                    P.wait_sems(e_, S_st)
                P.flush(f"A{l}")

        def phase_C2(l):
            lam_init = 0.8 - 0.6 * math.exp(-0.3 * l)
            with ExitStack() as st:
                ps_s = psum(st, "psCs", [128, 4, 512], F32)
                ps_a = psum(st, "psCa", [128, 4, 512], F32)
                NPT = 3
                PT = [sbuf(st, "PTc", [128, 2, 512], BF16) for _ in range(NPT)]
                ident = sbuf(st, "ident", [128, 128], BF16)
                ones = sbuf(st, "ones", [128, 128], BF16)
                bC = sbuf(st, "bC", [128, 4, 128], BF16)
                KT = [[sbuf(st, "KTc", [72, NT], BF16) for _ in range(2)] for _ in range(2)]
                Vh = [sbuf(st, "Vc", [128, 64, 128], BF16) for _ in range(2)]
                QTt = [[[sbuf(st, "QTc", [72, 512], BF16) for _ in range(3)] for _ in range(2)] for _ in range(2)]
                sgt = [sbuf(st, "sgc", [128, 512], BF16) for _ in range(2)]
                r1 = sbuf(st, "r1", [128, 512], F32)
                o1 = sbuf(st, "o1", [128, 512], F32)
                o2 = sbuf(st, "o2", [128, 512], F32)
                oo = sbuf(st, "oo", [128, 512], F32)
                sq = sbuf(st, "sq", [128, 512], BF16)
                rstd = sbuf(st, "rstd", [128, 512], F32)
                stg = [sbuf(st, "stgC", [128, 512], BF16) for _ in range(2)]
                lam = sbuf(st, "lam", [128, 4, 64], F32)
                lsc = sbuf(st, "lsc", [128, 8], F32)
                coef = sbuf(st, "coef", [128, 1], F32)
                epsb = sbuf(st, "epsbC", [128, 1], F32)
                S_c = P.sem("C_c"); S_k = [P.sem("C_k0"), P.sem("C_k1")]; S_q = [P.sem("C_q0"), P.sem("C_q1")]
                S_st = [P.sem("C_st0"), P.sem("C_st1")]
                P.dma("sync", ident[:], c_id[:, :], sig=S_c)
                P.dma("sync", bC[:], c_bC.rearrange("p (h c) -> p h c", h=4), sig=S_c)
                for i, v in enumerate((lam_q1, lam_k1, lam_q2, lam_k2)):
                    P.dma("sync", lam[:, i, :], bcast_rows(v[l:l + 1, :], 128), sig=S_c)
                P.dma("sync", coef[:], subln_g[l:l + 1, :].rearrange("a d -> d a"), sig=S_c)
                t_c = (S_c, S_c.n)
                P.emit("vector", MSET(ones[:], 1.0))
                P.emit("vector", MSET(epsb[:], EPS))
                P.emit("vector", TT(lam[:, 0, :], lam[:, 0, :], lam[:, 1, :], ALU.mult), waits=[t_c])
                P.emit("vector", TT(lam[:, 2, :], lam[:, 2, :], lam[:, 3, :], ALU.mult))
                P.emit("vector", lambda e: e.reduce_sum(out=lsc[:, 0:1], in_=lam[:, 0, :], axis=AX.X))
                td = P.emit("vector", lambda e: e.reduce_sum(out=lsc[:, 1:2], in_=lam[:, 2, :], axis=AX.X), sig=S_dve)
                ta = P.emit("scalar", ACT(lsc[:, 2:4], lsc[:, 0:2], AF.Exp), waits=[td], sig=S_act)
                P.emit("vector", TT(lsc[:, 4:5], lsc[:, 3:4], lsc[:, 2:3], ALU.subtract), waits=[ta])
                P.emit("vector", TS(lsc[:, 5:6], lsc[:, 4:5], -lam_init, ALU.add))
                t_l = P.emit("vector", TS(coef[:], coef[:], 1.0 - lam_init, ALU.mult), sig=S_dve)
                neglam = lsc[:, 5:6]

                s_free = [None, None]
                pt_free = [None] * NPT
                a_free = [None] * 4
                k_free = [None, None]
                q_free = [None, None]
                sg_free = [None, None]
                st_free = [None, None]
                stores = []

                units = [(h, c) for h in range(4) for c in range(16)]
                loads = {}

                def load_head(h):
                    kb_ = h % 2
                    for m in range(2):
                        f0 = h * 128 + m * 64
                        P.dma("sync", KT[kb_][m][0:64, :], KcT[f0:f0 + 64, :], waits=[k_free[kb_]], sig=S_k[kb_])
                        P.dma("sync", KT[kb_][m][64:72, :], c_kaug[h, :, :], waits=[k_free[kb_]], sig=S_k[kb_])
                    vsrc = Vc[:, h * 128:(h + 1) * 128].rearrange("(j p) d -> p j d", p=128)
                    for q4 in range(4):
                        P.dma("sync", Vh[kb_][:, q4 * 16:(q4 + 1) * 16, :], vsrc[:, q4 * 16:(q4 + 1) * 16, :],
                              waits=[k_free[kb_]], sig=S_k[kb_])
                    loads[("k", h)] = (S_k[kb_], S_k[kb_].n)

                def load_unit(ui):
                    h, c = units[ui]
                    qb_ = ui % 2
                    csl = slice(c * 512, (c + 1) * 512)
                    for m in range(2):
                        f0 = h * 128 + m * 64
                        for ver in range(3):
                            P.dma("sync", QTt[qb_][m][ver][0:64, :], QcT[f0:f0 + 64, csl], waits=[q_free[qb_], sg_free[qb_]], sig=S_q[qb_])
                            P.dma("sync", QTt[qb_][m][ver][64:72, :], c_qaug[ver, :, csl], waits=[q_free[qb_]], sig=S_q[qb_])
                    P.dma("sync", sgt[qb_][:], SGc[h * 128:(h + 1) * 128, csl], waits=[q_free[qb_], sg_free[qb_]], sig=S_q[qb_])
                    loads[("q", ui)] = (S_q[qb_], S_q[qb_].n)

                def unit_range(h, c):
                    m_ = 2.0 ** (-2.0 * (h + 1))
                    js = []
                    for jb in range(64):
                        if jb < 4 * c:
                            dmin = 512 * c - (128 * jb + 127)
                        elif jb > 4 * c + 3:
                            dmin = 128 * jb - (512 * c + 511)
                        else:
                            dmin = 0
                        if m_ * dmin < C_SKIP:
                            js.append(jb)
                    return js[0] // 2, js[-1] // 2

                urange = [unit_range(h, c) for (h, c) in units]
                groups = [(ui, m, g) for ui in range(len(units)) for m in range(2)
                          for g in range(urange[ui][0], urange[ui][1] + 1)]
                NG = len(groups)
                qk_tok = {}
                exp_tok = {}
                pending_ss = []
                unit_state = {}

                def do_qk(G):
                    ui, m, g = groups[G]
                    h, c = units[ui]
                    kb_, qb_ = h % 2, ui % 2
                    sl = G % 2
                    w0 = [s_free[sl], loads[("k", h)], loads[("q", ui)], t_c, t_l]
                    tk = None
                    for mm_ in range(2):
                        jb = 2 * g + mm_
                        out = ps_s[:, sl * 2 + mm_, :]
                        lhsT = KT[kb_][m][0:72, jb * 128:(jb + 1) * 128]
                        Q = QTt[qb_][m]
                        sig = S_pe if mm_ == 1 else None
                        if jb < 4 * c:
                            tk = P.emit("tensor", MM(out, lhsT, Q[0][0:72, :]), waits=w0, sig=sig)
                        elif jb > 4 * c + 3:
                            tk = P.emit("tensor", MM(out, lhsT, Q[1][0:72, :]), waits=w0, sig=sig)
                        else:
                            a = jb - 4 * c
                            if a > 0:
                                P.emit("tensor", MM(ps_s[:, sl * 2 + mm_, 0:a * 128], lhsT, Q[1][0:72, 0:a * 128]), waits=w0)
                            P.emit("tensor", MM(ps_s[:, sl * 2 + mm_, a * 128:(a + 1) * 128], lhsT, Q[2][0:72, a * 128:(a + 1) * 128], True, False), waits=w0)
                            tk = P.emit("tensor", MM(ps_s[:, sl * 2 + mm_, a * 128:(a + 1) * 128], ident[:], bC[:, h, :], False, True),
                                        sig=sig if a == 3 else None)
                            if a < 3:
                                tk = P.emit("tensor", MM(ps_s[:, sl * 2 + mm_, (a + 1) * 128:512], lhsT, Q[0][0:72, (a + 1) * 128:512]),
                                            waits=w0, sig=sig)
                    qk_tok[G] = tk

                def do_exp(G):
                    sl = G % 2
                    pi = G % NPT
                    ta = P.emit("scalar", ACT(PT[pi][:], ps_s[:, sl * 2:sl * 2 + 2, :], AF.Exp),
                                waits=[qk_tok[G], pt_free[pi]], sig=S_act, chain=False)
                    exp_tok[G] = ta
                    s_free[sl] = ta

                def do_av(G):
                    ui, m, g = groups[G]
                    h, c = units[ui]
                    kb_, qb_ = h % 2, ui % 2
                    pi = G % NPT
                    bo, bl_ = 2 * m, 2 * m + 1
                    tk = None
                    glo, ghi = urange[ui]
                    jfirst, jlast = 2 * glo, 2 * ghi + 1
                    for mm_ in range(2):
                        jb = 2 * g + mm_
                        w = [exp_tok[G]]
                        if jb == jfirst:
                            w += [a_free[bo], a_free[bl_]]
                        P.emit("tensor", MM(ps_a[:, bo, :], Vh[kb_][:, jb, :], PT[pi][:, mm_, :], jb == jfirst, jb == jlast), waits=w)
                        tk = P.emit("tensor", MM(ps_a[:, bl_, :], ones[:], PT[pi][:, mm_, :], jb == jfirst, jb == jlast),
                                    sig=S_pe if mm_ == 1 else None)
                    pt_free[pi] = tk
                    if g == ghi:
                        csl = slice(c * 512, (c + 1) * 512)
                        if m == 0:
                            P.emit("vector", RCP(r1[:], ps_a[:, 1, :]), waits=[tk])
                            td = P.emit("vector", TT(o1[:], ps_a[:, 0, :], r1[:], ALU.mult), sig=S_dve)
                            a_free[0] = td; a_free[1] = td
                        else:
                            P.emit("vector", RCP(r1[:], ps_a[:, 3, :]), waits=[tk])
                            P.emit("vector", TT(o2[:], ps_a[:, 2, :], r1[:], ALU.mult))
                            P.emit("vector", STT(oo[:], o2[:], neglam, o1[:], ALU.mult, ALU.add), waits=[t_l])
                            td = P.emit("vector", TT(sq[:], oo[:], oo[:], ALU.mult), sig=S_dve)
                            a_free[3] = td
                            a_free[2] = td
                            pending_ss.append((td, ui))
                            q_free[qb_] = tk
                            if c == 15:
                                k_free[kb_] = tk

                def do_ss():
                    td, ui = pending_ss.pop(0)
                    h, c = units[ui]
                    qb_ = ui % 2
                    csl = slice(c * 512, (c + 1) * 512)
                    tk = P.emit("tensor", MM(ps_a[:, 2, :], ones[:], sq[:]), waits=[td], sig=S_pe)
                    P.emit("scalar", ACT(rstd[:], ps_a[:, 2, :], AF.Ln, scale=1.0 / 128, bias=epsb[:, 0:1]), waits=[tk])
                    ta = P.emit("scalar", ACT(rstd[:], rstd[:], AF.Exp, scale=-0.5), sig=S_act)
                    a_free[2] = ta
                    si = ui % 2
                    P.emit("vector", STT(oo[:], oo[:], coef[:, 0:1], rstd[:], ALU.mult, ALU.mult), waits=[ta])
                    te = P.emit("vector", TT(stg[si][:], oo[:], sgt[qb_][:], ALU.mult), waits=[st_free[si], loads[("q", ui)]], sig=S_dve)
                    t_st = P.dma("gpsimd", OGc[h * 128:(h + 1) * 128, csl], stg[si][:], waits=[te], sig=S_st[si])
                    st_free[si] = t_st
                    stores.append(t_st)
                    sg_free[qb_] = te

                load_head(0)
                load_unit(0)
                load_unit(1)
                for step in range(NG + 2):
                    if step < NG:
                        ui, m, g = groups[step]
                        h, c = units[ui]
                        do_qk(step)
                        do_exp(step)
                        glo, ghi = urange[ui]
                        if m == 0 and g - glo == min(6, ghi - glo) and pending_ss:
                            do_ss()
                        if m == 0 and g - glo == min(7, ghi - glo):
                            if ui >= 1 and ui + 1 < len(units):
                                load_unit(ui + 1)
                            if c == 8 and h + 1 < 4:
                                load_head(h + 1)
                    if step >= 2:
                        do_av(step - 2)
                while pending_ss:
                    do_ss()
                for e_ in ("sync", "gpsimd", "scalar", "vector", "tensor"):
                    P.wait_sems(e_, S_st)
                P.flush(f"C{l}")

        phase_cast()
        done = False
        for l in range(depth):
            for nm, ph in (("T", phase_T), ("B", phase_B), ("A", phase_A), ("C", phase_C2)):
                ph(l)
                if stop_after == f"{nm}{l}":
                    done = True
                    break
            if done:
                break
        if not done:
            phase_T(depth)
    return nc


_CACHE = {}
_RUN_KW = {}


def kernel(x_prompt, x_sample, norm_g, w_in, w_oa, w_ob, w_oc, w_out, b_sink,
           lam_q1, lam_k1, lam_q2, lam_k2, c_subln_g, final_norm_g):
    f = lambda a: np.ascontiguousarray(np.asarray(a, dtype=np.float32))
    x_prompt = f(x_prompt); x_sample = f(x_sample)
    consts = make_consts()
    seg_p = np.zeros(NT, np.int64)
    seg_s = np.arange(NT) // 2048
    qa_p, ka_p = make_aug(seg_p, False)
    qa_s, ka_s = make_aug(seg_s, True)
    shared = dict(norm_g=f(norm_g), w_in=f(w_in), w_oa=f(w_oa), w_ob=f(w_ob), w_oc=f(w_oc), w_out=f(w_out),
                  b_sink=f(b_sink), lam_q1=f(lam_q1), lam_k1=f(lam_k1), lam_q2=f(lam_q2), lam_k2=f(lam_k2),
                  c_subln_g=f(c_subln_g), final_norm_g=f(final_norm_g).reshape(1, D), **consts)
    in_maps = []
    for c in range(NCORES):
        if c < 2 or c >= 6:
            xs = x_prompt[c % 2]
            qa, ka = qa_p, ka_p
        else:
            xs = x_sample[4 * (c - 2):4 * (c - 2) + 4].reshape(NT, D)
            qa, ka = qa_s, ka_s
        in_maps.append(dict(x=np.ascontiguousarray(xs), c_qaug=qa, c_kaug=ka, **shared))
    if "nc" not in _CACHE:
        _CACHE["nc"] = build()
    res = run_bass_kernel_spmd(_CACHE["nc"], in_maps, core_ids=list(range(NCORES)), **_RUN_KW)
    _CACHE["res"] = res
    ys = [np.asarray(r["y"], dtype=np.float32) for r in res.results]
    y_prompt = np.stack([ys[0], ys[1]], 0)
    y_sample = np.concatenate([ys[c].reshape(4, 2048, D) for c in range(2, 6)], 0)
    return (y_prompt, y_sample)
```

```python
import math
from contextlib import ExitStack

import numpy as np
import ml_dtypes

import concourse.bass as bass
import concourse.mybir as mybir
from concourse.bass_utils import run_bass_kernel_spmd

F32 = mybir.dt.float32
BF16 = mybir.dt.bfloat16
AF = mybir.ActivationFunctionType
ALU = mybir.AluOpType
AX = mybir.AxisListType

NT = 8192
D = 1024
DIN = 11520
DEPTH = 4
NCORES = 8
NEG = -30000.0
EPS = 1e-6
C_SKIP = 144.0
A_PAT = ((128, 1), (512, 4), (2048, 16))

SEGS = [
    ("qa", 0, 1536, "q"), ("ka", 1536, 1536, "k"), ("va", 3072, 1536, "v"), ("ga", 4608, 512, "silu"),
    ("qb", 5120, 512, "q"), ("kb", 5632, 128, "k"), ("vb", 5760, 128, "v"), ("gb", 5888, 512, "silu"),
    ("qc", 6400, 512, "q"), ("kc", 6912, 512, "k"), ("vc", 7424, 512, "v"), ("gc", 7936, 512, "silu"),
    ("gm", 8448, 3072, "sigmoid"),
]


class Sem:
    def __init__(self, h):
        self.h = h
        self.n = 0


class Ring:
    def __init__(self, n):
        self.n = n
        self.i = 0
        self.free = [None] * n

    def next(self):
        i = self.i
        self.i = (i + 1) % self.n
        return i, self.free[i]


class Prog:
    ENG = ("sync", "scalar", "gpsimd", "vector", "tensor")

    def __init__(self, nc, stack):
        self.nc = nc
        self.stack = stack
        self.q = {e: [] for e in self.ENG}
        self.waited = {e: {} for e in self.ENG}
        self.sems = {}
        self.uid = 0
        self.engsem = {}
        self.last = {e: None for e in self.ENG}

    def sem(self, name):
        if name not in self.sems:
            self.sems[name] = Sem(self.stack.enter_context(self.nc.semaphore("s_" + name)))
        return self.sems[name]

    def name(self, base):
        self.uid += 1
        return f"{base}_{self.uid}"

    def emit(self, eng, fn, waits=(), sig=None, amt=1, chain=None, is_dma=False):
        comp = fn is not None and not is_dma and eng in self.engsem
        if comp:
            if sig is None:
                sig = self.engsem[eng]
            if chain is None:
                chain = True
            if chain and self.last[eng] is not None:
                waits = list(waits) + [self.last[eng]]
        ws = []
        for t in waits:
            if t is None:
                continue
            sem, val = t
            if self.waited[eng].get(sem, 0) >= val:
                continue
            self.waited[eng][sem] = val
            ws.append((sem.h, val))
        tok = None
        if sig is not None:
            sig.n += amt
            tok = (sig, sig.n)
        if comp:
            self.last[eng] = tok
        self.q[eng].append((ws, fn, sig.h if sig is not None else None, amt))
        return tok

    def dma(self, eng, out, in_, waits=(), sig=None):
        return self.emit(eng, lambda e: e.dma_start(out=out, in_=in_), waits, sig, 16, is_dma=True)

    def wait(self, eng, toks):
        self.emit(eng, None, toks)

    def wait_sems(self, eng, sems):
        self.emit(eng, None, [(s_, s_.n) for s_ in sems if s_.n > 0])

    def flush(self, scope=None, barrier=True):
        if scope is not None:
            with self.nc.named_scope(scope):
                self._flush(barrier)
        else:
            self._flush(barrier)

    def _flush(self, barrier=True):
        with self.nc.Block() as block:
            for name in self.ENG:
                items = self.q[name]

                def body(e, items=items):
                    for ws, fn, sh, amt in items:
                        for h, v in ws:
                            e.wait_ge(h, v)
                        if fn is not None:
                            ins = fn(e)
                            if sh is not None:
                                ins.then_inc(sh, amt)

                getattr(block, name)(body)
        if barrier:
            self.nc.all_engine_barrier()
        self.q = {e: [] for e in self.ENG}


def MM(out, lhsT, rhs, start=True, stop=True):
    return lambda e: e.matmul(out, lhsT=lhsT, rhs=rhs, start=start, stop=stop)


def TR(out, in_, ident):
    return lambda e: e.transpose(out, in_, ident)


def ACT(out, in_, func, scale=1.0, bias=None, accum_out=None):
    kw = {}
    if bias is not None:
        kw["bias"] = bias
    if accum_out is not None:
        kw["accum_out"] = accum_out
    return lambda e: e.activation(out=out, in_=in_, func=func, scale=scale, **kw)


def TT(out, in0, in1, op):
    return lambda e: e.tensor_tensor(out=out, in0=in0, in1=in1, op=op)


def TS(out, in0, s1, op0, s2=None, op1=None):
    if op1 is None:
        return lambda e: e.tensor_scalar(out=out, in0=in0, scalar1=s1, scalar2=None, op0=op0)
    return lambda e: e.tensor_scalar(out=out, in0=in0, scalar1=s1, scalar2=s2, op0=op0, op1=op1)


def STT(out, in0, scalar, in1, op0, op1):
    return lambda e: e.scalar_tensor_tensor(out=out, in0=in0, scalar=scalar, in1=in1, op0=op0, op1=op1)


def CP(out, in_):
    return lambda e: e.tensor_copy(out=out, in_=in_)


def RCP(out, in_):
    return lambda e: e.reciprocal(out=out, in_=in_)


def MSET(ap, v):
    return lambda e: e.memset(ap, v)


def ssl(start, n, step):
    if step == 1:
        return slice(start, start + n)
    return slice(start, start + step * (n - 1) + 1, step)


def bcast_rows(ap2d_row, nparts):
    a = ap2d_row
    return bass.AP(a.tensor, a.offset, [[0, nparts]] + [list(x) for x in a.ap[1:]])


def bf16(a):
    return np.asarray(a, np.float32).astype(ml_dtypes.bfloat16)


def make_consts():
    c = {}
    kp = np.arange(128)[:, None].astype(np.float64)
    cq = np.arange(384)[None, :].astype(np.float64)
    delta = (cq - 128.0) - kp
    bB = np.empty((128, 8, 384), np.float64)
    for h in range(8):
        slope = 2.0 ** (-(h + 1))
        bB[:, h, :] = np.where(np.abs(delta) <= 128, -slope * np.abs(delta), NEG)
    c["c_bB"] = bB.reshape(128, 8 * 384).astype(np.float32)
    cq = np.arange(256)[None, :]
    cc = cq // 128
    qq = (cq % 128).astype(np.float64)
    delta = qq - kp + np.where(cc == 0, -64.0, 64.0)
    bA = np.empty((128, 24, 256), np.float64)
    for g, (_, dil) in enumerate(A_PAT):
        for h in range(8):
            slope = np.float32(2.0 ** (-8.0 * (8 * g + h + 1) / 24))
            bA[:, g * 8 + h, :] = np.where(np.abs(delta) <= 64, -(np.float64(slope) * dil) * np.abs(delta), NEG)
    c["c_bA"] = bA.reshape(128, 24 * 256).astype(np.float32)
    q1 = np.arange(128)[None, :].astype(np.float64)
    bC = np.empty((128, 4, 128), np.float64)
    for h in range(4):
        m = 2.0 ** (-2.0 * (h + 1))
        bC[:, h, :] = -m * np.abs(q1 - kp)
    c["c_bC"] = bf16(bC.reshape(128, 512))
    c["c_id"] = bf16(np.eye(128))
    return c


def make_aug(seg_ids, masked):
    t = np.arange(NT)
    A = (t // 128).astype(np.float64)
    b = (t % 128).astype(np.float64)
    oh = np.zeros((4, NT), np.float64)
    oh[seg_ids, t] = 1.0
    qa = np.zeros((3, 8, NT), np.float64)
    qa[:, 0:4, :] = oh[None]
    al = np.stack([A, b, np.ones(NT), np.ones(NT)])
    qa[0, 4:8] = al
    qa[1, 4:8] = -al
    ka = np.zeros((4, 8, NT), np.float64)
    if masked:
        ka[:, 0:4, :] = (NEG * (1.0 - oh))[None]
    for h in range(4):
        m = 2.0 ** (-2.0 * (h + 1))
        ka[h, 4] = -128.0 * m
        ka[h, 5] = -m
        ka[h, 6] = 128.0 * m * A
        ka[h, 7] = m * b
    return bf16(qa), bf16(ka)


def build(depth=DEPTH, dbg=None, stop_after=None):
    nc = bass.Bass("TRN2", target_bir_lowering=False)
    dbg = dbg or ()

    def din(name, shape, dt=F32):
        return nc.dram_tensor(name, list(shape), dt, kind="ExternalInput").ap()

    def dscr(name, shape, dt=BF16):
        kind = {"kind": "ExternalOutput"} if name in dbg else {}
        return nc.dram_tensor(name, list(shape), dt, **kind).ap()

    x_in = din("x", [NT, D])
    y_out = nc.dram_tensor("y", [NT, D], F32, kind="ExternalOutput").ap()
    norm_g = din("norm_g", [DEPTH, D])
    w_in = din("w_in", [DEPTH, D, DIN])
    w_oa = din("w_oa", [DEPTH, 512, D])
    w_ob = din("w_ob", [DEPTH, 512, D])
    w_oc = din("w_oc", [DEPTH, 512, D])
    w_out = din("w_out", [DEPTH, D, D])
    b_sink = din("b_sink", [DEPTH, 8])
    lam_q1 = din("lam_q1", [DEPTH, 64])
    lam_k1 = din("lam_k1", [DEPTH, 64])
    lam_q2 = din("lam_q2", [DEPTH, 64])
    lam_k2 = din("lam_k2", [DEPTH, 64])
    subln_g = din("c_subln_g", [DEPTH, 128])
    final_g = din("final_norm_g", [1, D])
    c_bB = din("c_bB", [128, 8 * 384], F32)
    c_bA = din("c_bA", [128, 24 * 256], F32)
    c_bC = din("c_bC", [128, 512], BF16)
    c_id = din("c_id", [128, 128], BF16)
    c_qaug = din("c_qaug", [3, 8, NT], BF16)
    c_kaug = din("c_kaug", [4, 8, NT], BF16)

    wb_in = dscr("wb_in", [depth, D, DIN])
    wb_o = [dscr(f"wb_o{i}", [depth, 512, D]) for i in range(3)]
    wb_out = dscr("wb_out", [depth, D, D])
    XR = dscr("XR", [NT, D], F32)
    QaT = dscr("QaT", [1536, NT]); KaT = dscr("KaT", [1536, NT]); Va = dscr("Va", [NT, 1536])
    QbT = dscr("QbT", [512, NT]); KbT = dscr("KbT", [128, NT]); Vb = dscr("Vb", [NT, 128])
    QcT = dscr("QcT", [512, NT]); KcT = dscr("KcT", [512, NT]); Vc = dscr("Vc", [NT, 512])
    SGa = dscr("SGa", [512, NT]); SGb = dscr("SGb", [512, NT]); SGc = dscr("SGc", [512, NT])
    SGm = dscr("SGm", [3072, NT])
    OGa = dscr("OGa", [512, NT]); OGb = dscr("OGb", [512, NT]); OGc = dscr("OGc", [512, NT])
    DEST = {"qa": QaT, "ka": KaT, "va": Va, "ga": SGa, "qb": QbT, "kb": KbT, "vb": Vb, "gb": SGb,
            "qc": QcT, "kc": KcT, "vc": Vc, "gc": SGc, "gm": SGm}

    with ExitStack() as gst:
        P = Prog(nc, gst)
        S_pe, S_act, S_dve, S_pool = P.sem("pe"), P.sem("act"), P.sem("dve"), P.sem("pool")
        P.engsem = {"scalar": S_act, "vector": S_dve, "gpsimd": S_pool}

        def sbuf(st, base, shape, dt):
            return st.enter_context(nc.sbuf_tensor(P.name(base), list(shape), dt))

        def psum(st, base, shape, dt):
            return st.enter_context(nc.psum_tensor(P.name(base), list(shape), dt))

        cast_tok = {}

        def phase_cast():
            for l in range(depth):
                S = P.sem(f"cast{l}")
                for r in range(8):
                    P.dma("gpsimd", wb_in[l, r * 128:(r + 1) * 128, :], w_in[l, r * 128:(r + 1) * 128, :], sig=S)
                for i, w in enumerate((w_oa, w_ob, w_oc)):
                    P.dma("gpsimd", wb_o[i][l, :, :], w[l, :, :], sig=S)
                P.dma("gpsimd", wb_out[l, :, :], w_out[l, :, :], sig=S)
                cast_tok[l] = (S, S.n)
            P.flush("cast", barrier=False)

        def phase_T(l):
            first = l == 0
            last = l == depth
            Xsrc = x_in if l <= 1 else XR
            with ExitStack() as st:
                ps = psum(st, "psT", [128, 6, 512], F32)
                pt = psum(st, "ptT", [128, 2, 1024], BF16)
                ring = Ring(6)
                ident = sbuf(st, "ident", [128, 128], BF16)
                gb = sbuf(st, "gb", [128, D], F32)
                xts = [sbuf(st, "xt", [128, 4, D], F32) for _ in range(2)]
                junk = sbuf(st, "junk", [128, D], BF16)
                ssq = sbuf(st, "ssq", [128, 32], F32)
                epsb = sbuf(st, "epsb", [128, 1], F32)
                S_c = P.sem("T_const"); S_x = [P.sem("T_x0"), P.sem("T_x1")]; S_xst = P.sem("T_xst")
                P.wait("sync", [cast_tok[k] for k in range(min(l, depth - 1) + 1)])
                P.dma("sync", ident[:], c_id[:, :], sig=S_c)
                gsrc = final_g[0:1, :] if last else norm_g[l:l + 1, :]
                P.dma("sync", gb[:], bcast_rows(gsrc, 128), sig=S_c)
                P.emit("vector", MSET(epsb[:], EPS), sig=S_dve)
                t_eps = (S_dve, S_dve.n)
                if not first:
                    ogs = [sbuf(st, "og", [128, 3, 4, 512], BF16) for _ in range(2)]
                    gm = [sbuf(st, "gm", [128, 3, 512], BF16) for _ in range(2)]
                    tmp = [[sbuf(st, "tmp", [128, 512], F32) for _ in range(3)] for _ in range(2)]
                    mixed = sbuf(st, "mixed", [128, 8, 512], BF16)
                    Wo = [sbuf(st, "Wo", [128, 4, D], BF16) for _ in range(3)]
                    wout = sbuf(st, "wout", [128, 8, D], BF16)
                    for i in range(3):
                        P.dma("sync", Wo[i][:], wb_o[i][l - 1].rearrange("(k p) f -> p k f", p=128), sig=S_c)
                    P.dma("sync", wout[:], wb_out[l - 1].rearrange("(k p) f -> p k f", p=128), sig=S_c)
                    S_og = [P.sem("T_og0"), P.sem("T_og1")]; S_gm = [P.sem("T_gm0"), P.sem("T_gm1")]
                if not last:
                    hT = sbuf(st, "hT", [128, 8, 2048], BF16)
                    hb = sbuf(st, "hb", [128, 4, D], BF16)
                    Wt = [sbuf(st, "Wt", [128, 8, 512], BF16) for _ in range(2)]
                    stF = [sbuf(st, "stF", [128, 2048], BF16) for _ in range(3)]
                    stV = [sbuf(st, "stV", [128, 4, 512], BF16) for _ in range(2)]
                    S_w = [P.sem("T_w0"), P.sem("T_w1")]
                    S_stF = [P.sem(f"T_stF{i}") for i in range(3)]
                    S_stV = [P.sem(f"T_stV{i}") for i in range(2)]
                t_c = (S_c, S_c.n)
                state = dict(og_free=[None, None], mixed_free=None, x_free=[[], []], gm_free=[None, None], hb_free=None,
                             w_free=[None, None], stF_free=[None] * 3, stV_free=[None] * 2, wi=0, fi=0, vi=0,
                             pt_free=[None, None], pti=0, stores=[], ld={}, tmp_free=[None, None], ssq_free=[None, None])

                def T1_load(tt):
                    par = tt % 2
                    xsl = slice(tt * 512, tt * 512 + 512)
                    t_x = P.dma("sync", xts[par][:], Xsrc[xsl, :].rearrange("(s p) f -> p s f", p=128),
                                waits=state["x_free"][par], sig=S_x[par])
                    t_og = None
                    if not first:
                        for b_, src in enumerate((OGa, OGb, OGc)):
                            P.dma("sync", ogs[par][:, b_], src.rearrange("(k p) t -> p k t", p=128)[:, :, xsl],
                                  waits=[state["og_free"][par]], sig=S_og[par])
                        t_og = (S_og[par], S_og[par].n)
                    state["ld"][tt] = (t_x, t_og)

                def T1(tt):
                    tok0 = tt * 512
                    par = tt % 2
                    xt = xts[par]
                    xsl = slice(tok0, tok0 + 512)
                    t_x, t_og = state["ld"].pop(tt)
                    if first and tt + 1 < 16:
                        T1_load(tt + 1)
                    x_ready = t_x
                    if not first:
                        og = ogs[par]
                        gsrc_all = SGm.rearrange("(b f p) t -> p b f t", b=3, p=128)
                        t_mixed = None
                        for fc in range(8):
                            sl = fc % 2
                            t_gm = P.dma("sync", gm[sl][:], gsrc_all[:, :, fc, xsl],
                                         waits=[state["gm_free"][sl]], sig=S_gm[sl])
                            if fc == 3 and tt + 1 < 16:
                                T1_load(tt + 1)
                            bks = []
                            for b in range(3):
                                bi, bfree = ring.next()
                                for kc in range(4):
                                    tk = P.emit("tensor", MM(ps[:, bi, :], Wo[b][:, kc, fc * 128:(fc + 1) * 128],
                                                             og[:, b, kc, :], kc == 0, kc == 3),
                                                waits=[bfree, t_og, t_c], sig=S_pe if kc == 3 else None)
                                bks.append((bi, tk))
                            if fc == 7:
                                state["og_free"][par] = bks[-1][1]
                            for b in range(3):
                                bi, tk = bks[b]
                                td = P.emit("vector", TT(tmp[sl][b][:], ps[:, bi, :], gm[sl][:, b, :], ALU.mult),
                                            waits=[tk, t_gm, state["tmp_free"][sl]], sig=S_dve, chain=False)
                                ring.free[bi] = td
                            state["gm_free"][sl] = td
                            w = [td]
                            if fc == 0:
                                w.append(state["mixed_free"])
                            P.emit("gpsimd", TT(tmp[sl][0][:], tmp[sl][0][:], tmp[sl][1][:], ALU.add), waits=w, sig=S_pool, chain=False)
                            t_mixed = P.emit("gpsimd", TT(mixed[:, fc, :], tmp[sl][0][:], tmp[sl][2][:], ALU.add), sig=S_pool)
                            state["tmp_free"][sl] = t_mixed
                        for sub in range(4):
                            for fh in range(2):
                                bi, bfree = ring.next()
                                for kc in range(8):
                                    tk = P.emit("tensor", MM(ps[:, bi, :], mixed[:, kc, sub * 128:(sub + 1) * 128],
                                                             wout[:, kc, fh * 512:(fh + 1) * 512], kc == 0, kc == 7),
                                                waits=[bfree, t_mixed], sig=S_pe if kc == 7 else None)
                                td = P.emit("vector", TT(xt[:, sub, fh * 512:(fh + 1) * 512], ps[:, bi, :],
                                                         xt[:, sub, fh * 512:(fh + 1) * 512], ALU.add),
                                            waits=[tk, t_x], sig=S_dve, chain=False)
                                ring.free[bi] = td
                        state["mixed_free"] = tk
                        x_ready = td
                    frees = []
                    if not first and not last:
                        t_st = P.dma("gpsimd", XR[xsl, :].rearrange("(s p) f -> p s f", p=128), xt[:],
                                     waits=[x_ready], sig=S_xst)
                        frees.append(t_st)
                    q0 = 16 * par
                    for sub in range(4):
                        t_sq = P.emit("scalar", ACT(junk[:], xt[:, sub, :], AF.Square, accum_out=ssq[:, q0 + sub:q0 + sub + 1]),
                                      waits=[x_ready, state["ssq_free"][par]], sig=S_act)
                    P.emit("scalar", ACT(ssq[:, q0 + 4:q0 + 8], ssq[:, q0:q0 + 4], AF.Ln, scale=1.0 / D, bias=epsb[:, 0:1]), waits=[t_eps])
                    t_r = P.emit("scalar", ACT(ssq[:, q0 + 8:q0 + 12], ssq[:, q0 + 4:q0 + 8], AF.Exp, scale=-0.5), sig=S_act)
                    if last:
                        for sub in range(4):
                            td = P.emit("vector", STT(xt[:, sub, :], xt[:, sub, :], ssq[:, q0 + 8 + sub:q0 + 9 + sub], gb[:],
                                                      ALU.mult, ALU.mult), waits=[t_r, t_c], sig=S_dve, chain=False)
                        t_st = P.dma("gpsimd", y_out[xsl, :].rearrange("(s p) f -> p s f", p=128), xt[:],
                                     waits=[td], sig=S_xst)
                        state["x_free"][par] = [t_st]
                        state["ssq_free"][par] = td
                        return
                    for sub in range(4):
                        w = [t_r, t_c]
                        if sub == 0:
                            w.append(state["hb_free"])
                        td = P.emit("vector", STT(hb[:, sub, :], xt[:, sub, :], ssq[:, q0 + 8 + sub:q0 + 9 + sub], gb[:],
                                                  ALU.mult, ALU.mult), waits=w, sig=S_dve, chain=False)
                        pi = state["pti"]; state["pti"] = 1 - pi
                        for kc in range(8):
                            tk = P.emit("tensor", TR(pt[:, pi, kc * 128:(kc + 1) * 128], hb[:, sub, kc * 128:(kc + 1) * 128], ident[:]),
                                        waits=[td, state["pt_free"][pi], t_c], sig=S_pe if kc == 7 else None)
                        off = (tt % 4) * 512 + sub * 128
                        ta = P.emit("scalar", ACT(hT[:, :, off:off + 128], pt[:, pi, :].rearrange("p (k t) -> p k t", k=8), AF.Copy),
                                    waits=[tk], sig=S_act, chain=False)
                        state["pt_free"][pi] = ta
                    state["hb_free"] = tk
                    state["hT_ready"] = ta
                    state["ssq_free"][par] = td
                    frees += [t_sq, td]
                    state["x_free"][par] = frees

                wtiles = [(name, c0, kind, w0, min(512, ncols - w0)) for (name, c0, ncols, kind) in SEGS
                          for w0 in range(0, ncols, 512)]
                NW = len(wtiles)
                w_tok = {}

                def load_w(k):
                    name, c0, kind, w0, wc = wtiles[k % NW]
                    wi = k % 2
                    w_tok[k] = P.dma("sync", Wt[wi][:, :, 0:wc],
                                     wb_in[l].rearrange("(k p) c -> p k c", p=128)[:, :, c0 + w0:c0 + w0 + wc],
                                     waits=[state["w_free"][wi]], sig=S_w[wi])

                def T2(s):
                    tsl = slice(s * 2048, (s + 1) * 2048)
                    t_h = state["hT_ready"]
                    for ti, (name, c0, kind, w0, wc) in enumerate(wtiles):
                        dest = DEST[name]
                        k = s * NW + ti
                        wi = k % 2
                        if k + 1 < 4 * NW:
                            load_w(k + 1)
                        t_w = w_tok[k]
                        if kind == "v":
                            for s16 in range(16):
                                vi = state["vi"]
                                bi, bfree = ring.next()
                                for kc in range(8):
                                    tk = P.emit("tensor", MM(ps[:, bi, 0:wc], hT[:, kc, s16 * 128:(s16 + 1) * 128],
                                                             Wt[wi][:, kc, 0:wc], kc == 0, kc == 7),
                                                waits=[bfree, t_w, t_h], sig=S_pe if kc == 7 else None)
                                w = [tk]
                                if s16 % 4 == 0:
                                    w.append(state["stV_free"][vi])
                                ta = P.emit("scalar", ACT(stV[vi][:, s16 % 4, 0:wc], ps[:, bi, 0:wc], AF.Copy),
                                            waits=w, sig=S_act, chain=False)
                                ring.free[bi] = ta
                                if s16 % 4 == 3:
                                    r0 = s * 2048 + (s16 // 4) * 512
                                    t_st = P.dma("gpsimd", dest[r0:r0 + 512, w0:w0 + wc].rearrange("(s p) c -> p s c", p=128),
                                                 stV[vi][:, :, 0:wc], waits=[ta], sig=S_stV[vi])
                                    state["stV_free"][vi] = t_st
                                    state["vi"] = 1 - vi
                            state["w_free"][wi] = tk
                            continue
                        for sc in range(wc // 128):
                            fi = state["fi"]; state["fi"] = (fi + 1) % 3
                            for i in range(4):
                                bi, bfree = ring.next()
                                for kc in range(8):
                                    tk = P.emit("tensor", MM(ps[:, bi, :], Wt[wi][:, kc, sc * 128:(sc + 1) * 128],
                                                             hT[:, kc, i * 512:(i + 1) * 512], kc == 0, kc == 7),
                                                waits=[bfree, t_w, t_h], sig=S_pe if kc == 7 else None)
                                w = [tk]
                                if i == 0:
                                    w.append(state["stF_free"][fi])
                                o = stF[fi][:, i * 512:(i + 1) * 512]
                                if kind == "q":
                                    te = P.emit("vector", TS(o, ps[:, bi, :], 0.125, ALU.mult), waits=w, sig=S_dve, chain=False)
                                elif kind == "k":
                                    te = P.emit("vector", CP(o, ps[:, bi, :]), waits=w, sig=S_dve, chain=False)
                                elif kind == "silu":
                                    te = P.emit("scalar", ACT(o, ps[:, bi, :], AF.Silu), waits=w, sig=S_act, chain=False)
                                else:
                                    te = P.emit("scalar", ACT(o, ps[:, bi, :], AF.Sigmoid), waits=w, sig=S_act, chain=False)
                                ring.free[bi] = te
                            f0 = w0 + sc * 128
                            t_st = P.dma("gpsimd", dest[f0:f0 + 128, tsl], stF[fi][:], waits=[te], sig=S_stF[fi])
                            state["stF_free"][fi] = t_st
                        state["w_free"][wi] = tk

                if not last:
                    load_w(0)
                T1_load(0)
                for s in range(4):
                    for i in range(4):
                        T1(4 * s + i)
                    if not last:
                        T2(s)
                st_sems = [S_xst]
                if not last:
                    st_sems += S_stF + S_stV
                for e_ in ("sync", "gpsimd", "scalar", "vector", "tensor"):
                    P.wait_sems(e_, st_sems)
                P.flush(f"T{l}")

        def run_banded(res, kbs):
            ps_s, ps_a, PT, SB = res["ps_s"], res["ps_a"], res["PT"], res["SB"]
            NPT = res["NPT"]
            G = 2
            ngroups = (len(kbs) + G - 1) // G
            s_free = res["s_free"]
            sb_free = res["sb_free"]
            pt_free = res["pt_free"]
            a_free = res["a_free"]
            exp_tok = {}
            qk_tok = {}
            bias_tok = {}

            def do_qk(gi):
                sl = gi % 2
                tk = None
                members = kbs[gi * G:(gi + 1) * G]
                for m, kb in enumerate(members):
                    for qi_, (lo, hi, lhsT, rhs) in enumerate(kb["qk"]):
                        islast = (qi_ == len(kb["qk"]) - 1) and (m == len(members) - 1)
                        tk = P.emit("tensor", MM(ps_s[:, sl * G + m, lo:hi], lhsT, rhs, True, True),
                                    waits=[s_free[sl]] + list(kb["waits"]), sig=S_pe if islast else None)
                qk_tok[gi] = tk

            def do_bias(gi):
                sl = gi % 2
                members = kbs[gi * G:(gi + 1) * G]
                cols = members[0]["cols"]
                n = len(members)
                bap = members[0]["bias"]
                bb = bass.AP(bap.tensor, bap.offset, [list(bap.ap[0]), [0, n], list(bap.ap[-1])])
                td = P.emit("vector", TT(SB[:, sl, 0:n, 0:cols], ps_s[:, sl * G:sl * G + n, 0:cols], bb, ALU.add),
                            waits=[qk_tok[gi], sb_free[sl]] + list(members[0]["waits"]), sig=S_dve, chain=False)
                bias_tok[gi] = td
                s_free[sl] = td

            def do_exp(gi):
                sl = gi % 2
                members = kbs[gi * G:(gi + 1) * G]
                cols = members[0]["cols"]
                n = len(members)
                slots = [(gi * G + m) % NPT for m in range(n)]
                w = [bias_tok[gi]] + [pt_free[s_] for s_ in slots]
                ta = P.emit("scalar", ACT(PT[:, slots[0]:slots[0] + n, 0:cols], SB[:, sl, 0:n, 0:cols], AF.Exp),
                            waits=w, sig=S_act, chain=False)
                exp_tok[gi] = ta
                sb_free[sl] = ta

            def do_av(gi):
                members = kbs[gi * G:(gi + 1) * G]
                for m, kb in enumerate(members):
                    for job in kb["av"]:
                        bank, slot = job["bank"], job["slot"]
                        qa, qb = job["qa"], job["qb"]
                        np_ = len(job["parts"])
                        for pi_, (kidx, c, vap) in enumerate(job["parts"]):
                            w = [exp_tok[kidx // G]]
                            if pi_ == 0 and job["bank_first"]:
                                w += list(a_free[bank] or ())
                            w += list(job.get("waits", ()))
                            tk = P.emit("tensor", MM(ps_a[:, bank, slot * 128 + qa:slot * 128 + qb], vap,
                                                     PT[:, kidx % NPT, c * 128 + qa:c * 128 + qb], pi_ == 0, pi_ == np_ - 1),
                                        waits=w, sig=S_pe if (pi_ == np_ - 1) else None)
                            pt_free[kidx % NPT] = tk
                        if job["bank_done"]:
                            a_free[bank] = job["evac"](tk)

            for step in range(ngroups + 2):
                if step < ngroups:
                    do_qk(step)
                    do_bias(step)
                    do_exp(step)
                if step >= 2:
                    do_av(step - 2)

        def phase_B(l):
            with ExitStack() as st:
                ps_s = psum(st, "psBs", [128, 4, 512], F32)
                ps_a = psum(st, "psBa", [128, 2, 512], F32)
                NPT = 8
                PT = sbuf(st, "PT", [128, NPT, 384], BF16)
                ident = sbuf(st, "ident", [128, 128], BF16)
                bias = sbuf(st, "biasB", [128, 8, 384], F32)
                SB = sbuf(st, "SBb", [128, 2, 2, 384], F32)
                QT = [sbuf(st, "QTb", [68, NT], BF16) for _ in range(2)]
                KT = [sbuf(st, "KTb", [68, NT], BF16) for _ in range(2)]
                Vg = sbuf(st, "Vb", [128, 64, 2, 128], BF16)
                sg = [sbuf(st, "sgb", [64, NT], BF16) for _ in range(2)]
                esink = sbuf(st, "esink", [128, 8], F32)
                rec = [sbuf(st, "recB", [64, 512], F32) for _ in range(2)]
                of = [sbuf(st, "ofB", [64, 512], F32) for _ in range(2)]
                rec_free = [None, None]
                stg = [sbuf(st, "stgB", [64, 512], BF16) for _ in range(2)]
                S_c = P.sem("B_c"); S_q = [P.sem("B_q0"), P.sem("B_q1")]; S_st = [P.sem("B_st0"), P.sem("B_st1")]
                P.dma("sync", ident[:], c_id[:, :], sig=S_c)
                P.dma("sync", bias[:], c_bB.rearrange("p (h c) -> p h c", h=8), sig=S_c)
                P.dma("sync", esink[:], bcast_rows(b_sink[l:l + 1, :], 128), sig=S_c)
                for kv in range(2):
                    P.dma("sync", KT[kv][0:64, :], KbT[kv * 64:(kv + 1) * 64, :], sig=S_c)
                    P.dma("sync", KT[kv][64:68, :], c_kaug[0, 0:4, :], sig=S_c)
                    P.dma("sync", QT[kv][64:68, :], c_qaug[2, 0:4, :], sig=S_c)
                P.emit("gpsimd", MSET(Vg[:], 1.0), sig=S_pool)
                t_ms = (S_pool, S_pool.n)
                vsrc = Vb.rearrange("(j p) (k d) -> p j k d", p=128, k=2)
                for q4 in range(4):
                    for kv in range(2):
                        P.dma("sync", Vg[:, q4 * 16:(q4 + 1) * 16, kv, 0:64], vsrc[:, q4 * 16:(q4 + 1) * 16, kv, :],
                              waits=[t_ms], sig=S_c)
                t_c = (S_c, S_c.n)
                t_es = P.emit("scalar", ACT(esink[:], esink[:], AF.Exp), waits=[t_c], sig=S_act)
                res = dict(ps_s=ps_s, ps_a=ps_a, PT=PT, SB=SB, NPT=NPT, s_free=[None, None], sb_free=[None, None],
                           pt_free=[None] * NPT, a_free=[None, None])
                q_free = [None, None]
                st_free = [None, None]
                stores = []
                sti = [0]
                tq = {}

                def load_head(h):
                    qi = h % 2
                    P.dma("sync", QT[qi][0:64, :], QbT[h * 64:(h + 1) * 64, :], waits=[q_free[qi]], sig=S_q[qi])
                    P.dma("sync", sg[qi][:], SGb[h * 64:(h + 1) * 64, :], waits=[q_free[qi]], sig=S_q[qi])
                    tq[h] = (S_q[qi], S_q[qi].n)

                load_head(0)
                for h in range(8):
                    kv = h // 4
                    qi = h % 2
                    if h + 1 < 8:
                        load_head(h + 1)
                    t_q = tq[h]
                    kbs = []
                    last_tok = [None]

                    def mk_evac(bank, q0, h=h, qi=qi):
                        def evac(tk):
                            tsl = slice(q0 * 128, q0 * 128 + 512)
                            ri = sti[0]; sti[0] = 1 - ri
                            ta0 = P.emit("scalar", ACT(rec[ri][:], ps_a[64:128, bank, :], AF.Ln, bias=esink[64:128, h:h + 1]),
                                         waits=[tk, t_es, rec_free[ri]], sig=S_act, chain=False)
                            ta = P.emit("scalar", ACT(rec[ri][:], rec[ri][:], AF.Exp, scale=-1.0), sig=S_act)
                            td = P.emit("vector", TT(of[ri][:], ps_a[0:64, bank, :], sg[qi][:, tsl], ALU.mult),
                                        waits=[tk, t_q, rec_free[ri]], sig=S_dve, chain=False)
                            te = P.emit("gpsimd", TT(stg[ri][:], of[ri][:], rec[ri][:], ALU.mult),
                                        waits=[ta, td, st_free[ri]], sig=S_pool, chain=False)
                            rec_free[ri] = te
                            t_st = P.dma("gpsimd", OGb[h * 64:(h + 1) * 64, tsl], stg[ri][:], waits=[te], sig=S_st[ri])
                            st_free[ri] = t_st
                            stores.append(t_st)
                            last_tok[0] = te
                            return [ta0, td]
                        return evac

                    for j in range(64):
                        qlo = max(j - 1, 0); qhi = min(j + 1, 63)
                        lo = (qlo - (j - 1)) * 128; hi = (qhi - (j - 1) + 1) * 128
                        kb = dict(qk=[(lo, hi, KT[kv][0:68, j * 128:(j + 1) * 128], QT[qi][0:68, qlo * 128:(qhi + 1) * 128])],
                                  bias=bias[:, h, :], cols=384, waits=[t_q, t_c], av=[])
                        done = []
                        if j >= 1:
                            done.append(j - 1)
                        if j == 63:
                            done.append(63)
                        for qb_ in done:
                            parts = [(jj, qb_ - jj + 1, Vg[:, jj, kv, :]) for jj in (qb_ - 1, qb_, qb_ + 1) if 0 <= jj < 64]
                            bank = (qb_ // 4) % 2
                            job = dict(parts=parts, qa=0, qb=128, slot=qb_ % 4, bank=bank, bank_first=(qb_ % 4 == 0),
                                       bank_done=(qb_ % 4 == 3), waits=[t_c])
                            if job["bank_done"]:
                                job["evac"] = mk_evac(bank, qb_ - 3)
                            kb["av"].append(job)
                        kbs.append(kb)
                    run_banded(res, kbs)
                    q_free[qi] = last_tok[0]
                for e_ in ("sync", "gpsimd", "scalar", "vector", "tensor"):
                    P.wait_sems(e_, S_st)
                P.flush(f"B{l}")

        def phase_A(l):
            with ExitStack() as st:
                ps_s = psum(st, "psAs", [128, 4, 512], F32)
                ps_a = psum(st, "psAa", [128, 2, 512], F32)
                NPT = 8
                PT = sbuf(st, "PTa", [128, NPT, 256], BF16)
                ident = sbuf(st, "ident", [128, 128], BF16)
                bA = sbuf(st, "bA", [128, 24, 256], F32)
                SB = sbuf(st, "SBa", [128, 2, 2, 256], F32)
                QT = [sbuf(st, "QTa", [68, NT], BF16) for _ in range(2)]
                KT = [sbuf(st, "KTa", [68, NT], BF16) for _ in range(2)]
                Vg = [sbuf(st, "Va", [128, 64, 128], BF16) for _ in range(2)]
                accs = [sbuf(st, "accA", [128, NT], F32) for _ in range(2)]
                sgt = [sbuf(st, "sga", [64, 512], BF16) for _ in range(2)]
                rec = [sbuf(st, "recA", [64, 512], F32) for _ in range(2)]
                of = [sbuf(st, "ofA", [64, 512], F32) for _ in range(2)]
                rec_free = [None, None]
                stg = [sbuf(st, "stgA", [64, 512], BF16) for _ in range(2)]
                S_c = P.sem("A_c"); S_q = [P.sem("A_q0"), P.sem("A_q1")]
                S_st = [P.sem("A_st0"), P.sem("A_st1")]; S_sg = [P.sem("A_sg0"), P.sem("A_sg1")]
                P.dma("sync", ident[:], c_id[:, :], sig=S_c)
                P.dma("sync", bA[:], c_bA.rearrange("p (h c) -> p h c", h=24), sig=S_c)
                for i in range(2):
                    P.dma("sync", KT[i][64:68, :], c_kaug[0, 0:4, :], sig=S_c)
                    P.dma("sync", QT[i][64:68, :], c_qaug[2, 0:4, :], sig=S_c)
                    P.emit("gpsimd", MSET(Vg[i][:], 1.0), sig=S_pool)
                t_ms = (S_pool, S_pool.n)
                t_c = (S_c, S_c.n)
                res = dict(ps_s=ps_s, ps_a=ps_a, PT=PT, SB=SB, NPT=NPT, s_free=[None, None], sb_free=[None, None],
                           pt_free=[None] * NPT, a_free=[None, None])
                q_free = [None, None]
                st_free = [None, None]
                sg_free = [None, None]
                stores = []
                acc_free = [None, None]
                ui = 0
                tq = {}

                def load_unit(u):
                    h, g = divmod(u, 3)
                    dil = A_PAT[g][1]
                    qi = u % 2
                    f0 = g * 512 + h * 64
                    P.dma("sync", QT[qi][0:64, :], QaT[f0:f0 + 64, :], waits=[q_free[qi]], sig=S_q[qi])
                    P.dma("sync", KT[qi][0:64, :], KaT[f0:f0 + 64, :], waits=[q_free[qi]], sig=S_q[qi])
                    U = NT // dil
                    nb = U // 128
                    vsrc = Va.rearrange("(u d) c -> d u c", d=dil)
                    for r in range(dil):
                        vr = vsrc[r, :, f0:f0 + 64].rearrange("(j p) c -> p j c", p=128)
                        for j0 in range(0, nb, 16):
                            j1 = min(nb, j0 + 16)
                            P.dma("sync", Vg[qi][:, r * nb + j0:r * nb + j1, 0:64], vr[:, j0:j1, :],
                                  waits=[q_free[qi], t_ms], sig=S_q[qi])
                    tq[u] = (S_q[qi], S_q[qi].n)

                load_unit(0)
                for h in range(8):
                    last_acc = None
                    acc = accs[h % 2]
                    for g, (_, dil) in enumerate(A_PAT):
                        qi = ui % 2
                        if ui + 1 < 24:
                            load_unit(ui + 1)
                        t_q = tq[ui]
                        ui += 1
                        f0 = g * 512 + h * 64
                        U = NT // dil
                        nb = U // 128
                        kbs = []
                        last_tok = [None]

                        def mk_evac(bank, lo, hi, t0, n, g=g, dil=dil, acc=acc, h=h):
                            def evac(tk):
                                nonlocal last_acc
                                dst = acc[:, ssl(t0, n, dil)]
                                w = [tk]
                                if g == 0:
                                    w += list(acc_free[h % 2] or ())
                                    td = P.emit("vector", CP(dst, ps_a[:, bank, lo:hi]), waits=w, sig=S_dve, chain=False)
                                else:
                                    td = P.emit("vector", TT(dst, ps_a[:, bank, lo:hi], dst, ALU.add), waits=w, sig=S_dve)
                                last_tok[0] = td
                                last_acc = td
                                return [td]
                            return evac

                        kidx = 0
                        for r in range(dil):
                            for j in range(nb):
                                tb = r + dil * 128 * j
                                ulo = max(128 * j - 64, 0); uhi = min(128 * j + 192, U)
                                lo = ulo - (128 * j - 64); hi = uhi - (128 * j - 64)
                                kb = dict(qk=[(lo, hi, KT[qi][0:68, ssl(tb, 128, dil)],
                                               QT[qi][0:68, ssl(r + dil * ulo, uhi - ulo, dil)])],
                                          bias=bA[:, g * 8 + h, :], cols=256, waits=[t_q, t_c], av=[])
                                done = [j - 1]
                                if j == nb - 1:
                                    done.append(nb - 1)
                                for qb_ in done:
                                    parts = []
                                    for jj in (qb_, qb_ + 1):
                                        if 0 <= jj < nb:
                                            parts.append((kidx - (j - jj), qb_ - jj + 1, Vg[qi][:, r * nb + jj, :]))
                                    qa = 64 if qb_ == -1 else 0
                                    qbb = 64 if qb_ == nb - 1 else 128
                                    seq = qb_ + 1
                                    bank = (seq // 4) % 2
                                    slot = seq % 4
                                    bank_done = (slot == 3) or (qb_ == nb - 1)
                                    job = dict(parts=parts, qa=qa, qb=qbb, slot=slot, bank=bank, bank_first=(slot == 0),
                                               bank_done=bank_done, waits=[t_c])
                                    if bank_done:
                                        first_qb = qb_ - slot
                                        c_lo = 64 if first_qb == -1 else 0
                                        c_hi = slot * 128 + qbb
                                        u0 = 128 * first_qb + 64 + c_lo
                                        job["evac"] = mk_evac(bank, c_lo, c_hi, r + dil * u0, c_hi - c_lo)
                                    kb["av"].append(job)
                                kbs.append(kb)
                                kidx += 1
                        run_banded(res, kbs)
                        q_free[qi] = last_tok[0]
                    for c in range(16):
                        tsl = slice(c * 512, (c + 1) * 512)
                        si = c % 2
                        t_sg = P.dma("sync", sgt[si][:], SGa[h * 64:(h + 1) * 64, tsl], waits=[sg_free[si]], sig=S_sg[si])
                        P.emit("scalar", ACT(rec[si][:], acc[64:128, tsl], AF.Ln), waits=[last_acc, rec_free[si]], sig=S_act, chain=False)
                        ta = P.emit("scalar", ACT(rec[si][:], rec[si][:], AF.Exp, scale=-1.0), sig=S_act)
                        tp = P.emit("gpsimd", TT(of[si][:], acc[0:64, tsl], sgt[si][:], ALU.mult),
                                    waits=[last_acc, t_sg, rec_free[si]], sig=S_pool, chain=False)
                        te = P.emit("gpsimd", TT(stg[si][:], of[si][:], rec[si][:], ALU.mult), waits=[ta, st_free[si]], sig=S_pool)
                        sg_free[si] = te
                        rec_free[si] = te
                        t_st = P.dma("gpsimd", OGa[h * 64:(h + 1) * 64, tsl], stg[si][:], waits=[te], sig=S_st[si])
                        st_free[si] = t_st
                        stores.append(t_st)
                    acc_free[h % 2] = [te, ta]
                for e_ in ("sync", "gpsimd", "scalar", "vector", "tensor"):
                    P.wait_sems(e_, S_st)
                P.flush(f"A{l}")

        def phase_C2(l):
            lam_init = 0.8 - 0.6 * math.exp(-0.3 * l)
            with ExitStack() as st:
                ps_s = psum(st, "psCs", [128, 4, 512], F32)
                ps_a = psum(st, "psCa", [128, 4, 512], F32)
                NPT = 3
                PT = [sbuf(st, "PTc", [128, 2, 512], BF16) for _ in range(NPT)]
                ident = sbuf(st, "ident", [128, 128], BF16)
                ones = sbuf(st, "ones", [128, 128], BF16)
                bC = sbuf(st, "bC", [128, 4, 128], BF16)
                KT = [[sbuf(st, "KTc", [72, NT], BF16) for _ in range(2)] for _ in range(2)]
                Vh = [sbuf(st, "Vc", [128, 64, 128], BF16) for _ in range(2)]
                QTt = [[[sbuf(st, "QTc", [72, 512], BF16) for _ in range(3)] for _ in range(2)] for _ in range(2)]
                sgt = [sbuf(st, "sgc", [128, 512], BF16) for _ in range(2)]
                r1 = sbuf(st, "r1", [128, 512], F32)
                o1 = sbuf(st, "o1", [128, 512], F32)
                o2 = sbuf(st, "o2", [128, 512], F32)
                oo = sbuf(st, "oo", [128, 512], F32)
                sq = sbuf(st, "sq", [128, 512], BF16)
                rstd = sbuf(st, "rstd", [128, 512], F32)
                stg = [sbuf(st, "stgC", [128, 512], BF16) for _ in range(2)]
                lam = sbuf(st, "lam", [128, 4, 64], F32)
                lsc = sbuf(st, "lsc", [128, 8], F32)
                coef = sbuf(st, "coef", [128, 1], F32)
                epsb = sbuf(st, "epsbC", [128, 1], F32)
                S_c = P.sem("C_c"); S_k = [P.sem("C_k0"), P.sem("C_k1")]; S_q = [P.sem("C_q0"), P.sem("C_q1")]
                S_st = [P.sem("C_st0"), P.sem("C_st1")]
                P.dma("sync", ident[:], c_id[:, :], sig=S_c)
                P.dma("sync", bC[:], c_bC.rearrange("p (h c) -> p h c", h=4), sig=S_c)
                for i, v in enumerate((lam_q1, lam_k1, lam_q2, lam_k2)):
                    P.dma("sync", lam[:, i, :], bcast_rows(v[l:l + 1, :], 128), sig=S_c)
                P.dma("sync", coef[:], subln_g[l:l + 1, :].rearrange("a d -> d a"), sig=S_c)
                t_c = (S_c, S_c.n)
                P.emit("vector", MSET(ones[:], 1.0))
                P.emit("vector", MSET(epsb[:], EPS))
                P.emit("vector", TT(lam[:, 0, :], lam[:, 0, :], lam[:, 1, :], ALU.mult), waits=[t_c])
                P.emit("vector", TT(lam[:, 2, :], lam[:, 2, :], lam[:, 3, :], ALU.mult))
                P.emit("vector", lambda e: e.reduce_sum(out=lsc[:, 0:1], in_=lam[:, 0, :], axis=AX.X))
                td = P.emit("vector", lambda e: e.reduce_sum(out=lsc[:, 1:2], in_=lam[:, 2, :], axis=AX.X), sig=S_dve)
                ta = P.emit("scalar", ACT(lsc[:, 2:4], lsc[:, 0:2], AF.Exp), waits=[td], sig=S_act)
                P.emit("vector", TT(lsc[:, 4:5], lsc[:, 3:4], lsc[:, 2:3], ALU.subtract), waits=[ta])
                P.emit("vector", TS(lsc[:, 5:6], lsc[:, 4:5], -lam_init, ALU.add))
                t_l = P.emit("vector", TS(coef[:], coef[:], 1.0 - lam_init, ALU.mult), sig=S_dve)
                neglam = lsc[:, 5:6]

                s_free = [None, None]
                pt_free = [None] * NPT
                a_free = [None] * 4
                k_free = [None, None]
                q_free = [None, None]
                sg_free = [None, None]
                st_free = [None, None]
                stores = []

                units = [(h, c) for h in range(4) for c in range(16)]
                loads = {}

                def load_head(h):
                    kb_ = h % 2
                    for m in range(2):
                        f0 = h * 128 + m * 64
                        P.dma("sync", KT[kb_][m][0:64, :], KcT[f0:f0 + 64, :], waits=[k_free[kb_]], sig=S_k[kb_])
                        P.dma("sync", KT[kb_][m][64:72, :], c_kaug[h, :, :], waits=[k_free[kb_]], sig=S_k[kb_])
                    vsrc = Vc[:, h * 128:(h + 1) * 128].rearrange("(j p) d -> p j d", p=128)
                    for q4 in range(4):
                        P.dma("sync", Vh[kb_][:, q4 * 16:(q4 + 1) * 16, :], vsrc[:, q4 * 16:(q4 + 1) * 16, :],
                              waits=[k_free[kb_]], sig=S_k[kb_])
                    loads[("k", h)] = (S_k[kb_], S_k[kb_].n)

                def load_unit(ui):
                    h, c = units[ui]
                    qb_ = ui % 2
                    csl = slice(c * 512, (c + 1) * 512)
                    for m in range(2):
                        f0 = h * 128 + m * 64
                        for ver in range(3):
                            P.dma("sync", QTt[qb_][m][ver][0:64, :], QcT[f0:f0 + 64, csl], waits=[q_free[qb_], sg_free[qb_]], sig=S_q[qb_])
                            P.dma("sync", QTt[qb_][m][ver][64:72, :], c_qaug[ver, :, csl], waits=[q_free[qb_]], sig=S_q[qb_])
                    P.dma("sync", sgt[qb_][:], SGc[h * 128:(h + 1) * 128, csl], waits=[q_free[qb_], sg_free[qb_]], sig=S_q[qb_])
                    loads[("q", ui)] = (S_q[qb_], S_q[qb_].n)

                def unit_range(h, c):
                    m_ = 2.0 ** (-2.0 * (h + 1))
                    js = []
                    for jb in range(64):
                        if jb < 4 * c:
                            dmin = 512 * c - (128 * jb + 127)
                        elif jb > 4 * c + 3:
                            dmin = 128 * jb - (512 * c + 511)
                        else:
                            dmin = 0
                        if m_ * dmin < C_SKIP:
                            js.append(jb)
                    return js[0] // 2, js[-1] // 2

                urange = [unit_range(h, c) for (h, c) in units]
                groups = [(ui, m, g) for ui in range(len(units)) for m in range(2)
                          for g in range(urange[ui][0], urange[ui][1] + 1)]
                NG = len(groups)
                qk_tok = {}
                exp_tok = {}
                pending_ss = []
                unit_state = {}

                def do_qk(G):
                    ui, m, g = groups[G]
                    h, c = units[ui]
                    kb_, qb_ = h % 2, ui % 2
                    sl = G % 2
                    w0 = [s_free[sl], loads[("k", h)], loads[("q", ui)], t_c, t_l]
                    tk = None
                    for mm_ in range(2):
                        jb = 2 * g + mm_
                        out = ps_s[:, sl * 2 + mm_, :]
                        lhsT = KT[kb_][m][0:72, jb * 128:(jb + 1) * 128]
                        Q = QTt[qb_][m]
                        sig = S_pe if mm_ == 1 else None
                        if jb < 4 * c:
                            tk = P.emit("tensor", MM(out, lhsT, Q[0][0:72, :]), waits=w0, sig=sig)
                        elif jb > 4 * c + 3:
                            tk = P.emit("tensor", MM(out, lhsT, Q[1][0:72, :]), waits=w0, sig=sig)
                        else:
                            a = jb - 4 * c
                            if a > 0:
                                P.emit("tensor", MM(ps_s[:, sl * 2 + mm_, 0:a * 128], lhsT, Q[1][0:72, 0:a * 128]), waits=w0)
                            P.emit("tensor", MM(ps_s[:, sl * 2 + mm_, a * 128:(a + 1) * 128], lhsT, Q[2][0:72, a * 128:(a + 1) * 128], True, False), waits=w0)
                            tk = P.emit("tensor", MM(ps_s[:, sl * 2 + mm_, a * 128:(a + 1) * 128], ident[:], bC[:, h, :], False, True),
                                        sig=sig if a == 3 else None)
                            if a < 3:
                                tk = P.emit("tensor", MM(ps_s[:, sl * 2 + mm_, (a + 1) * 128:512], lhsT, Q[0][0:72, (a + 1) * 128:512]),
                                            waits=w0, sig=sig)
                    qk_tok[G] = tk

                def do_exp(G):
                    sl = G % 2
                    pi = G % NPT
                    ta = P.emit("scalar", ACT(PT[pi][:], ps_s[:, sl * 2:sl * 2 + 2, :], AF.Exp),
                                waits=[qk_tok[G], pt_free[pi]], sig=S_act, chain=False)
                    exp_tok[G] = ta
                    s_free[sl] = ta

                def do_av(G):
                    ui, m, g = groups[G]
                    h, c = units[ui]
                    kb_, qb_ = h % 2, ui % 2
                    pi = G % NPT
                    bo, bl_ = 2 * m, 2 * m + 1
                    tk = None
                    glo, ghi = urange[ui]
                    jfirst, jlast = 2 * glo, 2 * ghi + 1
                    for mm_ in range(2):
                        jb = 2 * g + mm_
                        w = [exp_tok[G]]
                        if jb == jfirst:
                            w += [a_free[bo], a_free[bl_]]
                        P.emit("tensor", MM(ps_a[:, bo, :], Vh[kb_][:, jb, :], PT[pi][:, mm_, :], jb == jfirst, jb == jlast), waits=w)
                        tk = P.emit("tensor", MM(ps_a[:, bl_, :], ones[:], PT[pi][:, mm_, :], jb == jfirst, jb == jlast),
                                    sig=S_pe if mm_ == 1 else None)
                    pt_free[pi] = tk
                    if g == ghi:
                        csl = slice(c * 512, (c + 1) * 512)
                        if m == 0:
                            P.emit("vector", RCP(r1[:], ps_a[:, 1, :]), waits=[tk])
                            td = P.emit("vector", TT(o1[:], ps_a[:, 0, :], r1[:], ALU.mult), sig=S_dve)
                            a_free[0] = td; a_free[1] = td
                        else:
                            P.emit("vector", RCP(r1[:], ps_a[:, 3, :]), waits=[tk])
                            P.emit("vector", TT(o2[:], ps_a[:, 2, :], r1[:], ALU.mult))
                            P.emit("vector", STT(oo[:], o2[:], neglam, o1[:], ALU.mult, ALU.add), waits=[t_l])
                            td = P.emit("vector", TT(sq[:], oo[:], oo[:], ALU.mult), sig=S_dve)
                            a_free[3] = td
                            a_free[2] = td
                            pending_ss.append((td, ui))
                            q_free[qb_] = tk
                            if c == 15:
                                k_free[kb_] = tk

                def do_ss():
                    td, ui = pending_ss.pop(0)
                    h, c = units[ui]
                    qb_ = ui % 2
                    csl = slice(c * 512, (c + 1) * 512)
                    tk = P.emit("tensor", MM(ps_a[:, 2, :], ones[:], sq[:]), waits=[td], sig=S_pe)
                    P.emit("scalar", ACT(rstd[:], ps_a[:, 2, :], AF.Ln, scale=1.0 / 128, bias=epsb[:, 0:1]), waits=[tk])
                    ta = P.emit("scalar", ACT(rstd[:], rstd[:], AF.Exp, scale=-0.5), sig=S_act)
                    a_free[2] = ta
                    si = ui % 2
                    P.emit("vector", STT(oo[:], oo[:], coef[:, 0:1], rstd[:], ALU.mult, ALU.mult), waits=[ta])
                    te = P.emit("vector", TT(stg[si][:], oo[:], sgt[qb_][:], ALU.mult), waits=[st_free[si], loads[("q", ui)]], sig=S_dve)
                    t_st = P.dma("gpsimd", OGc[h * 128:(h + 1) * 128, csl], stg[si][:], waits=[te], sig=S_st[si])
                    st_free[si] = t_st
                    stores.append(t_st)
                    sg_free[qb_] = te

                load_head(0)
                load_unit(0)
                load_unit(1)
                for step in range(NG + 2):
                    if step < NG:
                        ui, m, g = groups[step]
                        h, c = units[ui]
                        do_qk(step)
                        do_exp(step)
                        glo, ghi = urange[ui]
                        if m == 0 and g - glo == min(6, ghi - glo) and pending_ss:
                            do_ss()
                        if m == 0 and g - glo == min(7, ghi - glo):
                            if ui >= 1 and ui + 1 < len(units):
                                load_unit(ui + 1)
                            if c == 8 and h + 1 < 4:
                                load_head(h + 1)
                    if step >= 2:
                        do_av(step - 2)
                while pending_ss:
                    do_ss()
                for e_ in ("sync", "gpsimd", "scalar", "vector", "tensor"):
                    P.wait_sems(e_, S_st)
                P.flush(f"C{l}")

        phase_cast()
        done = False
        for l in range(depth):
            for nm, ph in (("T", phase_T), ("B", phase_B), ("A", phase_A), ("C", phase_C2)):
                ph(l)
                if stop_after == f"{nm}{l}":
                    done = True
                    break
            if done:
                break
        if not done:
            phase_T(depth)
    return nc


_CACHE = {}
_RUN_KW = {}


def kernel(x_prompt, x_sample, norm_g, w_in, w_oa, w_ob, w_oc, w_out, b_sink,
           lam_q1, lam_k1, lam_q2, lam_k2, c_subln_g, final_norm_g):
    f = lambda a: np.ascontiguousarray(np.asarray(a, dtype=np.float32))
    x_prompt = f(x_prompt); x_sample = f(x_sample)
    consts = make_consts()
    seg_p = np.zeros(NT, np.int64)
    seg_s = np.arange(NT) // 2048
    qa_p, ka_p = make_aug(seg_p, False)
    qa_s, ka_s = make_aug(seg_s, True)
    shared = dict(norm_g=f(norm_g), w_in=f(w_in), w_oa=f(w_oa), w_ob=f(w_ob), w_oc=f(w_oc), w_out=f(w_out),
                  b_sink=f(b_sink), lam_q1=f(lam_q1), lam_k1=f(lam_k1), lam_q2=f(lam_q2), lam_k2=f(lam_k2),
                  c_subln_g=f(c_subln_g), final_norm_g=f(final_norm_g).reshape(1, D), **consts)
    PROMPT_CORES = (0, 4)
    SAMPLE_CORES = (1, 2, 5, 6)
    zeros = np.zeros((NT, D), np.float32)
    in_maps = []
    for c in range(NCORES):
        if c in PROMPT_CORES:
            xs = x_prompt[PROMPT_CORES.index(c)]
            qa, ka = qa_p, ka_p
        elif c in SAMPLE_CORES:
            i = SAMPLE_CORES.index(c)
            xs = x_sample[4 * i:4 * i + 4].reshape(NT, D)
            qa, ka = qa_s, ka_s
        else:
            xs = zeros
            qa, ka = qa_p, ka_p
        in_maps.append(dict(x=np.ascontiguousarray(xs), c_qaug=qa, c_kaug=ka, **shared))
    if "nc" not in _CACHE:
        _CACHE["nc"] = build()
    res = run_bass_kernel_spmd(_CACHE["nc"], in_maps, core_ids=list(range(NCORES)), **_RUN_KW)
    _CACHE["res"] = res
    ys = [np.asarray(r["y"], dtype=np.float32) for r in res.results]
    y_prompt = np.stack([ys[c] for c in PROMPT_CORES], 0)
    y_sample = np.concatenate([ys[c].reshape(4, 2048, D) for c in SAMPLE_CORES], 0)
    return (y_prompt, y_sample)
```

```python
import math
from contextlib import ExitStack

import numpy as np
import ml_dtypes

import concourse.bass as bass
import concourse.mybir as mybir
from concourse.bass_utils import run_bass_kernel_spmd

F32 = mybir.dt.float32
BF16 = mybir.dt.bfloat16
AF = mybir.ActivationFunctionType
ALU = mybir.AluOpType
AX = mybir.AxisListType

NT = 8192
D = 1024
DIN = 11520
DEPTH = 4
NCORES = 8
NEG = -30000.0
EPS = 1e-6
C_SKIP = 144.0
A_PAT = ((128, 1), (512, 4), (2048, 16))

SEGS = [
    ("qa", 0, 1536, "q"), ("ka", 1536, 1536, "k"), ("va", 3072, 1536, "v"), ("ga", 4608, 512, "silu"),
    ("qb", 5120, 512, "q"), ("kb", 5632, 128, "k"), ("vb", 5760, 128, "v"), ("gb", 5888, 512, "silu"),
    ("qc", 6400, 512, "q"), ("kc", 6912, 512, "k"), ("vc", 7424, 512, "v"), ("gc", 7936, 512, "silu"),
    ("gm", 8448, 3072, "sigmoid"),
]


class Sem:
    def __init__(self, h):
        self.h = h
        self.n = 0


class Ring:
    def __init__(self, n):
        self.n = n
        self.i = 0
        self.free = [None] * n

    def next(self):
        i = self.i
        self.i = (i + 1) % self.n
        return i, self.free[i]


class Prog:
    ENG = ("sync", "scalar", "gpsimd", "vector", "tensor")

    def __init__(self, nc, stack):
        self.nc = nc
        self.stack = stack
        self.q = {e: [] for e in self.ENG}
        self.waited = {e: {} for e in self.ENG}
        self.sems = {}
        self.uid = 0
        self.engsem = {}
        self.last = {e: None for e in self.ENG}

    def sem(self, name):
        if name not in self.sems:
            self.sems[name] = Sem(self.stack.enter_context(self.nc.semaphore("s_" + name)))
        return self.sems[name]

    def name(self, base):
        self.uid += 1
        return f"{base}_{self.uid}"

    def emit(self, eng, fn, waits=(), sig=None, amt=1, chain=None, is_dma=False):
        comp = fn is not None and not is_dma and eng in self.engsem
        if comp:
            if sig is None:
                sig = self.engsem[eng]
            if chain is None:
                chain = True
            if chain and self.last[eng] is not None:
                waits = list(waits) + [self.last[eng]]
        ws = []
        for t in waits:
            if t is None:
                continue
            sem, val = t
            if self.waited[eng].get(sem, 0) >= val:
                continue
            self.waited[eng][sem] = val
            ws.append((sem.h, val))
        tok = None
        if sig is not None:
            sig.n += amt
            tok = (sig, sig.n)
        if comp:
            self.last[eng] = tok
        self.q[eng].append((ws, fn, sig.h if sig is not None else None, amt))
        return tok

    def dma(self, eng, out, in_, waits=(), sig=None):
        return self.emit(eng, lambda e: e.dma_start(out=out, in_=in_), waits, sig, 16, is_dma=True)

    def wait(self, eng, toks):
        self.emit(eng, None, toks)

    def wait_sems(self, eng, sems):
        self.emit(eng, None, [(s_, s_.n) for s_ in sems if s_.n > 0])

    def flush(self, scope=None, barrier=True):
        if scope is not None:
            with self.nc.named_scope(scope):
                self._flush(barrier)
        else:
            self._flush(barrier)

    def _flush(self, barrier=True):
        with self.nc.Block() as block:
            for name in self.ENG:
                items = self.q[name]

                def body(e, items=items):
                    for ws, fn, sh, amt in items:
                        for h, v in ws:
                            e.wait_ge(h, v)
                        if fn is not None:
                            ins = fn(e)
                            if sh is not None:
                                ins.then_inc(sh, amt)

                getattr(block, name)(body)
        if barrier:
            self.nc.all_engine_barrier()
        self.q = {e: [] for e in self.ENG}


def MM(out, lhsT, rhs, start=True, stop=True):
    return lambda e: e.matmul(out, lhsT=lhsT, rhs=rhs, start=start, stop=stop)


def TR(out, in_, ident):
    return lambda e: e.transpose(out, in_, ident)


def ACT(out, in_, func, scale=1.0, bias=None, accum_out=None):
    kw = {}
    if bias is not None:
        kw["bias"] = bias
    if accum_out is not None:
        kw["accum_out"] = accum_out
    return lambda e: e.activation(out=out, in_=in_, func=func, scale=scale, **kw)


def TT(out, in0, in1, op):
    return lambda e: e.tensor_tensor(out=out, in0=in0, in1=in1, op=op)


def TS(out, in0, s1, op0, s2=None, op1=None):
    if op1 is None:
        return lambda e: e.tensor_scalar(out=out, in0=in0, scalar1=s1, scalar2=None, op0=op0)
    return lambda e: e.tensor_scalar(out=out, in0=in0, scalar1=s1, scalar2=s2, op0=op0, op1=op1)


def STT(out, in0, scalar, in1, op0, op1):
    return lambda e: e.scalar_tensor_tensor(out=out, in0=in0, scalar=scalar, in1=in1, op0=op0, op1=op1)


def CP(out, in_):
    return lambda e: e.tensor_copy(out=out, in_=in_)


def RCP(out, in_):
    return lambda e: e.reciprocal(out=out, in_=in_)


def MSET(ap, v):
    return lambda e: e.memset(ap, v)


def ssl(start, n, step):
    if step == 1:
        return slice(start, start + n)
    return slice(start, start + step * (n - 1) + 1, step)


def bcast_rows(ap2d_row, nparts):
    a = ap2d_row
    return bass.AP(a.tensor, a.offset, [[0, nparts]] + [list(x) for x in a.ap[1:]])


def bf16(a):
    return np.asarray(a, np.float32).astype(ml_dtypes.bfloat16)


def make_consts():
    c = {}
    kp = np.arange(128)[:, None].astype(np.float64)
    cq = np.arange(384)[None, :].astype(np.float64)
    delta = (cq - 128.0) - kp
    bB = np.empty((128, 8, 384), np.float64)
    for h in range(8):
        slope = 2.0 ** (-(h + 1))
        bB[:, h, :] = np.where(np.abs(delta) <= 128, -slope * np.abs(delta), NEG)
    c["c_bB"] = bB.reshape(128, 8 * 384).astype(np.float32)
    cq = np.arange(256)[None, :]
    cc = cq // 128
    qq = (cq % 128).astype(np.float64)
    delta = qq - kp + np.where(cc == 0, -64.0, 64.0)
    bA = np.empty((128, 24, 256), np.float64)
    for g, (_, dil) in enumerate(A_PAT):
        for h in range(8):
            slope = np.float32(2.0 ** (-8.0 * (8 * g + h + 1) / 24))
            bA[:, g * 8 + h, :] = np.where(np.abs(delta) <= 64, -(np.float64(slope) * dil) * np.abs(delta), NEG)
    c["c_bA"] = bA.reshape(128, 24 * 256).astype(np.float32)
    q1 = np.arange(128)[None, :].astype(np.float64)
    bC = np.empty((128, 4, 128), np.float64)
    for h in range(4):
        m = 2.0 ** (-2.0 * (h + 1))
        bC[:, h, :] = -m * np.abs(q1 - kp)
    c["c_bC"] = bf16(bC.reshape(128, 512))
    c["c_id"] = bf16(np.eye(128))
    return c


def make_aug(seg_ids, masked):
    t = np.arange(NT)
    A = (t // 128).astype(np.float64)
    b = (t % 128).astype(np.float64)
    oh = np.zeros((4, NT), np.float64)
    oh[seg_ids, t] = 1.0
    qa = np.zeros((3, 8, NT), np.float64)
    qa[:, 0:4, :] = oh[None]
    al = np.stack([A, b, np.ones(NT), np.ones(NT)])
    qa[0, 4:8] = al
    qa[1, 4:8] = -al
    ka = np.zeros((4, 8, NT), np.float64)
    if masked:
        ka[:, 0:4, :] = (NEG * (1.0 - oh))[None]
    for h in range(4):
        m = 2.0 ** (-2.0 * (h + 1))
        ka[h, 4] = -128.0 * m
        ka[h, 5] = -m
        ka[h, 6] = 128.0 * m * A
        ka[h, 7] = m * b
    return bf16(qa), bf16(ka)


def build(depth=DEPTH, dbg=None, stop_after=None):
    nc = bass.Bass("TRN2", target_bir_lowering=False)
    dbg = dbg or ()

    def din(name, shape, dt=F32):
        return nc.dram_tensor(name, list(shape), dt, kind="ExternalInput").ap()

    def dscr(name, shape, dt=BF16):
        kind = {"kind": "ExternalOutput"} if name in dbg else {}
        return nc.dram_tensor(name, list(shape), dt, **kind).ap()

    x_in = din("x", [NT, D])
    y_out = nc.dram_tensor("y", [NT, D], F32, kind="ExternalOutput").ap()
    norm_g = din("norm_g", [DEPTH, D])
    w_in = din("w_in", [DEPTH, D, DIN])
    w_oa = din("w_oa", [DEPTH, 512, D])
    w_ob = din("w_ob", [DEPTH, 512, D])
    w_oc = din("w_oc", [DEPTH, 512, D])
    w_out = din("w_out", [DEPTH, D, D])
    b_sink = din("b_sink", [DEPTH, 8])
    lam_q1 = din("lam_q1", [DEPTH, 64])
    lam_k1 = din("lam_k1", [DEPTH, 64])
    lam_q2 = din("lam_q2", [DEPTH, 64])
    lam_k2 = din("lam_k2", [DEPTH, 64])
    subln_g = din("c_subln_g", [DEPTH, 128])
    final_g = din("final_norm_g", [1, D])
    c_bB = din("c_bB", [128, 8 * 384], F32)
    c_bA = din("c_bA", [128, 24 * 256], F32)
    c_bC = din("c_bC", [128, 512], BF16)
    c_id = din("c_id", [128, 128], BF16)
    c_qaug = din("c_qaug", [3, 8, NT], BF16)
    c_kaug = din("c_kaug", [4, 8, NT], BF16)

    wb_in = dscr("wb_in", [depth, D, DIN])
    wb_o = [dscr(f"wb_o{i}", [depth, 512, D]) for i in range(3)]
    wb_out = dscr("wb_out", [depth, D, D])
    XR = dscr("XR", [NT, D], F32)
    QaT = dscr("QaT", [1536, NT]); KaT = dscr("KaT", [1536, NT]); Va = dscr("Va", [NT, 1536])
    QbT = dscr("QbT", [512, NT]); KbT = dscr("KbT", [128, NT]); Vb = dscr("Vb", [NT, 128])
    QcT = dscr("QcT", [512, NT]); KcT = dscr("KcT", [512, NT]); Vc = dscr("Vc", [NT, 512])
    SGa = dscr("SGa", [512, NT]); SGb = dscr("SGb", [512, NT]); SGc = dscr("SGc", [512, NT])
    SGm = dscr("SGm", [3072, NT])
    OGa = dscr("OGa", [512, NT]); OGb = dscr("OGb", [512, NT]); OGc = dscr("OGc", [512, NT])
    DEST = {"qa": QaT, "ka": KaT, "va": Va, "ga": SGa, "qb": QbT, "kb": KbT, "vb": Vb, "gb": SGb,
            "qc": QcT, "kc": KcT, "vc": Vc, "gc": SGc, "gm": SGm}

    with ExitStack() as gst:
        P = Prog(nc, gst)
        S_pe, S_act, S_dve, S_pool = P.sem("pe"), P.sem("act"), P.sem("dve"), P.sem("pool")
        P.engsem = {"scalar": S_act, "vector": S_dve, "gpsimd": S_pool}

        def sbuf(st, base, shape, dt):
            return st.enter_context(nc.sbuf_tensor(P.name(base), list(shape), dt))

        def psum(st, base, shape, dt):
            return st.enter_context(nc.psum_tensor(P.name(base), list(shape), dt))

        cast_tok = {}

        def phase_cast():
            for l in range(depth):
                S = P.sem(f"cast{l}")
                for r in range(8):
                    P.dma("gpsimd", wb_in[l, r * 128:(r + 1) * 128, :], w_in[l, r * 128:(r + 1) * 128, :], sig=S)
                for i, w in enumerate((w_oa, w_ob, w_oc)):
                    P.dma("gpsimd", wb_o[i][l, :, :], w[l, :, :], sig=S)
                P.dma("gpsimd", wb_out[l, :, :], w_out[l, :, :], sig=S)
                cast_tok[l] = (S, S.n)
            P.flush("cast", barrier=False)

        def phase_T(l):
            first = l == 0
            last = l == depth
            Xsrc = x_in if l <= 1 else XR
            with ExitStack() as st:
                ps = psum(st, "psT", [128, 6, 512], F32)
                pt = psum(st, "ptT", [128, 2, 1024], BF16)
                ring = Ring(6)
                ident = sbuf(st, "ident", [128, 128], BF16)
                gb = sbuf(st, "gb", [128, D], F32)
                xts = [sbuf(st, "xt", [128, 4, D], F32) for _ in range(2)]
                junk = sbuf(st, "junk", [128, D], BF16)
                ssq = sbuf(st, "ssq", [128, 32], F32)
                epsb = sbuf(st, "epsb", [128, 1], F32)
                S_c = P.sem("T_const"); S_x = [P.sem("T_x0"), P.sem("T_x1")]; S_xst = P.sem("T_xst")
                P.wait("sync", [cast_tok[k] for k in range(min(l, depth - 1) + 1)])
                P.dma("sync", ident[:], c_id[:, :], sig=S_c)
                gsrc = final_g[0:1, :] if last else norm_g[l:l + 1, :]
                P.dma("sync", gb[:], bcast_rows(gsrc, 128), sig=S_c)
                P.emit("vector", MSET(epsb[:], EPS), sig=S_dve)
                t_eps = (S_dve, S_dve.n)
                if not first:
                    ogs = [sbuf(st, "og", [128, 3, 4, 512], BF16) for _ in range(2)]
                    gm = [sbuf(st, "gm", [128, 3, 512], BF16) for _ in range(2)]
                    tmp = [[sbuf(st, "tmp", [128, 512], F32) for _ in range(3)] for _ in range(2)]
                    mixed = sbuf(st, "mixed", [128, 8, 512], BF16)
                    Wo = [sbuf(st, "Wo", [128, 4, D], BF16) for _ in range(3)]
                    wout = sbuf(st, "wout", [128, 8, D], BF16)
                    for i in range(3):
                        P.dma("sync", Wo[i][:], wb_o[i][l - 1].rearrange("(k p) f -> p k f", p=128), sig=S_c)
                    P.dma("sync", wout[:], wb_out[l - 1].rearrange("(k p) f -> p k f", p=128), sig=S_c)
                    S_og = [P.sem("T_og0"), P.sem("T_og1")]; S_gm = [P.sem("T_gm0"), P.sem("T_gm1")]
                if not last:
                    hT = sbuf(st, "hT", [128, 8, 2048], BF16)
                    hb = sbuf(st, "hb", [128, 4, D], BF16)
                    Wt = [sbuf(st, "Wt", [128, 8, 512], BF16) for _ in range(2)]
                    stF = [sbuf(st, "stF", [128, 2048], BF16) for _ in range(3)]
                    stV = [sbuf(st, "stV", [128, 4, 512], BF16) for _ in range(2)]
                    S_w = [P.sem("T_w0"), P.sem("T_w1")]
                    S_stF = [P.sem(f"T_stF{i}") for i in range(3)]
                    S_stV = [P.sem(f"T_stV{i}") for i in range(2)]
                t_c = (S_c, S_c.n)
                state = dict(og_free=[None, None], mixed_free=None, x_free=[[], []], gm_free=[None, None], hb_free=None,
                             w_free=[None, None], stF_free=[None] * 3, stV_free=[None] * 2, wi=0, fi=0, vi=0,
                             pt_free=[None, None], pti=0, stores=[], ld={}, tmp_free=[None, None], ssq_free=[None, None])

                def T1_load(tt):
                    par = tt % 2
                    xsl = slice(tt * 512, tt * 512 + 512)
                    t_x = P.dma("sync", xts[par][:], Xsrc[xsl, :].rearrange("(s p) f -> p s f", p=128),
                                waits=state["x_free"][par], sig=S_x[par])
                    t_og = None
                    if not first:
                        for b_, src in enumerate((OGa, OGb, OGc)):
                            P.dma("sync", ogs[par][:, b_], src.rearrange("(k p) t -> p k t", p=128)[:, :, xsl],
                                  waits=[state["og_free"][par]], sig=S_og[par])
                        t_og = (S_og[par], S_og[par].n)
                    state["ld"][tt] = (t_x, t_og)

                def T1(tt):
                    tok0 = tt * 512
                    par = tt % 2
                    xt = xts[par]
                    xsl = slice(tok0, tok0 + 512)
                    t_x, t_og = state["ld"].pop(tt)
                    if first and tt + 1 < 16:
                        T1_load(tt + 1)
                    x_ready = t_x
                    if not first:
                        og = ogs[par]
                        gsrc_all = SGm.rearrange("(b f p) t -> p b f t", b=3, p=128)
                        t_mixed = None
                        for fc in range(8):
                            sl = fc % 2
                            t_gm = P.dma("sync", gm[sl][:], gsrc_all[:, :, fc, xsl],
                                         waits=[state["gm_free"][sl]], sig=S_gm[sl])
                            if fc == 3 and tt + 1 < 16:
                                T1_load(tt + 1)
                            bks = []
                            for b in range(3):
                                bi, bfree = ring.next()
                                for kc in range(4):
                                    tk = P.emit("tensor", MM(ps[:, bi, :], Wo[b][:, kc, fc * 128:(fc + 1) * 128],
                                                             og[:, b, kc, :], kc == 0, kc == 3),
                                                waits=[bfree, t_og, t_c], sig=S_pe if kc == 3 else None)
                                bks.append((bi, tk))
                            if fc == 7:
                                state["og_free"][par] = bks[-1][1]
                            for b in range(3):
                                bi, tk = bks[b]
                                td = P.emit("vector", TT(tmp[sl][b][:], ps[:, bi, :], gm[sl][:, b, :], ALU.mult),
                                            waits=[tk, t_gm, state["tmp_free"][sl]], sig=S_dve, chain=False)
                                ring.free[bi] = td
                            state["gm_free"][sl] = td
                            w = [td]
                            if fc == 0:
                                w.append(state["mixed_free"])
                            P.emit("gpsimd", TT(tmp[sl][0][:], tmp[sl][0][:], tmp[sl][1][:], ALU.add), waits=w, sig=S_pool, chain=False)
                            t_mixed = P.emit("gpsimd", TT(mixed[:, fc, :], tmp[sl][0][:], tmp[sl][2][:], ALU.add), sig=S_pool)
                            state["tmp_free"][sl] = t_mixed
                        for sub in range(4):
                            for fh in range(2):
                                bi, bfree = ring.next()
                                for kc in range(8):
                                    tk = P.emit("tensor", MM(ps[:, bi, :], mixed[:, kc, sub * 128:(sub + 1) * 128],
                                                             wout[:, kc, fh * 512:(fh + 1) * 512], kc == 0, kc == 7),
                                                waits=[bfree, t_mixed], sig=S_pe if kc == 7 else None)
                                td = P.emit("vector", TT(xt[:, sub, fh * 512:(fh + 1) * 512], ps[:, bi, :],
                                                         xt[:, sub, fh * 512:(fh + 1) * 512], ALU.add),
                                            waits=[tk, t_x], sig=S_dve, chain=False)
                                ring.free[bi] = td
                        state["mixed_free"] = tk
                        x_ready = td
                    frees = []
                    if not first and not last:
                        t_st = P.dma("gpsimd", XR[xsl, :].rearrange("(s p) f -> p s f", p=128), xt[:],
                                     waits=[x_ready], sig=S_xst)
                        frees.append(t_st)
                    q0 = 16 * par
                    for sub in range(4):
                        t_sq = P.emit("scalar", ACT(junk[:], xt[:, sub, :], AF.Square, accum_out=ssq[:, q0 + sub:q0 + sub + 1]),
                                      waits=[x_ready, state["ssq_free"][par]], sig=S_act)
                    P.emit("scalar", ACT(ssq[:, q0 + 4:q0 + 8], ssq[:, q0:q0 + 4], AF.Ln, scale=1.0 / D, bias=epsb[:, 0:1]), waits=[t_eps])
                    t_r = P.emit("scalar", ACT(ssq[:, q0 + 8:q0 + 12], ssq[:, q0 + 4:q0 + 8], AF.Exp, scale=-0.5), sig=S_act)
                    if last:
                        for sub in range(4):
                            td = P.emit("vector", STT(xt[:, sub, :], xt[:, sub, :], ssq[:, q0 + 8 + sub:q0 + 9 + sub], gb[:],
                                                      ALU.mult, ALU.mult), waits=[t_r, t_c], sig=S_dve, chain=False)
                        t_st = P.dma("gpsimd", y_out[xsl, :].rearrange("(s p) f -> p s f", p=128), xt[:],
                                     waits=[td], sig=S_xst)
                        state["x_free"][par] = [t_st]
                        state["ssq_free"][par] = td
                        return
                    for sub in range(4):
                        w = [t_r, t_c]
                        if sub == 0:
                            w.append(state["hb_free"])
                        td = P.emit("vector", STT(hb[:, sub, :], xt[:, sub, :], ssq[:, q0 + 8 + sub:q0 + 9 + sub], gb[:],
                                                  ALU.mult, ALU.mult), waits=w, sig=S_dve, chain=False)
                        pi = state["pti"]; state["pti"] = 1 - pi
                        for kc in range(8):
                            tk = P.emit("tensor", TR(pt[:, pi, kc * 128:(kc + 1) * 128], hb[:, sub, kc * 128:(kc + 1) * 128], ident[:]),
                                        waits=[td, state["pt_free"][pi], t_c], sig=S_pe if kc == 7 else None)
                        off = (tt % 4) * 512 + sub * 128
                        ta = P.emit("scalar", ACT(hT[:, :, off:off + 128], pt[:, pi, :].rearrange("p (k t) -> p k t", k=8), AF.Copy),
                                    waits=[tk], sig=S_act, chain=False)
                        state["pt_free"][pi] = ta
                    state["hb_free"] = tk
                    state["hT_ready"] = ta
                    state["ssq_free"][par] = td
                    frees += [t_sq, td]
                    state["x_free"][par] = frees

                wtiles = [(name, c0, kind, w0, min(512, ncols - w0)) for (name, c0, ncols, kind) in SEGS
                          for w0 in range(0, ncols, 512)]
                NW = len(wtiles)
                w_tok = {}

                def load_w(k):
                    name, c0, kind, w0, wc = wtiles[k % NW]
                    wi = k % 2
                    w_tok[k] = P.dma("sync", Wt[wi][:, :, 0:wc],
                                     wb_in[l].rearrange("(k p) c -> p k c", p=128)[:, :, c0 + w0:c0 + w0 + wc],
                                     waits=[state["w_free"][wi]], sig=S_w[wi])

                def T2(s):
                    tsl = slice(s * 2048, (s + 1) * 2048)
                    t_h = state["hT_ready"]
                    for ti, (name, c0, kind, w0, wc) in enumerate(wtiles):
                        dest = DEST[name]
                        k = s * NW + ti
                        wi = k % 2
                        if k + 1 < 4 * NW:
                            load_w(k + 1)
                        t_w = w_tok[k]
                        if kind == "v":
                            for s16 in range(16):
                                vi = state["vi"]
                                bi, bfree = ring.next()
                                for kc in range(8):
                                    tk = P.emit("tensor", MM(ps[:, bi, 0:wc], hT[:, kc, s16 * 128:(s16 + 1) * 128],
                                                             Wt[wi][:, kc, 0:wc], kc == 0, kc == 7),
                                                waits=[bfree, t_w, t_h], sig=S_pe if kc == 7 else None)
                                w = [tk]
                                if s16 % 4 == 0:
                                    w.append(state["stV_free"][vi])
                                ta = P.emit("scalar", ACT(stV[vi][:, s16 % 4, 0:wc], ps[:, bi, 0:wc], AF.Copy),
                                            waits=w, sig=S_act, chain=False)
                                ring.free[bi] = ta
                                if s16 % 4 == 3:
                                    r0 = s * 2048 + (s16 // 4) * 512
                                    t_st = P.dma("gpsimd", dest[r0:r0 + 512, w0:w0 + wc].rearrange("(s p) c -> p s c", p=128),
                                                 stV[vi][:, :, 0:wc], waits=[ta], sig=S_stV[vi])
                                    state["stV_free"][vi] = t_st
                                    state["vi"] = 1 - vi
                            state["w_free"][wi] = tk
                            continue
                        for sc in range(wc // 128):
                            fi = state["fi"]; state["fi"] = (fi + 1) % 3
                            for i in range(4):
                                bi, bfree = ring.next()
                                for kc in range(8):
                                    tk = P.emit("tensor", MM(ps[:, bi, :], Wt[wi][:, kc, sc * 128:(sc + 1) * 128],
                                                             hT[:, kc, i * 512:(i + 1) * 512], kc == 0, kc == 7),
                                                waits=[bfree, t_w, t_h], sig=S_pe if kc == 7 else None)
                                w = [tk]
                                if i == 0:
                                    w.append(state["stF_free"][fi])
                                o = stF[fi][:, i * 512:(i + 1) * 512]
                                if kind == "q":
                                    te = P.emit("vector", TS(o, ps[:, bi, :], 0.125, ALU.mult), waits=w, sig=S_dve, chain=False)
                                elif kind == "k":
                                    te = P.emit("vector", CP(o, ps[:, bi, :]), waits=w, sig=S_dve, chain=False)
                                elif kind == "silu":
                                    te = P.emit("scalar", ACT(o, ps[:, bi, :], AF.Silu), waits=w, sig=S_act, chain=False)
                                else:
                                    te = P.emit("scalar", ACT(o, ps[:, bi, :], AF.Sigmoid), waits=w, sig=S_act, chain=False)
                                ring.free[bi] = te
                            f0 = w0 + sc * 128
                            t_st = P.dma("gpsimd", dest[f0:f0 + 128, tsl], stF[fi][:], waits=[te], sig=S_stF[fi])
                            state["stF_free"][fi] = t_st
                        state["w_free"][wi] = tk

                if not last:
                    load_w(0)
                T1_load(0)
                for s in range(4):
                    for i in range(4):
                        T1(4 * s + i)
                    if not last:
                        T2(s)
                st_sems = [S_xst]
                if not last:
                    st_sems += S_stF + S_stV
                for e_ in ("sync", "gpsimd", "scalar", "vector", "tensor"):
                    P.wait_sems(e_, st_sems)
                P.flush(f"T{l}")

        def run_banded(res, kbs):
            ps_s, ps_a, PT, SB = res["ps_s"], res["ps_a"], res["PT"], res["SB"]
            NPT = res["NPT"]
            G = 2
            ngroups = (len(kbs) + G - 1) // G
            s_free = res["s_free"]
            sb_free = res["sb_free"]
            pt_free = res["pt_free"]
            a_free = res["a_free"]
            exp_tok = {}
            qk_tok = {}
            bias_tok = {}

            def do_qk(gi):
                sl = gi % 2
                tk = None
                members = kbs[gi * G:(gi + 1) * G]
                for m, kb in enumerate(members):
                    for qi_, (lo, hi, lhsT, rhs) in enumerate(kb["qk"]):
                        islast = (qi_ == len(kb["qk"]) - 1) and (m == len(members) - 1)
                        tk = P.emit("tensor", MM(ps_s[:, sl * G + m, lo:hi], lhsT, rhs, True, True),
                                    waits=[s_free[sl]] + list(kb["waits"]), sig=S_pe if islast else None)
                qk_tok[gi] = tk

            def do_bias(gi):
                sl = gi % 2
                members = kbs[gi * G:(gi + 1) * G]
                cols = members[0]["cols"]
                n = len(members)
                bap = members[0]["bias"]
                bb = bass.AP(bap.tensor, bap.offset, [list(bap.ap[0]), [0, n], list(bap.ap[-1])])
                td = P.emit("vector", TT(SB[:, sl, 0:n, 0:cols], ps_s[:, sl * G:sl * G + n, 0:cols], bb, ALU.add),
                            waits=[qk_tok[gi], sb_free[sl]] + list(members[0]["waits"]), sig=S_dve, chain=False)
                bias_tok[gi] = td
                s_free[sl] = td

            def do_exp(gi):
                sl = gi % 2
                members = kbs[gi * G:(gi + 1) * G]
                cols = members[0]["cols"]
                n = len(members)
                slots = [(gi * G + m) % NPT for m in range(n)]
                w = [bias_tok[gi]] + [pt_free[s_] for s_ in slots]
                ta = P.emit("scalar", ACT(PT[:, slots[0]:slots[0] + n, 0:cols], SB[:, sl, 0:n, 0:cols], AF.Exp),
                            waits=w, sig=S_act, chain=False)
                exp_tok[gi] = ta
                sb_free[sl] = ta

            def do_av(gi):
                members = kbs[gi * G:(gi + 1) * G]
                for m, kb in enumerate(members):
                    for job in kb["av"]:
                        bank, slot = job["bank"], job["slot"]
                        qa, qb = job["qa"], job["qb"]
                        np_ = len(job["parts"])
                        for pi_, (kidx, c, vap) in enumerate(job["parts"]):
                            w = [exp_tok[kidx // G]]
                            if pi_ == 0 and job["bank_first"]:
                                w += list(a_free[bank] or ())
                            w += list(job.get("waits", ()))
                            tk = P.emit("tensor", MM(ps_a[:, bank, slot * 128 + qa:slot * 128 + qb], vap,
                                                     PT[:, kidx % NPT, c * 128 + qa:c * 128 + qb], pi_ == 0, pi_ == np_ - 1),
                                        waits=w, sig=S_pe if (pi_ == np_ - 1) else None)
                            pt_free[kidx % NPT] = tk
                        if job["bank_done"]:
                            a_free[bank] = job["evac"](tk)

            for step in range(ngroups + 2):
                if step < ngroups:
                    do_qk(step)
                    do_bias(step)
                    do_exp(step)
                if step >= 2:
                    do_av(step - 2)

        def phase_B(l):
            with ExitStack() as st:
                ps_s = psum(st, "psBs", [128, 4, 512], F32)
                ps_a = psum(st, "psBa", [128, 2, 512], F32)
                NPT = 8
                PT = sbuf(st, "PT", [128, NPT, 384], BF16)
                ident = sbuf(st, "ident", [128, 128], BF16)
                bias = sbuf(st, "biasB", [128, 8, 384], F32)
                SB = sbuf(st, "SBb", [128, 2, 2, 384], F32)
                QT = [sbuf(st, "QTb", [68, NT], BF16) for _ in range(2)]
                KT = [sbuf(st, "KTb", [68, NT], BF16) for _ in range(2)]
                Vg = sbuf(st, "Vb", [128, 64, 2, 128], BF16)
                sg = [sbuf(st, "sgb", [64, NT], BF16) for _ in range(2)]
                esink = sbuf(st, "esink", [128, 8], F32)
                rec = [sbuf(st, "recB", [64, 512], F32) for _ in range(2)]
                of = [sbuf(st, "ofB", [64, 512], F32) for _ in range(2)]
                rec_free = [None, None]
                stg = [sbuf(st, "stgB", [64, 512], BF16) for _ in range(2)]
                S_c = P.sem("B_c"); S_q = [P.sem("B_q0"), P.sem("B_q1")]; S_st = [P.sem("B_st0"), P.sem("B_st1")]
                P.dma("sync", ident[:], c_id[:, :], sig=S_c)
                P.dma("sync", bias[:], c_bB.rearrange("p (h c) -> p h c", h=8), sig=S_c)
                P.dma("sync", esink[:], bcast_rows(b_sink[l:l + 1, :], 128), sig=S_c)
                for kv in range(2):
                    P.dma("sync", KT[kv][0:64, :], KbT[kv * 64:(kv + 1) * 64, :], sig=S_c)
                    P.dma("sync", KT[kv][64:68, :], c_kaug[0, 0:4, :], sig=S_c)
                    P.dma("sync", QT[kv][64:68, :], c_qaug[2, 0:4, :], sig=S_c)
                P.emit("gpsimd", MSET(Vg[:], 1.0), sig=S_pool)
                t_ms = (S_pool, S_pool.n)
                vsrc = Vb.rearrange("(j p) (k d) -> p j k d", p=128, k=2)
                for q4 in range(4):
                    for kv in range(2):
                        P.dma("sync", Vg[:, q4 * 16:(q4 + 1) * 16, kv, 0:64], vsrc[:, q4 * 16:(q4 + 1) * 16, kv, :],
                              waits=[t_ms], sig=S_c)
                t_c = (S_c, S_c.n)
                t_es = P.emit("scalar", ACT(esink[:], esink[:], AF.Exp), waits=[t_c], sig=S_act)
                res = dict(ps_s=ps_s, ps_a=ps_a, PT=PT, SB=SB, NPT=NPT, s_free=[None, None], sb_free=[None, None],
                           pt_free=[None] * NPT, a_free=[None, None])
                q_free = [None, None]
                st_free = [None, None]
                stores = []
                sti = [0]
                tq = {}

                def load_head(h):
                    qi = h % 2
                    P.dma("sync", QT[qi][0:64, :], QbT[h * 64:(h + 1) * 64, :], waits=[q_free[qi]], sig=S_q[qi])
                    P.dma("sync", sg[qi][:], SGb[h * 64:(h + 1) * 64, :], waits=[q_free[qi]], sig=S_q[qi])
                    tq[h] = (S_q[qi], S_q[qi].n)

                load_head(0)
                for h in range(8):
                    kv = h // 4
                    qi = h % 2
                    if h + 1 < 8:
                        load_head(h + 1)
                    t_q = tq[h]
                    kbs = []
                    last_tok = [None]

                    def mk_evac(bank, q0, h=h, qi=qi):
                        def evac(tk):
                            tsl = slice(q0 * 128, q0 * 128 + 512)
                            ri = sti[0]; sti[0] = 1 - ri
                            ta0 = P.emit("scalar", ACT(rec[ri][:], ps_a[64:128, bank, :], AF.Ln, bias=esink[64:128, h:h + 1]),
                                         waits=[tk, t_es, rec_free[ri]], sig=S_act, chain=False)
                            ta = P.emit("scalar", ACT(rec[ri][:], rec[ri][:], AF.Exp, scale=-1.0), sig=S_act)
                            td = P.emit("vector", TT(of[ri][:], ps_a[0:64, bank, :], sg[qi][:, tsl], ALU.mult),
                                        waits=[tk, t_q, rec_free[ri]], sig=S_dve, chain=False)
                            te = P.emit("gpsimd", TT(stg[ri][:], of[ri][:], rec[ri][:], ALU.mult),
                                        waits=[ta, td, st_free[ri]], sig=S_pool, chain=False)
                            rec_free[ri] = te
                            t_st = P.dma("gpsimd", OGb[h * 64:(h + 1) * 64, tsl], stg[ri][:], waits=[te], sig=S_st[ri])
                            st_free[ri] = t_st
                            stores.append(t_st)
                            last_tok[0] = te
                            return [ta0, td]
                        return evac

                    for j in range(64):
                        qlo = max(j - 1, 0); qhi = min(j + 1, 63)
                        lo = (qlo - (j - 1)) * 128; hi = (qhi - (j - 1) + 1) * 128
                        kb = dict(qk=[(lo, hi, KT[kv][0:68, j * 128:(j + 1) * 128], QT[qi][0:68, qlo * 128:(qhi + 1) * 128])],
                                  bias=bias[:, h, :], cols=384, waits=[t_q, t_c], av=[])
                        done = []
                        if j >= 1:
                            done.append(j - 1)
                        if j == 63:
                            done.append(63)
                        for qb_ in done:
                            parts = [(jj, qb_ - jj + 1, Vg[:, jj, kv, :]) for jj in (qb_ - 1, qb_, qb_ + 1) if 0 <= jj < 64]
                            bank = (qb_ // 4) % 2
                            job = dict(parts=parts, qa=0, qb=128, slot=qb_ % 4, bank=bank, bank_first=(qb_ % 4 == 0),
                                       bank_done=(qb_ % 4 == 3), waits=[t_c])
                            if job["bank_done"]:
                                job["evac"] = mk_evac(bank, qb_ - 3)
                            kb["av"].append(job)
                        kbs.append(kb)
                    run_banded(res, kbs)
                    q_free[qi] = last_tok[0]
                for e_ in ("sync", "gpsimd", "scalar", "vector", "tensor"):
                    P.wait_sems(e_, S_st)
                P.flush(f"B{l}")

        def phase_A(l):
            with ExitStack() as st:
                ps_s = psum(st, "psAs", [128, 4, 512], F32)
                ps_a = psum(st, "psAa", [128, 2, 512], F32)
                NPT = 8
                PT = sbuf(st, "PTa", [128, NPT, 256], BF16)
                ident = sbuf(st, "ident", [128, 128], BF16)
                bA = sbuf(st, "bA", [128, 24, 256], F32)
                SB = sbuf(st, "SBa", [128, 2, 2, 256], F32)
                QT = [sbuf(st, "QTa", [68, NT], BF16) for _ in range(2)]
                KT = [sbuf(st, "KTa", [68, NT], BF16) for _ in range(2)]
                Vg = [sbuf(st, "Va", [128, 64, 128], BF16) for _ in range(2)]
                accs = [sbuf(st, "accA", [128, NT], F32) for _ in range(2)]
                sgt = [sbuf(st, "sga", [64, 512], BF16) for _ in range(2)]
                rec = [sbuf(st, "recA", [64, 512], F32) for _ in range(2)]
                of = [sbuf(st, "ofA", [64, 512], F32) for _ in range(2)]
                rec_free = [None, None]
                stg = [sbuf(st, "stgA", [64, 512], BF16) for _ in range(2)]
                S_c = P.sem("A_c"); S_q = [P.sem("A_q0"), P.sem("A_q1")]
                S_st = [P.sem("A_st0"), P.sem("A_st1")]; S_sg = [P.sem("A_sg0"), P.sem("A_sg1")]
                P.dma("sync", ident[:], c_id[:, :], sig=S_c)
                P.dma("sync", bA[:], c_bA.rearrange("p (h c) -> p h c", h=24), sig=S_c)
                for i in range(2):
                    P.dma("sync", KT[i][64:68, :], c_kaug[0, 0:4, :], sig=S_c)
                    P.dma("sync", QT[i][64:68, :], c_qaug[2, 0:4, :], sig=S_c)
                    P.emit("gpsimd", MSET(Vg[i][:], 1.0), sig=S_pool)
                t_ms = (S_pool, S_pool.n)
                t_c = (S_c, S_c.n)
                res = dict(ps_s=ps_s, ps_a=ps_a, PT=PT, SB=SB, NPT=NPT, s_free=[None, None], sb_free=[None, None],
                           pt_free=[None] * NPT, a_free=[None, None])
                q_free = [None, None]
                st_free = [None, None]
                sg_free = [None, None]
                stores = []
                acc_free = [None, None]
                ui = 0
                tq = {}

                def load_unit(u):
                    h, g = divmod(u, 3)
                    dil = A_PAT[g][1]
                    qi = u % 2
                    f0 = g * 512 + h * 64
                    P.dma("sync", QT[qi][0:64, :], QaT[f0:f0 + 64, :], waits=[q_free[qi]], sig=S_q[qi])
                    P.dma("sync", KT[qi][0:64, :], KaT[f0:f0 + 64, :], waits=[q_free[qi]], sig=S_q[qi])
                    U = NT // dil
                    nb = U // 128
                    vsrc = Va.rearrange("(u d) c -> d u c", d=dil)
                    for r in range(dil):
                        vr = vsrc[r, :, f0:f0 + 64].rearrange("(j p) c -> p j c", p=128)
                        for j0 in range(0, nb, 16):
                            j1 = min(nb, j0 + 16)
                            P.dma("sync", Vg[qi][:, r * nb + j0:r * nb + j1, 0:64], vr[:, j0:j1, :],
                                  waits=[q_free[qi], t_ms], sig=S_q[qi])
                    tq[u] = (S_q[qi], S_q[qi].n)

                load_unit(0)
                for h in range(8):
                    last_acc = None
                    acc = accs[h % 2]
                    for g, (_, dil) in enumerate(A_PAT):
                        qi = ui % 2
                        if ui + 1 < 24:
                            load_unit(ui + 1)
                        t_q = tq[ui]
                        ui += 1
                        f0 = g * 512 + h * 64
                        U = NT // dil
                        nb = U // 128
                        kbs = []
                        last_tok = [None]

                        def mk_evac(bank, lo, hi, t0, n, g=g, dil=dil, acc=acc, h=h):
                            def evac(tk):
                                nonlocal last_acc
                                dst = acc[:, ssl(t0, n, dil)]
                                w = [tk]
                                if g == 0:
                                    w += list(acc_free[h % 2] or ())
                                    td = P.emit("vector", CP(dst, ps_a[:, bank, lo:hi]), waits=w, sig=S_dve, chain=False)
                                else:
                                    td = P.emit("vector", TT(dst, ps_a[:, bank, lo:hi], dst, ALU.add), waits=w, sig=S_dve)
                                last_tok[0] = td
                                last_acc = td
                                return [td]
                            return evac

                        kidx = 0
                        for r in range(dil):
                            for j in range(nb):
                                tb = r + dil * 128 * j
                                ulo = max(128 * j - 64, 0); uhi = min(128 * j + 192, U)
                                lo = ulo - (128 * j - 64); hi = uhi - (128 * j - 64)
                                kb = dict(qk=[(lo, hi, KT[qi][0:68, ssl(tb, 128, dil)],
                                               QT[qi][0:68, ssl(r + dil * ulo, uhi - ulo, dil)])],
                                          bias=bA[:, g * 8 + h, :], cols=256, waits=[t_q, t_c], av=[])
                                done = [j - 1]
                                if j == nb - 1:
                                    done.append(nb - 1)
                                for qb_ in done:
                                    parts = []
                                    for jj in (qb_, qb_ + 1):
                                        if 0 <= jj < nb:
                                            parts.append((kidx - (j - jj), qb_ - jj + 1, Vg[qi][:, r * nb + jj, :]))
                                    qa = 64 if qb_ == -1 else 0
                                    qbb = 64 if qb_ == nb - 1 else 128
                                    seq = qb_ + 1
                                    bank = (seq // 4) % 2
                                    slot = seq % 4
                                    bank_done = (slot == 3) or (qb_ == nb - 1)
                                    job = dict(parts=parts, qa=qa, qb=qbb, slot=slot, bank=bank, bank_first=(slot == 0),
                                               bank_done=bank_done, waits=[t_c])
                                    if bank_done:
                                        first_qb = qb_ - slot
                                        c_lo = 64 if first_qb == -1 else 0
                                        c_hi = slot * 128 + qbb
                                        u0 = 128 * first_qb + 64 + c_lo
                                        job["evac"] = mk_evac(bank, c_lo, c_hi, r + dil * u0, c_hi - c_lo)
                                    kb["av"].append(job)
                                kbs.append(kb)
                                kidx += 1
                        run_banded(res, kbs)
                        q_free[qi] = last_tok[0]
                    for c in range(16):
                        tsl = slice(c * 512, (c + 1) * 512)
                        si = c % 2
                        t_sg = P.dma("sync", sgt[si][:], SGa[h * 64:(h + 1) * 64, tsl], waits=[sg_free[si]], sig=S_sg[si])
                        P.emit("scalar", ACT(rec[si][:], acc[64:128, tsl], AF.Ln), waits=[last_acc, rec_free[si]], sig=S_act, chain=False)
                        ta = P.emit("scalar", ACT(rec[si][:], rec[si][:], AF.Exp, scale=-1.0), sig=S_act)
                        tp = P.emit("gpsimd", TT(of[si][:], acc[0:64, tsl], sgt[si][:], ALU.mult),
                                    waits=[last_acc, t_sg, rec_free[si]], sig=S_pool, chain=False)
                        te = P.emit("gpsimd", TT(stg[si][:], of[si][:], rec[si][:], ALU.mult), waits=[ta, st_free[si]], sig=S_pool)
                        sg_free[si] = te
                        rec_free[si] = te
                        t_st = P.dma("gpsimd", OGa[h * 64:(h + 1) * 64, tsl], stg[si][:], waits=[te], sig=S_st[si])
                        st_free[si] = t_st
                        stores.append(t_st)
                    acc_free[h % 2] = [te, ta]
                for e_ in ("sync", "gpsimd", "scalar", "vector", "tensor"):
                    P.wait_sems(e_, S_st)
                P.flush(f"A{l}")

        def phase_C2(l):
            lam_init = 0.8 - 0.6 * math.exp(-0.3 * l)
            with ExitStack() as st:
                ps_s = psum(st, "psCs", [128, 4, 512], F32)
                ps_a = psum(st, "psCa", [128, 4, 512], F32)
                NPT = 6
                PT = [sbuf(st, "PTc", [128, 2, 512], BF16) for _ in range(NPT)]
                LD = [[sbuf(st, "LD", [128, 512], F32) for _ in range(2)] for _ in range(2)]
                LP = [[sbuf(st, "LP", [128, 512], F32) for _ in range(2)] for _ in range(2)]
                Lt = sbuf(st, "Lt", [128, 512], F32)
                Lhi = sbuf(st, "Lhi", [128, 512], BF16)
                Llo = sbuf(st, "Llo", [128, 512], BF16)
                l_free = [None, None]
                ident = sbuf(st, "ident", [128, 128], BF16)
                ones = sbuf(st, "ones", [128, 128], BF16)
                bC = sbuf(st, "bC", [128, 4, 128], BF16)
                KT = [[sbuf(st, "KTc", [72, NT], BF16) for _ in range(2)] for _ in range(2)]
                Vh = [sbuf(st, "Vc", [128, 64, 128], BF16) for _ in range(2)]
                QTt = [[[sbuf(st, "QTc", [72, 512], BF16) for _ in range(3)] for _ in range(2)] for _ in range(2)]
                sgt = [sbuf(st, "sgc", [128, 512], BF16) for _ in range(2)]
                r1 = sbuf(st, "r1", [128, 512], F32)
                o1 = sbuf(st, "o1", [128, 512], F32)
                o2 = sbuf(st, "o2", [128, 512], F32)
                oo = sbuf(st, "oo", [128, 512], F32)
                sq = sbuf(st, "sq", [128, 512], BF16)
                rstd = sbuf(st, "rstd", [128, 512], F32)
                stg = [sbuf(st, "stgC", [128, 512], BF16) for _ in range(2)]
                lam = sbuf(st, "lam", [128, 4, 64], F32)
                lsc = sbuf(st, "lsc", [128, 8], F32)
                coef = sbuf(st, "coef", [128, 1], F32)
                epsb = sbuf(st, "epsbC", [128, 1], F32)
                S_c = P.sem("C_c"); S_k = [P.sem("C_k0"), P.sem("C_k1")]; S_q = [P.sem("C_q0"), P.sem("C_q1")]
                S_st = [P.sem("C_st0"), P.sem("C_st1")]
                P.dma("sync", ident[:], c_id[:, :], sig=S_c)
                P.dma("sync", bC[:], c_bC.rearrange("p (h c) -> p h c", h=4), sig=S_c)
                for i, v in enumerate((lam_q1, lam_k1, lam_q2, lam_k2)):
                    P.dma("sync", lam[:, i, :], bcast_rows(v[l:l + 1, :], 128), sig=S_c)
                P.dma("sync", coef[:], subln_g[l:l + 1, :].rearrange("a d -> d a"), sig=S_c)
                t_c = (S_c, S_c.n)
                P.emit("vector", MSET(ones[:], 1.0))
                P.emit("vector", MSET(epsb[:], EPS))
                P.emit("vector", TT(lam[:, 0, :], lam[:, 0, :], lam[:, 1, :], ALU.mult), waits=[t_c])
                P.emit("vector", TT(lam[:, 2, :], lam[:, 2, :], lam[:, 3, :], ALU.mult))
                P.emit("vector", lambda e: e.reduce_sum(out=lsc[:, 0:1], in_=lam[:, 0, :], axis=AX.X))
                td = P.emit("vector", lambda e: e.reduce_sum(out=lsc[:, 1:2], in_=lam[:, 2, :], axis=AX.X), sig=S_dve)
                ta = P.emit("scalar", ACT(lsc[:, 2:4], lsc[:, 0:2], AF.Exp), waits=[td], sig=S_act)
                P.emit("vector", TT(lsc[:, 4:5], lsc[:, 3:4], lsc[:, 2:3], ALU.subtract), waits=[ta])
                P.emit("vector", TS(lsc[:, 5:6], lsc[:, 4:5], -lam_init, ALU.add))
                t_l = P.emit("vector", TS(coef[:], coef[:], 1.0 - lam_init, ALU.mult), sig=S_dve)
                neglam = lsc[:, 5:6]

                s_free = [None, None]
                pt_free = [None] * NPT
                a_free = [None] * 4
                k_free = [None, None]
                q_free = [None, None]
                sg_free = [None, None]
                st_free = [None, None]
                stores = []

                units = [(h, c) for h in range(4) for c in range(16)]
                loads = {}

                def load_head(h):
                    kb_ = h % 2
                    for m in range(2):
                        f0 = h * 128 + m * 64
                        P.dma("sync", KT[kb_][m][0:64, :], KcT[f0:f0 + 64, :], waits=[k_free[kb_]], sig=S_k[kb_])
                        P.dma("sync", KT[kb_][m][64:72, :], c_kaug[h, :, :], waits=[k_free[kb_]], sig=S_k[kb_])
                    vsrc = Vc[:, h * 128:(h + 1) * 128].rearrange("(j p) d -> p j d", p=128)
                    for q4 in range(4):
                        P.dma("sync", Vh[kb_][:, q4 * 16:(q4 + 1) * 16, :], vsrc[:, q4 * 16:(q4 + 1) * 16, :],
                              waits=[k_free[kb_]], sig=S_k[kb_])
                    loads[("k", h)] = (S_k[kb_], S_k[kb_].n)

                def load_unit(ui):
                    h, c = units[ui]
                    qb_ = ui % 2
                    csl = slice(c * 512, (c + 1) * 512)
                    for m in range(2):
                        f0 = h * 128 + m * 64
                        for ver in range(3):
                            P.dma("sync", QTt[qb_][m][ver][0:64, :], QcT[f0:f0 + 64, csl], waits=[q_free[qb_], sg_free[qb_]], sig=S_q[qb_])
                            P.dma("sync", QTt[qb_][m][ver][64:72, :], c_qaug[ver, :, csl], waits=[q_free[qb_]], sig=S_q[qb_])
                    P.dma("sync", sgt[qb_][:], SGc[h * 128:(h + 1) * 128, csl], waits=[q_free[qb_], sg_free[qb_]], sig=S_q[qb_])
                    loads[("q", ui)] = (S_q[qb_], S_q[qb_].n)

                def unit_range(h, c):
                    m_ = 2.0 ** (-2.0 * (h + 1))
                    js = []
                    for jb in range(64):
                        if jb < 4 * c:
                            dmin = 512 * c - (128 * jb + 127)
                        elif jb > 4 * c + 3:
                            dmin = 128 * jb - (512 * c + 511)
                        else:
                            dmin = 0
                        if m_ * dmin < C_SKIP:
                            js.append(jb)
                    return js[0] // 2, js[-1] // 2

                urange = [unit_range(h, c) for (h, c) in units]
                groups = [(ui, m, g) for ui in range(len(units)) for m in range(2)
                          for g in range(urange[ui][0], urange[ui][1] + 1)]
                NG = len(groups)
                qk_tok = {}
                exp_tok = {}
                pending_ss = []
                unit_state = {}

                def do_qk(G):
                    ui, m, g = groups[G]
                    h, c = units[ui]
                    kb_, qb_ = h % 2, ui % 2
                    sl = G % 2
                    w0 = [s_free[sl], loads[("k", h)], loads[("q", ui)], t_c, t_l]
                    tk = None
                    for mm_ in range(2):
                        jb = 2 * g + mm_
                        out = ps_s[:, sl * 2 + mm_, :]
                        lhsT = KT[kb_][m][0:72, jb * 128:(jb + 1) * 128]
                        Q = QTt[qb_][m]
                        sig = S_pe if mm_ == 1 else None
                        if jb < 4 * c:
                            tk = P.emit("tensor", MM(out, lhsT, Q[0][0:72, :]), waits=w0, sig=sig)
                        elif jb > 4 * c + 3:
                            tk = P.emit("tensor", MM(out, lhsT, Q[1][0:72, :]), waits=w0, sig=sig)
                        else:
                            a = jb - 4 * c
                            if a > 0:
                                P.emit("tensor", MM(ps_s[:, sl * 2 + mm_, 0:a * 128], lhsT, Q[1][0:72, 0:a * 128]), waits=w0)
                            P.emit("tensor", MM(ps_s[:, sl * 2 + mm_, a * 128:(a + 1) * 128], lhsT, Q[2][0:72, a * 128:(a + 1) * 128], True, False), waits=w0)
                            tk = P.emit("tensor", MM(ps_s[:, sl * 2 + mm_, a * 128:(a + 1) * 128], ident[:], bC[:, h, :], False, True),
                                        sig=sig if a == 3 else None)
                            if a < 3:
                                tk = P.emit("tensor", MM(ps_s[:, sl * 2 + mm_, (a + 1) * 128:512], lhsT, Q[0][0:72, (a + 1) * 128:512]),
                                            waits=w0, sig=sig)
                    qk_tok[G] = tk

                def do_exp(G):
                    sl = G % 2
                    pi = G % NPT
                    ta = P.emit("scalar", ACT(PT[pi][:], ps_s[:, sl * 2:sl * 2 + 2, :], AF.Exp),
                                waits=[qk_tok[G]] + list(pt_free[pi] or ()), sig=S_act, chain=False)
                    exp_tok[G] = ta
                    s_free[sl] = ta

                lstate = {}
                pending_L = []

                def do_av(G):
                    ui, m, g = groups[G]
                    h, c = units[ui]
                    kb_, qb_ = h % 2, ui % 2
                    pi = G % NPT
                    bo, bl_ = 2 * m, 2 * m + 1
                    tk = None
                    glo, ghi = urange[ui]
                    jfirst, jlast = 2 * glo, 2 * ghi + 1
                    for mm_ in range(2):
                        jb = 2 * g + mm_
                        w = [exp_tok[G]]
                        if jb == jfirst:
                            w += [a_free[bo], a_free[bl_]]
                        tk = P.emit("tensor", MM(ps_a[:, bo, :], Vh[kb_][:, jb, :], PT[pi][:, mm_, :], jb == jfirst, jb == jlast),
                                    waits=w, sig=S_pe if mm_ == 1 else None)
                        if mm_ == 0:
                            P.emit("tensor", MM(ps_a[:, bl_, :], ones[:], PT[pi][:, 0, :], jb == jfirst, False))
                    k = (2 * ui + m) % 2
                    stt = lstate.setdefault((ui, m), dict(tok={"vector": [None, None]}, cnt={"vector": 0}))
                    eng = "vector"
                    accs_ = LD[k]
                    n_ = stt["cnt"][eng]
                    par = n_ % 2
                    src = PT[pi][:, 1, :]
                    if n_ < 2:
                        tl = P.emit(eng, CP(accs_[par][:], src), waits=[exp_tok[G], l_free[k]], sig=S_dve, chain=False)
                    else:
                        tl = P.emit(eng, TT(accs_[par][:], src, accs_[par][:], ALU.add),
                                    waits=[exp_tok[G], stt["tok"][eng][par]], sig=S_dve, chain=False)
                    stt["tok"][eng][par] = tl
                    stt["cnt"][eng] = n_ + 1
                    pt_free[pi] = [tk, tl]
                    if g == ghi:
                        pending_L.append((ui, m, tk))
                        if m == 1:
                            q_free[qb_] = tk
                            if c == 15:
                                k_free[kb_] = tk

                def do_L():
                    ui, m, tk_o = pending_L.pop(0)
                    h, c = units[ui]
                    k = (2 * ui + m) % 2
                    bo, bl_ = 2 * m, 2 * m + 1
                    stt = lstate.pop((ui, m))
                    parts = []
                    for eng, accs_ in (("vector", LD[k]),):
                        for par in range(2):
                            if stt["cnt"][eng] > par:
                                parts.append((accs_[par], stt["tok"][eng][par]))
                    toks = [t_ for _, t_ in parts]
                    first_ = True
                    if len(parts) == 1:
                        P.emit("vector", CP(Lt[:], parts[0][0][:]), waits=toks)
                    else:
                        P.emit("vector", TT(Lt[:], parts[0][0][:], parts[1][0][:], ALU.add), waits=toks)
                        for (ap_, _) in parts[2:]:
                            P.emit("vector", TT(Lt[:], Lt[:], ap_[:], ALU.add))
                    P.emit("vector", CP(Lhi[:], Lt[:]))
                    td0 = P.emit("vector", TT(Llo[:], Lt[:], Lhi[:], ALU.subtract), sig=S_dve)
                    l_free[k] = td0
                    P.emit("tensor", MM(ps_a[:, bl_, :], ones[:], Lhi[:], False, False), waits=[td0])
                    tk = P.emit("tensor", MM(ps_a[:, bl_, :], ones[:], Llo[:], False, True), sig=S_pe)
                    if m == 0:
                        P.emit("vector", RCP(r1[:], ps_a[:, 1, :]), waits=[tk, tk_o])
                        td = P.emit("vector", TT(o1[:], ps_a[:, 0, :], r1[:], ALU.mult), sig=S_dve)
                        a_free[0] = td; a_free[1] = td
                    else:
                        P.emit("vector", RCP(r1[:], ps_a[:, 3, :]), waits=[tk, tk_o])
                        P.emit("vector", TT(o2[:], ps_a[:, 2, :], r1[:], ALU.mult))
                        P.emit("vector", STT(oo[:], o2[:], neglam, o1[:], ALU.mult, ALU.add), waits=[t_l])
                        td = P.emit("vector", TT(sq[:], oo[:], oo[:], ALU.mult), sig=S_dve)
                        a_free[3] = td
                        a_free[2] = td
                        pending_ss.append((td, ui))

                def do_ss():
                    td, ui = pending_ss.pop(0)
                    h, c = units[ui]
                    qb_ = ui % 2
                    csl = slice(c * 512, (c + 1) * 512)
                    tk = P.emit("tensor", MM(ps_a[:, 2, :], ones[:], sq[:]), waits=[td], sig=S_pe)
                    P.emit("scalar", ACT(rstd[:], ps_a[:, 2, :], AF.Ln, scale=1.0 / 128, bias=epsb[:, 0:1]), waits=[tk])
                    ta = P.emit("scalar", ACT(rstd[:], rstd[:], AF.Exp, scale=-0.5), sig=S_act)
                    a_free[2] = ta
                    si = ui % 2
                    P.emit("vector", STT(oo[:], oo[:], coef[:, 0:1], rstd[:], ALU.mult, ALU.mult), waits=[ta])
                    te = P.emit("vector", TT(stg[si][:], oo[:], sgt[qb_][:], ALU.mult), waits=[st_free[si], loads[("q", ui)]], sig=S_dve)
                    t_st = P.dma("gpsimd", OGc[h * 128:(h + 1) * 128, csl], stg[si][:], waits=[te], sig=S_st[si])
                    st_free[si] = t_st
                    stores.append(t_st)
                    sg_free[qb_] = te

                load_head(0)
                load_unit(0)
                load_unit(1)
                for step in range(NG + 2):
                    if step < NG:
                        ui, m, g = groups[step]
                        h, c = units[ui]
                        do_qk(step)
                        do_exp(step)
                        glo, ghi = urange[ui]
                        if g - glo == min(3, ghi - glo) and pending_L:
                            do_L()
                        if m == 0 and g - glo == min(6, ghi - glo) and pending_ss:
                            do_ss()
                        if m == 0 and g - glo == min(7, ghi - glo):
                            if ui >= 1 and ui + 1 < len(units):
                                load_unit(ui + 1)
                            if c == 8 and h + 1 < 4:
                                load_head(h + 1)
                    if step >= 2:
                        do_av(step - 2)
                while pending_L:
                    do_L()
                while pending_ss:
                    do_ss()
                for e_ in ("sync", "gpsimd", "scalar", "vector", "tensor"):
                    P.wait_sems(e_, S_st)
                P.flush(f"C{l}")

        phase_cast()
        done = False
        for l in range(depth):
            for nm, ph in (("T", phase_T), ("B", phase_B), ("A", phase_A), ("C", phase_C2)):
                ph(l)
                if stop_after == f"{nm}{l}":
                    done = True
                    break
            if done:
                break
        if not done:
            phase_T(depth)
    return nc


_CACHE = {}
_RUN_KW = {}


def kernel(x_prompt, x_sample, norm_g, w_in, w_oa, w_ob, w_oc, w_out, b_sink,
           lam_q1, lam_k1, lam_q2, lam_k2, c_subln_g, final_norm_g):
    f = lambda a: np.ascontiguousarray(np.asarray(a, dtype=np.float32))
    x_prompt = f(x_prompt); x_sample = f(x_sample)
    consts = make_consts()
    seg_p = np.zeros(NT, np.int64)
    seg_s = np.arange(NT) // 2048
    qa_p, ka_p = make_aug(seg_p, False)
    qa_s, ka_s = make_aug(seg_s, True)
    shared = dict(norm_g=f(norm_g), w_in=f(w_in), w_oa=f(w_oa), w_ob=f(w_ob), w_oc=f(w_oc), w_out=f(w_out),
                  b_sink=f(b_sink), lam_q1=f(lam_q1), lam_k1=f(lam_k1), lam_q2=f(lam_q2), lam_k2=f(lam_k2),
                  c_subln_g=f(c_subln_g), final_norm_g=f(final_norm_g).reshape(1, D), **consts)
    PROMPT_CORES = (0, 4)
    SAMPLE_CORES = (1, 2, 5, 6)
    zeros = np.zeros((NT, D), np.float32)
    in_maps = []
    for c in range(NCORES):
        if c in PROMPT_CORES:
            xs = x_prompt[PROMPT_CORES.index(c)]
            qa, ka = qa_p, ka_p
        elif c in SAMPLE_CORES:
            i = SAMPLE_CORES.index(c)
            xs = x_sample[4 * i:4 * i + 4].reshape(NT, D)
            qa, ka = qa_s, ka_s
        else:
            xs = zeros
            qa, ka = qa_p, ka_p
        in_maps.append(dict(x=np.ascontiguousarray(xs), c_qaug=qa, c_kaug=ka, **shared))
    if "nc" not in _CACHE:
        _CACHE["nc"] = build()
    res = run_bass_kernel_spmd(_CACHE["nc"], in_maps, core_ids=list(range(NCORES)), **_RUN_KW)
    _CACHE["res"] = res
    ys = [np.asarray(r["y"], dtype=np.float32) for r in res.results]
    y_prompt = np.stack([ys[c] for c in PROMPT_CORES], 0)
    y_sample = np.concatenate([ys[c].reshape(4, 2048, D) for c in SAMPLE_CORES], 0)
    return (y_prompt, y_sample)
```

```python
import math
from contextlib import ExitStack

import numpy as np
import ml_dtypes

import concourse.bass as bass
import concourse.mybir as mybir
from concourse.bass_utils import run_bass_kernel_spmd

F32 = mybir.dt.float32
BF16 = mybir.dt.bfloat16
AF = mybir.ActivationFunctionType
ALU = mybir.AluOpType
AX = mybir.AxisListType

NT = 8192
D = 1024
DIN = 11520
DEPTH = 4
NCORES = 8
NEG = -30000.0
EPS = 1e-6
C_SKIP = 144.0
A_PAT = ((128, 1), (512, 4), (2048, 16))

SEGS = [
    ("qa", 0, 1536, "q"), ("ka", 1536, 1536, "k"), ("va", 3072, 1536, "v"), ("ga", 4608, 512, "silu"),
    ("qb", 5120, 512, "q"), ("kb", 5632, 128, "k"), ("vb", 5760, 128, "v"), ("gb", 5888, 512, "silu"),
    ("qc", 6400, 512, "q"), ("kc", 6912, 512, "k"), ("vc", 7424, 512, "v"), ("gc", 7936, 512, "silu"),
    ("gm", 8448, 3072, "sigmoid"),
]


class Sem:
    def __init__(self, h):
        self.h = h
        self.n = 0


class Ring:
    def __init__(self, n):
        self.n = n
        self.i = 0
        self.free = [None] * n

    def next(self):
        i = self.i
        self.i = (i + 1) % self.n
        return i, self.free[i]


class Prog:
    ENG = ("sync", "scalar", "gpsimd", "vector", "tensor")

    def __init__(self, nc, stack):
        self.nc = nc
        self.stack = stack
        self.q = {e: [] for e in self.ENG}
        self.waited = {e: {} for e in self.ENG}
        self.sems = {}
        self.uid = 0
        self.engsem = {}
        self.last = {e: None for e in self.ENG}

    def sem(self, name):
        if name not in self.sems:
            self.sems[name] = Sem(self.stack.enter_context(self.nc.semaphore("s_" + name)))
        return self.sems[name]

    def name(self, base):
        self.uid += 1
        return f"{base}_{self.uid}"

    def emit(self, eng, fn, waits=(), sig=None, amt=1, chain=None, is_dma=False):
        comp = fn is not None and not is_dma and eng in self.engsem
        if comp:
            if sig is None:
                sig = self.engsem[eng]
            if chain is None:
                chain = True
            if chain and self.last[eng] is not None:
                waits = list(waits) + [self.last[eng]]
        ws = []
        for t in waits:
            if t is None:
                continue
            sem, val = t
            if self.waited[eng].get(sem, 0) >= val:
                continue
            self.waited[eng][sem] = val
            ws.append((sem.h, val))
        tok = None
        if sig is not None:
            sig.n += amt
            tok = (sig, sig.n)
        if comp:
            self.last[eng] = tok
        self.q[eng].append((ws, fn, sig.h if sig is not None else None, amt))
        return tok

    def dma(self, eng, out, in_, waits=(), sig=None):
        return self.emit(eng, lambda e: e.dma_start(out=out, in_=in_), waits, sig, 16, is_dma=True)

    def wait(self, eng, toks):
        self.emit(eng, None, toks)

    def wait_sems(self, eng, sems):
        self.emit(eng, None, [(s_, s_.n) for s_ in sems if s_.n > 0])

    def flush(self, scope=None, barrier=True):
        if scope is not None:
            with self.nc.named_scope(scope):
                self._flush(barrier)
        else:
            self._flush(barrier)

    def _flush(self, barrier=True):
        with self.nc.Block() as block:
            for name in self.ENG:
                items = self.q[name]

                def body(e, items=items):
                    for ws, fn, sh, amt in items:
                        for h, v in ws:
                            e.wait_ge(h, v)
                        if fn is not None:
                            ins = fn(e)
                            if sh is not None:
                                ins.then_inc(sh, amt)

                getattr(block, name)(body)
        if barrier:
            self.nc.all_engine_barrier()
        self.q = {e: [] for e in self.ENG}


def MM(out, lhsT, rhs, start=True, stop=True):
    return lambda e: e.matmul(out, lhsT=lhsT, rhs=rhs, start=start, stop=stop)


def TR(out, in_, ident):
    return lambda e: e.transpose(out, in_, ident)


def ACT(out, in_, func, scale=1.0, bias=None, accum_out=None):
    kw = {}
    if bias is not None:
        kw["bias"] = bias
    if accum_out is not None:
        kw["accum_out"] = accum_out
    return lambda e: e.activation(out=out, in_=in_, func=func, scale=scale, **kw)


def TT(out, in0, in1, op):
    return lambda e: e.tensor_tensor(out=out, in0=in0, in1=in1, op=op)


def TS(out, in0, s1, op0, s2=None, op1=None):
    if op1 is None:
        return lambda e: e.tensor_scalar(out=out, in0=in0, scalar1=s1, scalar2=None, op0=op0)
    return lambda e: e.tensor_scalar(out=out, in0=in0, scalar1=s1, scalar2=s2, op0=op0, op1=op1)


def STT(out, in0, scalar, in1, op0, op1):
    return lambda e: e.scalar_tensor_tensor(out=out, in0=in0, scalar=scalar, in1=in1, op0=op0, op1=op1)


def CP(out, in_):
    return lambda e: e.tensor_copy(out=out, in_=in_)


def RCP(out, in_):
    return lambda e: e.reciprocal(out=out, in_=in_)


def MSET(ap, v):
    return lambda e: e.memset(ap, v)


def ssl(start, n, step):
    if step == 1:
        return slice(start, start + n)
    return slice(start, start + step * (n - 1) + 1, step)


def bcast_rows(ap2d_row, nparts):
    a = ap2d_row
    return bass.AP(a.tensor, a.offset, [[0, nparts]] + [list(x) for x in a.ap[1:]])


def bf16(a):
    return np.asarray(a, np.float32).astype(ml_dtypes.bfloat16)


def make_consts():
    c = {}
    kp = np.arange(128)[:, None].astype(np.float64)
    cq = np.arange(384)[None, :].astype(np.float64)
    delta = (cq - 128.0) - kp
    bB = np.empty((128, 8, 384), np.float64)
    for h in range(8):
        slope = 2.0 ** (-(h + 1))
        bB[:, h, :] = np.where(np.abs(delta) <= 128, -slope * np.abs(delta), NEG)
    c["c_bB"] = bB.reshape(128, 8 * 384).astype(np.float32)
    cq = np.arange(256)[None, :]
    cc = cq // 128
    qq = (cq % 128).astype(np.float64)
    delta = qq - kp + np.where(cc == 0, -64.0, 64.0)
    bA = np.empty((128, 24, 256), np.float64)
    for g, (_, dil) in enumerate(A_PAT):
        for h in range(8):
            slope = np.float32(2.0 ** (-8.0 * (8 * g + h + 1) / 24))
            bA[:, g * 8 + h, :] = np.where(np.abs(delta) <= 64, -(np.float64(slope) * dil) * np.abs(delta), NEG)
    c["c_bA"] = bA.reshape(128, 24 * 256).astype(np.float32)
    q1 = np.arange(128)[None, :].astype(np.float64)
    bC = np.empty((128, 4, 128), np.float64)
    for h in range(4):
        m = 2.0 ** (-2.0 * (h + 1))
        bC[:, h, :] = -m * np.abs(q1 - kp)
    c["c_bC"] = bf16(bC.reshape(128, 512))
    c["c_id"] = bf16(np.eye(128))
    return c


def make_aug(seg_ids, masked):
    t = np.arange(NT)
    A = (t // 128).astype(np.float64)
    b = (t % 128).astype(np.float64)
    oh = np.zeros((4, NT), np.float64)
    oh[seg_ids, t] = 1.0
    qa = np.zeros((3, 8, NT), np.float64)
    qa[:, 0:4, :] = oh[None]
    al = np.stack([A, b, np.ones(NT), np.ones(NT)])
    qa[0, 4:8] = al
    qa[1, 4:8] = -al
    ka = np.zeros((4, 8, NT), np.float64)
    if masked:
        ka[:, 0:4, :] = (NEG * (1.0 - oh))[None]
    for h in range(4):
        m = 2.0 ** (-2.0 * (h + 1))
        ka[h, 4] = -128.0 * m
        ka[h, 5] = -m
        ka[h, 6] = 128.0 * m * A
        ka[h, 7] = m * b
    return bf16(qa), bf16(ka)


def build(depth=DEPTH, dbg=None, stop_after=None):
    nc = bass.Bass("TRN2", target_bir_lowering=False)
    dbg = dbg or ()

    def din(name, shape, dt=F32):
        return nc.dram_tensor(name, list(shape), dt, kind="ExternalInput").ap()

    def dscr(name, shape, dt=BF16):
        kind = {"kind": "ExternalOutput"} if name in dbg else {}
        return nc.dram_tensor(name, list(shape), dt, **kind).ap()

    x_in = din("x", [NT, D])
    y_out = nc.dram_tensor("y", [NT, D], F32, kind="ExternalOutput").ap()
    norm_g = din("norm_g", [DEPTH, D])
    w_in = din("w_in", [DEPTH, D, DIN])
    w_oa = din("w_oa", [DEPTH, 512, D])
    w_ob = din("w_ob", [DEPTH, 512, D])
    w_oc = din("w_oc", [DEPTH, 512, D])
    w_out = din("w_out", [DEPTH, D, D])
    b_sink = din("b_sink", [DEPTH, 8])
    lam_q1 = din("lam_q1", [DEPTH, 64])
    lam_k1 = din("lam_k1", [DEPTH, 64])
    lam_q2 = din("lam_q2", [DEPTH, 64])
    lam_k2 = din("lam_k2", [DEPTH, 64])
    subln_g = din("c_subln_g", [DEPTH, 128])
    final_g = din("final_norm_g", [1, D])
    c_bB = din("c_bB", [128, 8 * 384], F32)
    c_bA = din("c_bA", [128, 24 * 256], F32)
    c_bC = din("c_bC", [128, 512], BF16)
    c_id = din("c_id", [128, 128], BF16)
    c_qaug = din("c_qaug", [3, 8, NT], BF16)
    c_kaug = din("c_kaug", [4, 8, NT], BF16)

    wb_in = dscr("wb_in", [depth, D, DIN])
    wb_o = [dscr(f"wb_o{i}", [depth, 512, D]) for i in range(3)]
    wb_out = dscr("wb_out", [depth, D, D])
    XR = dscr("XR", [NT, D], F32)
    QaT = dscr("QaT", [1536, NT]); KaT = dscr("KaT", [1536, NT]); Va = dscr("Va", [NT, 1536])
    QbT = dscr("QbT", [512, NT]); KbT = dscr("KbT", [128, NT]); Vb = dscr("Vb", [NT, 128])
    QcT = dscr("QcT", [512, NT]); KcT = dscr("KcT", [512, NT]); Vc = dscr("Vc", [NT, 512])
    SGa = dscr("SGa", [512, NT]); SGb = dscr("SGb", [512, NT]); SGc = dscr("SGc", [512, NT])
    SGm = dscr("SGm", [3072, NT])
    OGa = dscr("OGa", [512, NT]); OGb = dscr("OGb", [512, NT]); OGc = dscr("OGc", [512, NT])
    DEST = {"qa": QaT, "ka": KaT, "va": Va, "ga": SGa, "qb": QbT, "kb": KbT, "vb": Vb, "gb": SGb,
            "qc": QcT, "kc": KcT, "vc": Vc, "gc": SGc, "gm": SGm}

    with ExitStack() as gst:
        P = Prog(nc, gst)
        S_pe, S_act, S_dve, S_pool = P.sem("pe"), P.sem("act"), P.sem("dve"), P.sem("pool")
        P.engsem = {"scalar": S_act, "vector": S_dve, "gpsimd": S_pool}

        def sbuf(st, base, shape, dt):
            return st.enter_context(nc.sbuf_tensor(P.name(base), list(shape), dt))

        def psum(st, base, shape, dt):
            return st.enter_context(nc.psum_tensor(P.name(base), list(shape), dt))

        cast_tok = {}

        def phase_cast():
            for l in range(depth):
                S = P.sem(f"cast{l}")
                for r in range(8):
                    P.dma("gpsimd", wb_in[l, r * 128:(r + 1) * 128, :], w_in[l, r * 128:(r + 1) * 128, :], sig=S)
                for i, w in enumerate((w_oa, w_ob, w_oc)):
                    P.dma("gpsimd", wb_o[i][l, :, :], w[l, :, :], sig=S)
                P.dma("gpsimd", wb_out[l, :, :], w_out[l, :, :], sig=S)
                cast_tok[l] = (S, S.n)
            P.flush("cast", barrier=False)

        def phase_T(l):
            first = l == 0
            last = l == depth
            Xsrc = x_in if l <= 1 else XR
            with ExitStack() as st:
                ps = psum(st, "psT", [128, 6, 512], F32)
                pt = psum(st, "ptT", [128, 2, 1024], BF16)
                ring = Ring(6)
                ident = sbuf(st, "ident", [128, 128], BF16)
                gb = sbuf(st, "gb", [128, D], F32)
                xts = [sbuf(st, "xt", [128, 4, D], F32) for _ in range(2)]
                junk = sbuf(st, "junk", [128, D], BF16)
                ssq = sbuf(st, "ssq", [128, 32], F32)
                epsb = sbuf(st, "epsb", [128, 1], F32)
                S_c = P.sem("T_const"); S_x = [P.sem("T_x0"), P.sem("T_x1")]; S_xst = P.sem("T_xst")
                P.wait("sync", [cast_tok[k] for k in range(min(l, depth - 1) + 1)])
                P.dma("sync", ident[:], c_id[:, :], sig=S_c)
                gsrc = final_g[0:1, :] if last else norm_g[l:l + 1, :]
                P.dma("sync", gb[:], bcast_rows(gsrc, 128), sig=S_c)
                P.emit("vector", MSET(epsb[:], EPS), sig=S_dve)
                t_eps = (S_dve, S_dve.n)
                if not first:
                    ogs = [sbuf(st, "og", [128, 3, 4, 512], BF16) for _ in range(2)]
                    gm = [sbuf(st, "gm", [128, 3, 512], BF16) for _ in range(2)]
                    tmp = [[sbuf(st, "tmp", [128, 512], F32) for _ in range(3)] for _ in range(2)]
                    mixed = sbuf(st, "mixed", [128, 8, 512], BF16)
                    Wo = [sbuf(st, "Wo", [128, 4, D], BF16) for _ in range(3)]
                    wout = sbuf(st, "wout", [128, 8, D], BF16)
                    for i in range(3):
                        P.dma("sync", Wo[i][:], wb_o[i][l - 1].rearrange("(k p) f -> p k f", p=128), sig=S_c)
                    P.dma("sync", wout[:], wb_out[l - 1].rearrange("(k p) f -> p k f", p=128), sig=S_c)
                    S_og = [P.sem("T_og0"), P.sem("T_og1")]; S_gm = [P.sem("T_gm0"), P.sem("T_gm1")]
                if not last:
                    hT = sbuf(st, "hT", [128, 8, 2048], BF16)
                    hb = sbuf(st, "hb", [128, 4, D], BF16)
                    Wt = [sbuf(st, "Wt", [128, 8, 512], BF16) for _ in range(2)]
                    stF = [sbuf(st, "stF", [128, 2048], BF16) for _ in range(3)]
                    stV = [sbuf(st, "stV", [128, 4, 512], BF16) for _ in range(2)]
                    S_w = [P.sem("T_w0"), P.sem("T_w1")]
                    S_stF = [P.sem(f"T_stF{i}") for i in range(3)]
                    S_stV = [P.sem(f"T_stV{i}") for i in range(2)]
                t_c = (S_c, S_c.n)
                state = dict(og_free=[None, None], mixed_free=None, x_free=[[], []], gm_free=[None, None], hb_free=None,
                             w_free=[None, None], stF_free=[None] * 3, stV_free=[None] * 2, wi=0, fi=0, vi=0,
                             pt_free=[None, None], pti=0, stores=[], ld={}, tmp_free=[None, None], ssq_free=[None, None])

                def T1_load(tt):
                    par = tt % 2
                    xsl = slice(tt * 512, tt * 512 + 512)
                    t_x = P.dma("sync", xts[par][:], Xsrc[xsl, :].rearrange("(s p) f -> p s f", p=128),
                                waits=state["x_free"][par], sig=S_x[par])
                    t_og = None
                    if not first:
                        for b_, src in enumerate((OGa, OGb, OGc)):
                            P.dma("sync", ogs[par][:, b_], src.rearrange("(k p) t -> p k t", p=128)[:, :, xsl],
                                  waits=[state["og_free"][par]], sig=S_og[par])
                        t_og = (S_og[par], S_og[par].n)
                    state["ld"][tt] = (t_x, t_og)

                def T1(tt):
                    tok0 = tt * 512
                    par = tt % 2
                    xt = xts[par]
                    xsl = slice(tok0, tok0 + 512)
                    t_x, t_og = state["ld"].pop(tt)
                    if first and tt + 1 < 16:
                        T1_load(tt + 1)
                    x_ready = t_x
                    if not first:
                        og = ogs[par]
                        gsrc_all = SGm.rearrange("(b f p) t -> p b f t", b=3, p=128)
                        t_mixed = None
                        for fc in range(8):
                            sl = fc % 2
                            t_gm = P.dma("sync", gm[sl][:], gsrc_all[:, :, fc, xsl],
                                         waits=[state["gm_free"][sl]], sig=S_gm[sl])
                            if fc == 3 and tt + 1 < 16:
                                T1_load(tt + 1)
                            bks = []
                            for b in range(3):
                                bi, bfree = ring.next()
                                for kc in range(4):
                                    tk = P.emit("tensor", MM(ps[:, bi, :], Wo[b][:, kc, fc * 128:(fc + 1) * 128],
                                                             og[:, b, kc, :], kc == 0, kc == 3),
                                                waits=[bfree, t_og, t_c], sig=S_pe if kc == 3 else None)
                                bks.append((bi, tk))
                            if fc == 7:
                                state["og_free"][par] = bks[-1][1]
                            for b in range(3):
                                bi, tk = bks[b]
                                td = P.emit("vector", TT(tmp[sl][b][:], ps[:, bi, :], gm[sl][:, b, :], ALU.mult),
                                            waits=[tk, t_gm, state["tmp_free"][sl]], sig=S_dve, chain=False)
                                ring.free[bi] = td
                            state["gm_free"][sl] = td
                            w = [td]
                            if fc == 0:
                                w.append(state["mixed_free"])
                            P.emit("gpsimd", TT(tmp[sl][0][:], tmp[sl][0][:], tmp[sl][1][:], ALU.add), waits=w, sig=S_pool, chain=False)
                            t_mixed = P.emit("gpsimd", TT(mixed[:, fc, :], tmp[sl][0][:], tmp[sl][2][:], ALU.add), sig=S_pool)
                            state["tmp_free"][sl] = t_mixed
                        for sub in range(4):
                            for fh in range(2):
                                bi, bfree = ring.next()
                                for kc in range(8):
                                    tk = P.emit("tensor", MM(ps[:, bi, :], mixed[:, kc, sub * 128:(sub + 1) * 128],
                                                             wout[:, kc, fh * 512:(fh + 1) * 512], kc == 0, kc == 7),
                                                waits=[bfree, t_mixed], sig=S_pe if kc == 7 else None)
                                td = P.emit("vector", TT(xt[:, sub, fh * 512:(fh + 1) * 512], ps[:, bi, :],
                                                         xt[:, sub, fh * 512:(fh + 1) * 512], ALU.add),
                                            waits=[tk, t_x], sig=S_dve, chain=False)
                                ring.free[bi] = td
                        state["mixed_free"] = tk
                        x_ready = td
                    frees = []
                    if not first and not last:
                        t_st = P.dma("gpsimd", XR[xsl, :].rearrange("(s p) f -> p s f", p=128), xt[:],
                                     waits=[x_ready], sig=S_xst)
                        frees.append(t_st)
                    q0 = 16 * par
                    for sub in range(4):
                        t_sq = P.emit("scalar", ACT(junk[:], xt[:, sub, :], AF.Square, accum_out=ssq[:, q0 + sub:q0 + sub + 1]),
                                      waits=[x_ready, state["ssq_free"][par]], sig=S_act)
                    P.emit("scalar", ACT(ssq[:, q0 + 4:q0 + 8], ssq[:, q0:q0 + 4], AF.Ln, scale=1.0 / D, bias=epsb[:, 0:1]), waits=[t_eps])
                    t_r = P.emit("scalar", ACT(ssq[:, q0 + 8:q0 + 12], ssq[:, q0 + 4:q0 + 8], AF.Exp, scale=-0.5), sig=S_act)
                    if last:
                        for sub in range(4):
                            td = P.emit("vector", STT(xt[:, sub, :], xt[:, sub, :], ssq[:, q0 + 8 + sub:q0 + 9 + sub], gb[:],
                                                      ALU.mult, ALU.mult), waits=[t_r, t_c], sig=S_dve, chain=False)
                        t_st = P.dma("gpsimd", y_out[xsl, :].rearrange("(s p) f -> p s f", p=128), xt[:],
                                     waits=[td], sig=S_xst)
                        state["x_free"][par] = [t_st]
                        state["ssq_free"][par] = td
                        return
                    for sub in range(4):
                        w = [t_r, t_c]
                        if sub == 0:
                            w.append(state["hb_free"])
                        td = P.emit("vector", STT(hb[:, sub, :], xt[:, sub, :], ssq[:, q0 + 8 + sub:q0 + 9 + sub], gb[:],
                                                  ALU.mult, ALU.mult), waits=w, sig=S_dve, chain=False)
                        pi = state["pti"]; state["pti"] = 1 - pi
                        for kc in range(8):
                            tk = P.emit("tensor", TR(pt[:, pi, kc * 128:(kc + 1) * 128], hb[:, sub, kc * 128:(kc + 1) * 128], ident[:]),
                                        waits=[td, state["pt_free"][pi], t_c], sig=S_pe if kc == 7 else None)
                        off = (tt % 4) * 512 + sub * 128
                        ta = P.emit("scalar", ACT(hT[:, :, off:off + 128], pt[:, pi, :].rearrange("p (k t) -> p k t", k=8), AF.Copy),
                                    waits=[tk], sig=S_act, chain=False)
                        state["pt_free"][pi] = ta
                    state["hb_free"] = tk
                    state["hT_ready"] = ta
                    state["ssq_free"][par] = td
                    frees += [t_sq, td]
                    state["x_free"][par] = frees

                wtiles = [(name, c0, kind, w0, min(512, ncols - w0)) for (name, c0, ncols, kind) in SEGS
                          for w0 in range(0, ncols, 512)]
                NW = len(wtiles)
                w_tok = {}

                def load_w(k):
                    name, c0, kind, w0, wc = wtiles[k % NW]
                    wi = k % 2
                    w_tok[k] = P.dma("sync", Wt[wi][:, :, 0:wc],
                                     wb_in[l].rearrange("(k p) c -> p k c", p=128)[:, :, c0 + w0:c0 + w0 + wc],
                                     waits=[state["w_free"][wi]], sig=S_w[wi])

                def T2(s):
                    tsl = slice(s * 2048, (s + 1) * 2048)
                    t_h = state["hT_ready"]
                    for ti, (name, c0, kind, w0, wc) in enumerate(wtiles):
                        dest = DEST[name]
                        k = s * NW + ti
                        wi = k % 2
                        if k + 1 < 4 * NW:
                            load_w(k + 1)
                        t_w = w_tok[k]
                        if kind == "v":
                            for s16 in range(16):
                                vi = state["vi"]
                                bi, bfree = ring.next()
                                for kc in range(8):
                                    tk = P.emit("tensor", MM(ps[:, bi, 0:wc], hT[:, kc, s16 * 128:(s16 + 1) * 128],
                                                             Wt[wi][:, kc, 0:wc], kc == 0, kc == 7),
                                                waits=[bfree, t_w, t_h], sig=S_pe if kc == 7 else None)
                                w = [tk]
                                if s16 % 4 == 0:
                                    w.append(state["stV_free"][vi])
                                ta = P.emit("scalar", ACT(stV[vi][:, s16 % 4, 0:wc], ps[:, bi, 0:wc], AF.Copy),
                                            waits=w, sig=S_act, chain=False)
                                ring.free[bi] = ta
                                if s16 % 4 == 3:
                                    r0 = s * 2048 + (s16 // 4) * 512
                                    t_st = P.dma("gpsimd", dest[r0:r0 + 512, w0:w0 + wc].rearrange("(s p) c -> p s c", p=128),
                                                 stV[vi][:, :, 0:wc], waits=[ta], sig=S_stV[vi])
                                    state["stV_free"][vi] = t_st
                                    state["vi"] = 1 - vi
                            state["w_free"][wi] = tk
                            continue
                        for sc in range(wc // 128):
                            fi = state["fi"]; state["fi"] = (fi + 1) % 3
                            for i in range(4):
                                bi, bfree = ring.next()
                                for kc in range(8):
                                    tk = P.emit("tensor", MM(ps[:, bi, :], Wt[wi][:, kc, sc * 128:(sc + 1) * 128],
                                                             hT[:, kc, i * 512:(i + 1) * 512], kc == 0, kc == 7),
                                                waits=[bfree, t_w, t_h], sig=S_pe if kc == 7 else None)
                                w = [tk]
                                if i == 0:
                                    w.append(state["stF_free"][fi])
                                o = stF[fi][:, i * 512:(i + 1) * 512]
                                if kind == "q":
                                    te = P.emit("vector", TS(o, ps[:, bi, :], 0.125, ALU.mult), waits=w, sig=S_dve, chain=False)
                                elif kind == "k":
                                    te = P.emit("vector", CP(o, ps[:, bi, :]), waits=w, sig=S_dve, chain=False)
                                elif kind == "silu":
                                    te = P.emit("scalar", ACT(o, ps[:, bi, :], AF.Silu), waits=w, sig=S_act, chain=False)
                                else:
                                    te = P.emit("scalar", ACT(o, ps[:, bi, :], AF.Sigmoid), waits=w, sig=S_act, chain=False)
                                ring.free[bi] = te
                            f0 = w0 + sc * 128
                            t_st = P.dma("gpsimd", dest[f0:f0 + 128, tsl], stF[fi][:], waits=[te], sig=S_stF[fi])
                            state["stF_free"][fi] = t_st
                        state["w_free"][wi] = tk

                if not last:
                    load_w(0)
                T1_load(0)
                for s in range(4):
                    for i in range(4):
                        T1(4 * s + i)
                    if not last:
                        T2(s)
                st_sems = [S_xst]
                if not last:
                    st_sems += S_stF + S_stV
                for e_ in ("sync", "gpsimd", "scalar", "vector", "tensor"):
                    P.wait_sems(e_, st_sems)
                P.flush(f"T{l}")

        def run_banded(res, kbs):
            ps_s, ps_a, PT, SB = res["ps_s"], res["ps_a"], res["PT"], res["SB"]
            NPT = res["NPT"]
            NS = res["NS"]
            G = 2
            ngroups = (len(kbs) + G - 1) // G
            s_free = res["s_free"]
            sb_free = res["sb_free"]
            pt_free = res["pt_free"]
            a_free = res["a_free"]
            exp_tok = {}
            qk_tok = {}
            bias_tok = {}

            def do_qk(gi):
                sl = gi % NS
                tk = None
                members = kbs[gi * G:(gi + 1) * G]
                for m, kb in enumerate(members):
                    for qi_, (lo, hi, lhsT, rhs) in enumerate(kb["qk"]):
                        islast = (qi_ == len(kb["qk"]) - 1) and (m == len(members) - 1)
                        tk = P.emit("tensor", MM(ps_s[:, sl * G + m, lo:hi], lhsT, rhs, True, True),
                                    waits=[s_free[sl]] + list(kb["waits"]), sig=S_pe if islast else None)
                qk_tok[gi] = tk

            def do_bias(gi):
                sl = gi % NS
                members = kbs[gi * G:(gi + 1) * G]
                cols = members[0]["cols"]
                n = len(members)
                bap = members[0]["bias"]
                bb = bass.AP(bap.tensor, bap.offset, [list(bap.ap[0]), [0, n], list(bap.ap[-1])])
                td = P.emit("vector", TT(SB[:, sl, 0:n, 0:cols], ps_s[:, sl * G:sl * G + n, 0:cols], bb, ALU.add),
                            waits=[qk_tok[gi], sb_free[sl]] + list(members[0]["waits"]), sig=S_dve, chain=False)
                bias_tok[gi] = td
                s_free[sl] = td

            def do_exp(gi):
                sl = gi % NS
                members = kbs[gi * G:(gi + 1) * G]
                cols = members[0]["cols"]
                n = len(members)
                slots = [(gi * G + m) % NPT for m in range(n)]
                w = [bias_tok[gi]] + [pt_free[s_] for s_ in slots]
                ta = P.emit("scalar", ACT(PT[:, slots[0]:slots[0] + n, 0:cols], SB[:, sl, 0:n, 0:cols], AF.Exp),
                            waits=w, sig=S_act, chain=False)
                exp_tok[gi] = ta
                sb_free[sl] = ta

            def do_av(gi):
                members = kbs[gi * G:(gi + 1) * G]
                for m, kb in enumerate(members):
                    for job in kb["av"]:
                        bank, slot = job["bank"], job["slot"]
                        qa, qb = job["qa"], job["qb"]
                        np_ = len(job["parts"])
                        for pi_, (kidx, c, vap) in enumerate(job["parts"]):
                            w = [exp_tok[kidx // G]]
                            if pi_ == 0 and job["bank_first"]:
                                w += list(a_free[bank] or ())
                            w += list(job.get("waits", ()))
                            tk = P.emit("tensor", MM(ps_a[:, bank, slot * 128 + qa:slot * 128 + qb], vap,
                                                     PT[:, kidx % NPT, c * 128 + qa:c * 128 + qb], pi_ == 0, pi_ == np_ - 1),
                                        waits=w, sig=S_pe if (pi_ == np_ - 1) else None)
                            pt_free[kidx % NPT] = tk
                        if job["bank_done"]:
                            a_free[bank] = job["evac"](tk)

            for step in range(ngroups + NS):
                if step < ngroups:
                    do_qk(step)
                    do_bias(step)
                    do_exp(step)
                if step >= NS:
                    do_av(step - NS)

        def phase_B(l):
            with ExitStack() as st:
                ps_s = psum(st, "psBs", [128, 6, 512], F32)
                ps_a = psum(st, "psBa", [128, 2, 512], F32)
                NPT = 12
                PT = sbuf(st, "PT", [128, NPT, 384], BF16)
                ident = sbuf(st, "ident", [128, 128], BF16)
                bias = sbuf(st, "biasB", [128, 8, 384], F32)
                SB = sbuf(st, "SBb", [128, 3, 2, 384], F32)
                QT = [sbuf(st, "QTb", [68, NT], BF16) for _ in range(2)]
                KT = [sbuf(st, "KTb", [68, NT], BF16) for _ in range(2)]
                Vg = sbuf(st, "Vb", [128, 64, 2, 128], BF16)
                sg = [sbuf(st, "sgb", [64, NT], BF16) for _ in range(2)]
                esink = sbuf(st, "esink", [128, 8], F32)
                rec = [sbuf(st, "recB", [64, 512], F32) for _ in range(2)]
                of = [sbuf(st, "ofB", [64, 512], F32) for _ in range(2)]
                rec_free = [None, None]
                stg = [sbuf(st, "stgB", [64, 512], BF16) for _ in range(2)]
                S_c = P.sem("B_c"); S_q = [P.sem("B_q0"), P.sem("B_q1")]; S_st = [P.sem("B_st0"), P.sem("B_st1")]
                P.dma("sync", ident[:], c_id[:, :], sig=S_c)
                P.dma("sync", bias[:], c_bB.rearrange("p (h c) -> p h c", h=8), sig=S_c)
                P.dma("sync", esink[:], bcast_rows(b_sink[l:l + 1, :], 128), sig=S_c)
                for kv in range(2):
                    P.dma("sync", KT[kv][0:64, :], KbT[kv * 64:(kv + 1) * 64, :], sig=S_c)
                    P.dma("sync", KT[kv][64:68, :], c_kaug[0, 0:4, :], sig=S_c)
                    P.dma("sync", QT[kv][64:68, :], c_qaug[2, 0:4, :], sig=S_c)
                P.emit("gpsimd", MSET(Vg[:], 1.0), sig=S_pool)
                t_ms = (S_pool, S_pool.n)
                vsrc = Vb.rearrange("(j p) (k d) -> p j k d", p=128, k=2)
                for q4 in range(4):
                    for kv in range(2):
                        P.dma("sync", Vg[:, q4 * 16:(q4 + 1) * 16, kv, 0:64], vsrc[:, q4 * 16:(q4 + 1) * 16, kv, :],
                              waits=[t_ms], sig=S_c)
                t_c = (S_c, S_c.n)
                t_es = P.emit("scalar", ACT(esink[:], esink[:], AF.Exp), waits=[t_c], sig=S_act)
                res = dict(ps_s=ps_s, ps_a=ps_a, PT=PT, SB=SB, NPT=NPT, NS=3, s_free=[None] * 3, sb_free=[None] * 3,
                           pt_free=[None] * NPT, a_free=[None, None])
                q_free = [None, None]
                st_free = [None, None]
                stores = []
                sti = [0]
                tq = {}

                def load_head(h):
                    qi = h % 2
                    P.dma("sync", QT[qi][0:64, :], QbT[h * 64:(h + 1) * 64, :], waits=[q_free[qi]], sig=S_q[qi])
                    P.dma("sync", sg[qi][:], SGb[h * 64:(h + 1) * 64, :], waits=[q_free[qi]], sig=S_q[qi])
                    tq[h] = (S_q[qi], S_q[qi].n)

                load_head(0)
                for h in range(8):
                    kv = h // 4
                    qi = h % 2
                    if h + 1 < 8:
                        load_head(h + 1)
                    t_q = tq[h]
                    kbs = []
                    last_tok = [None]

                    def mk_evac(bank, q0, h=h, qi=qi):
                        def evac(tk):
                            tsl = slice(q0 * 128, q0 * 128 + 512)
                            ri = sti[0]; sti[0] = 1 - ri
                            ta0 = P.emit("scalar", ACT(rec[ri][:], ps_a[64:128, bank, :], AF.Ln, bias=esink[64:128, h:h + 1]),
                                         waits=[tk, t_es, rec_free[ri]], sig=S_act, chain=False)
                            ta = P.emit("scalar", ACT(rec[ri][:], rec[ri][:], AF.Exp, scale=-1.0), sig=S_act)
                            td = P.emit("vector", TT(of[ri][:], ps_a[0:64, bank, :], sg[qi][:, tsl], ALU.mult),
                                        waits=[tk, t_q, rec_free[ri]], sig=S_dve, chain=False)
                            te = P.emit("gpsimd", TT(stg[ri][:], of[ri][:], rec[ri][:], ALU.mult),
                                        waits=[ta, td, st_free[ri]], sig=S_pool, chain=False)
                            rec_free[ri] = te
                            t_st = P.dma("gpsimd", OGb[h * 64:(h + 1) * 64, tsl], stg[ri][:], waits=[te], sig=S_st[ri])
                            st_free[ri] = t_st
                            stores.append(t_st)
                            last_tok[0] = te
                            return [ta0, td]
                        return evac

                    for j in range(64):
                        qlo = max(j - 1, 0); qhi = min(j + 1, 63)
                        lo = (qlo - (j - 1)) * 128; hi = (qhi - (j - 1) + 1) * 128
                        kb = dict(qk=[(lo, hi, KT[kv][0:68, j * 128:(j + 1) * 128], QT[qi][0:68, qlo * 128:(qhi + 1) * 128])],
                                  bias=bias[:, h, :], cols=384, waits=[t_q, t_c], av=[])
                        done = []
                        if j >= 1:
                            done.append(j - 1)
                        if j == 63:
                            done.append(63)
                        for qb_ in done:
                            parts = [(jj, qb_ - jj + 1, Vg[:, jj, kv, :]) for jj in (qb_ - 1, qb_, qb_ + 1) if 0 <= jj < 64]
                            bank = (qb_ // 4) % 2
                            job = dict(parts=parts, qa=0, qb=128, slot=qb_ % 4, bank=bank, bank_first=(qb_ % 4 == 0),
                                       bank_done=(qb_ % 4 == 3), waits=[t_c])
                            if job["bank_done"]:
                                job["evac"] = mk_evac(bank, qb_ - 3)
                            kb["av"].append(job)
                        kbs.append(kb)
                    run_banded(res, kbs)
                    q_free[qi] = last_tok[0]
                for e_ in ("sync", "gpsimd", "scalar", "vector", "tensor"):
                    P.wait_sems(e_, S_st)
                P.flush(f"B{l}")

        def phase_A(l):
            with ExitStack() as st:
                ps_s = psum(st, "psAs", [128, 6, 512], F32)
                ps_a = psum(st, "psAa", [128, 2, 512], F32)
                NPT = 10
                PT = sbuf(st, "PTa", [128, NPT, 256], BF16)
                bA = sbuf(st, "bA", [128, 24, 256], F32)
                SB = sbuf(st, "SBa", [128, 3, 2, 256], F32)
                QT = [sbuf(st, "QTa", [68, NT], BF16) for _ in range(2)]
                KT = [sbuf(st, "KTa", [68, NT], BF16) for _ in range(2)]
                Vg = [sbuf(st, "Va", [128, 64, 128], BF16) for _ in range(2)]
                accs = [sbuf(st, "accA", [128, NT], F32) for _ in range(2)]
                sgt = [sbuf(st, "sga", [64, 512], BF16) for _ in range(2)]
                rec = [sbuf(st, "recA", [64, 512], F32) for _ in range(2)]
                of = [sbuf(st, "ofA", [64, 512], F32)] * 2
                rec_free = [None, None]
                stg = [sbuf(st, "stgA", [64, 512], BF16) for _ in range(2)]
                S_c = P.sem("A_c"); S_q = [P.sem("A_q0"), P.sem("A_q1")]
                S_st = [P.sem("A_st0"), P.sem("A_st1")]; S_sg = [P.sem("A_sg0"), P.sem("A_sg1")]
                P.dma("sync", bA[:], c_bA.rearrange("p (h c) -> p h c", h=24), sig=S_c)
                for i in range(2):
                    P.dma("sync", KT[i][64:68, :], c_kaug[0, 0:4, :], sig=S_c)
                    P.dma("sync", QT[i][64:68, :], c_qaug[2, 0:4, :], sig=S_c)
                    P.emit("gpsimd", MSET(Vg[i][:], 1.0), sig=S_pool)
                t_ms = (S_pool, S_pool.n)
                t_c = (S_c, S_c.n)
                res = dict(ps_s=ps_s, ps_a=ps_a, PT=PT, SB=SB, NPT=NPT, NS=3, s_free=[None] * 3, sb_free=[None] * 3,
                           pt_free=[None] * NPT, a_free=[None, None])
                q_free = [None, None]
                st_free = [None, None]
                sg_free = [None, None]
                stores = []
                acc_free = [None, None]
                ui = 0
                tq = {}

                def load_unit(u):
                    h, g = divmod(u, 3)
                    dil = A_PAT[g][1]
                    qi = u % 2
                    f0 = g * 512 + h * 64
                    P.dma("sync", QT[qi][0:64, :], QaT[f0:f0 + 64, :], waits=[q_free[qi]], sig=S_q[qi])
                    P.dma("sync", KT[qi][0:64, :], KaT[f0:f0 + 64, :], waits=[q_free[qi]], sig=S_q[qi])
                    U = NT // dil
                    nb = U // 128
                    vsrc = Va.rearrange("(u d) c -> d u c", d=dil)
                    for r in range(dil):
                        vr = vsrc[r, :, f0:f0 + 64].rearrange("(j p) c -> p j c", p=128)
                        for j0 in range(0, nb, 16):
                            j1 = min(nb, j0 + 16)
                            P.dma("sync", Vg[qi][:, r * nb + j0:r * nb + j1, 0:64], vr[:, j0:j1, :],
                                  waits=[q_free[qi], t_ms], sig=S_q[qi])
                    tq[u] = (S_q[qi], S_q[qi].n)

                load_unit(0)
                for h in range(8):
                    last_acc = None
                    acc = accs[h % 2]
                    for g, (_, dil) in enumerate(A_PAT):
                        qi = ui % 2
                        if ui + 1 < 24:
                            load_unit(ui + 1)
                        t_q = tq[ui]
                        ui += 1
                        f0 = g * 512 + h * 64
                        U = NT // dil
                        nb = U // 128
                        kbs = []
                        last_tok = [None]

                        def mk_evac(bank, lo, hi, t0, n, g=g, dil=dil, acc=acc, h=h):
                            def evac(tk):
                                nonlocal last_acc
                                dst = acc[:, ssl(t0, n, dil)]
                                w = [tk]
                                if g == 0:
                                    w += list(acc_free[h % 2] or ())
                                    td = P.emit("vector", CP(dst, ps_a[:, bank, lo:hi]), waits=w, sig=S_dve, chain=False)
                                else:
                                    td = P.emit("vector", TT(dst, ps_a[:, bank, lo:hi], dst, ALU.add), waits=w, sig=S_dve)
                                last_tok[0] = td
                                last_acc = td
                                return [td]
                            return evac

                        kidx = 0
                        for r in range(dil):
                            for j in range(nb):
                                tb = r + dil * 128 * j
                                ulo = max(128 * j - 64, 0); uhi = min(128 * j + 192, U)
                                lo = ulo - (128 * j - 64); hi = uhi - (128 * j - 64)
                                kb = dict(qk=[(lo, hi, KT[qi][0:68, ssl(tb, 128, dil)],
                                               QT[qi][0:68, ssl(r + dil * ulo, uhi - ulo, dil)])],
                                          bias=bA[:, g * 8 + h, :], cols=256, waits=[t_q, t_c], av=[])
                                done = [j - 1]
                                if j == nb - 1:
                                    done.append(nb - 1)
                                for qb_ in done:
                                    parts = []
                                    for jj in (qb_, qb_ + 1):
                                        if 0 <= jj < nb:
                                            parts.append((kidx - (j - jj), qb_ - jj + 1, Vg[qi][:, r * nb + jj, :]))
                                    qa = 64 if qb_ == -1 else 0
                                    qbb = 64 if qb_ == nb - 1 else 128
                                    seq = qb_ + 1
                                    bank = (seq // 4) % 2
                                    slot = seq % 4
                                    bank_done = (slot == 3) or (qb_ == nb - 1)
                                    job = dict(parts=parts, qa=qa, qb=qbb, slot=slot, bank=bank, bank_first=(slot == 0),
                                               bank_done=bank_done, waits=[t_c])
                                    if bank_done:
                                        first_qb = qb_ - slot
                                        c_lo = 64 if first_qb == -1 else 0
                                        c_hi = slot * 128 + qbb
                                        u0 = 128 * first_qb + 64 + c_lo
                                        job["evac"] = mk_evac(bank, c_lo, c_hi, r + dil * u0, c_hi - c_lo)
                                    kb["av"].append(job)
                                kbs.append(kb)
                                kidx += 1
                        run_banded(res, kbs)
                        q_free[qi] = last_tok[0]
                    for c in range(16):
                        tsl = slice(c * 512, (c + 1) * 512)
                        si = c % 2
                        t_sg = P.dma("sync", sgt[si][:], SGa[h * 64:(h + 1) * 64, tsl], waits=[sg_free[si]], sig=S_sg[si])
                        P.emit("scalar", ACT(rec[si][:], acc[64:128, tsl], AF.Ln), waits=[last_acc, rec_free[si]], sig=S_act, chain=False)
                        ta = P.emit("scalar", ACT(rec[si][:], rec[si][:], AF.Exp, scale=-1.0), sig=S_act)
                        tp = P.emit("gpsimd", TT(of[si][:], acc[0:64, tsl], sgt[si][:], ALU.mult),
                                    waits=[last_acc, t_sg, rec_free[si]], sig=S_pool)
                        te = P.emit("gpsimd", TT(stg[si][:], of[si][:], rec[si][:], ALU.mult), waits=[ta, st_free[si]], sig=S_pool)
                        sg_free[si] = te
                        rec_free[si] = te
                        t_st = P.dma("gpsimd", OGa[h * 64:(h + 1) * 64, tsl], stg[si][:], waits=[te], sig=S_st[si])
                        st_free[si] = t_st
                        stores.append(t_st)
                    acc_free[h % 2] = [te, ta]
                for e_ in ("sync", "gpsimd", "scalar", "vector", "tensor"):
                    P.wait_sems(e_, S_st)
                P.flush(f"A{l}")

        def phase_C2(l):
            lam_init = 0.8 - 0.6 * math.exp(-0.3 * l)
            with ExitStack() as st:
                ps_s = psum(st, "psCs", [128, 4, 512], F32)
                ps_a = psum(st, "psCa", [128, 4, 512], F32)
                NPT = 6
                PT = [sbuf(st, "PTc", [128, 2, 512], BF16) for _ in range(NPT)]
                LD = [[sbuf(st, "LD", [128, 512], F32) for _ in range(2)] for _ in range(2)]
                LP = [[sbuf(st, "LP", [128, 512], F32) for _ in range(2)] for _ in range(2)]
                Lt = sbuf(st, "Lt", [128, 512], F32)
                Lhi = sbuf(st, "Lhi", [128, 512], BF16)
                Llo = sbuf(st, "Llo", [128, 512], BF16)
                l_free = [None, None]
                ident = sbuf(st, "ident", [128, 128], BF16)
                ones = sbuf(st, "ones", [128, 128], BF16)
                bC = sbuf(st, "bC", [128, 4, 128], BF16)
                KT = [[sbuf(st, "KTc", [72, NT], BF16) for _ in range(2)] for _ in range(2)]
                Vh = [sbuf(st, "Vc", [128, 64, 128], BF16) for _ in range(2)]
                QTt = [[[sbuf(st, "QTc", [72, 512], BF16) for _ in range(3)] for _ in range(2)] for _ in range(2)]
                sgt = [sbuf(st, "sgc", [128, 512], BF16) for _ in range(2)]
                r1 = sbuf(st, "r1", [128, 512], F32)
                o1 = sbuf(st, "o1", [128, 512], F32)
                o2 = sbuf(st, "o2", [128, 512], F32)
                oo = sbuf(st, "oo", [128, 512], F32)
                sq = sbuf(st, "sq", [128, 512], BF16)
                rstd = sbuf(st, "rstd", [128, 512], F32)
                stg = [sbuf(st, "stgC", [128, 512], BF16) for _ in range(2)]
                lam = sbuf(st, "lam", [128, 4, 64], F32)
                lsc = sbuf(st, "lsc", [128, 8], F32)
                coef = sbuf(st, "coef", [128, 1], F32)
                epsb = sbuf(st, "epsbC", [128, 1], F32)
                S_c = P.sem("C_c"); S_k = [P.sem("C_k0"), P.sem("C_k1")]; S_q = [P.sem("C_q0"), P.sem("C_q1")]
                S_st = [P.sem("C_st0"), P.sem("C_st1")]
                P.dma("sync", ident[:], c_id[:, :], sig=S_c)
                P.dma("sync", bC[:], c_bC.rearrange("p (h c) -> p h c", h=4), sig=S_c)
                for i, v in enumerate((lam_q1, lam_k1, lam_q2, lam_k2)):
                    P.dma("sync", lam[:, i, :], bcast_rows(v[l:l + 1, :], 128), sig=S_c)
                P.dma("sync", coef[:], subln_g[l:l + 1, :].rearrange("a d -> d a"), sig=S_c)
                t_c = (S_c, S_c.n)
                P.emit("vector", MSET(ones[:], 1.0))
                P.emit("vector", MSET(epsb[:], EPS))
                P.emit("vector", TT(lam[:, 0, :], lam[:, 0, :], lam[:, 1, :], ALU.mult), waits=[t_c])
                P.emit("vector", TT(lam[:, 2, :], lam[:, 2, :], lam[:, 3, :], ALU.mult))
                P.emit("vector", lambda e: e.reduce_sum(out=lsc[:, 0:1], in_=lam[:, 0, :], axis=AX.X))
                td = P.emit("vector", lambda e: e.reduce_sum(out=lsc[:, 1:2], in_=lam[:, 2, :], axis=AX.X), sig=S_dve)
                ta = P.emit("scalar", ACT(lsc[:, 2:4], lsc[:, 0:2], AF.Exp), waits=[td], sig=S_act)
                P.emit("vector", TT(lsc[:, 4:5], lsc[:, 3:4], lsc[:, 2:3], ALU.subtract), waits=[ta])
                P.emit("vector", TS(lsc[:, 5:6], lsc[:, 4:5], -lam_init, ALU.add))
                t_l = P.emit("vector", TS(coef[:], coef[:], 1.0 - lam_init, ALU.mult), sig=S_dve)
                neglam = lsc[:, 5:6]

                s_free = [None, None]
                pt_free = [None] * NPT
                a_free = [None] * 4
                k_free = [None, None]
                q_free = [None, None]
                sg_free = [None, None]
                st_free = [None, None]
                stores = []

                units = [(h, c) for h in range(4) for c in range(16)]
                loads = {}

                def load_head(h):
                    kb_ = h % 2
                    for m in range(2):
                        f0 = h * 128 + m * 64
                        P.dma("sync", KT[kb_][m][0:64, :], KcT[f0:f0 + 64, :], waits=[k_free[kb_]], sig=S_k[kb_])
                        P.dma("sync", KT[kb_][m][64:72, :], c_kaug[h, :, :], waits=[k_free[kb_]], sig=S_k[kb_])
                    vsrc = Vc[:, h * 128:(h + 1) * 128].rearrange("(j p) d -> p j d", p=128)
                    for q4 in range(4):
                        P.dma("sync", Vh[kb_][:, q4 * 16:(q4 + 1) * 16, :], vsrc[:, q4 * 16:(q4 + 1) * 16, :],
                              waits=[k_free[kb_]], sig=S_k[kb_])
                    loads[("k", h)] = (S_k[kb_], S_k[kb_].n)

                def load_unit(ui):
                    h, c = units[ui]
                    qb_ = ui % 2
                    csl = slice(c * 512, (c + 1) * 512)
                    for m in range(2):
                        f0 = h * 128 + m * 64
                        for ver in range(3):
                            P.dma("sync", QTt[qb_][m][ver][0:64, :], QcT[f0:f0 + 64, csl], waits=[q_free[qb_], sg_free[qb_]], sig=S_q[qb_])
                            P.dma("sync", QTt[qb_][m][ver][64:72, :], c_qaug[ver, :, csl], waits=[q_free[qb_]], sig=S_q[qb_])
                    P.dma("sync", sgt[qb_][:], SGc[h * 128:(h + 1) * 128, csl], waits=[q_free[qb_], sg_free[qb_]], sig=S_q[qb_])
                    loads[("q", ui)] = (S_q[qb_], S_q[qb_].n)

                def unit_range(h, c):
                    m_ = 2.0 ** (-2.0 * (h + 1))
                    js = []
                    for jb in range(64):
                        if jb < 4 * c:
                            dmin = 512 * c - (128 * jb + 127)
                        elif jb > 4 * c + 3:
                            dmin = 128 * jb - (512 * c + 511)
                        else:
                            dmin = 0
                        if m_ * dmin < C_SKIP:
                            js.append(jb)
                    return js[0] // 2, js[-1] // 2

                urange = [unit_range(h, c) for (h, c) in units]
                groups = [(ui, m, g) for ui in range(len(units)) for m in range(2)
                          for g in range(urange[ui][0], urange[ui][1] + 1)]
                NG = len(groups)
                qk_tok = {}
                exp_tok = {}
                pending_ss = []
                unit_state = {}

                def do_qk(G):
                    ui, m, g = groups[G]
                    h, c = units[ui]
                    kb_, qb_ = h % 2, ui % 2
                    sl = G % 2
                    w0 = [s_free[sl], loads[("k", h)], loads[("q", ui)], t_c, t_l]
                    tk = None
                    for mm_ in range(2):
                        jb = 2 * g + mm_
                        out = ps_s[:, sl * 2 + mm_, :]
                        lhsT = KT[kb_][m][0:72, jb * 128:(jb + 1) * 128]
                        Q = QTt[qb_][m]
                        sig = S_pe if mm_ == 1 else None
                        if jb < 4 * c:
                            tk = P.emit("tensor", MM(out, lhsT, Q[0][0:72, :]), waits=w0, sig=sig)
                        elif jb > 4 * c + 3:
                            tk = P.emit("tensor", MM(out, lhsT, Q[1][0:72, :]), waits=w0, sig=sig)
                        else:
                            a = jb - 4 * c
                            if a > 0:
                                P.emit("tensor", MM(ps_s[:, sl * 2 + mm_, 0:a * 128], lhsT, Q[1][0:72, 0:a * 128]), waits=w0)
                            P.emit("tensor", MM(ps_s[:, sl * 2 + mm_, a * 128:(a + 1) * 128], lhsT, Q[2][0:72, a * 128:(a + 1) * 128], True, False), waits=w0)
                            tk = P.emit("tensor", MM(ps_s[:, sl * 2 + mm_, a * 128:(a + 1) * 128], ident[:], bC[:, h, :], False, True),
                                        sig=sig if a == 3 else None)
                            if a < 3:
                                tk = P.emit("tensor", MM(ps_s[:, sl * 2 + mm_, (a + 1) * 128:512], lhsT, Q[0][0:72, (a + 1) * 128:512]),
                                            waits=w0, sig=sig)
                    qk_tok[G] = tk

                def do_exp(G):
                    sl = G % 2
                    pi = G % NPT
                    ta = P.emit("scalar", ACT(PT[pi][:], ps_s[:, sl * 2:sl * 2 + 2, :], AF.Exp),
                                waits=[qk_tok[G]] + list(pt_free[pi] or ()), sig=S_act, chain=False)
                    exp_tok[G] = ta
                    s_free[sl] = ta

                lstate = {}
                pending_L = []

                def do_av(G):
                    ui, m, g = groups[G]
                    h, c = units[ui]
                    kb_, qb_ = h % 2, ui % 2
                    pi = G % NPT
                    bo, bl_ = 2 * m, 2 * m + 1
                    tk = None
                    glo, ghi = urange[ui]
                    jfirst, jlast = 2 * glo, 2 * ghi + 1
                    for mm_ in range(2):
                        jb = 2 * g + mm_
                        w = [exp_tok[G]]
                        if jb == jfirst:
                            w += [a_free[bo], a_free[bl_]]
                        tk = P.emit("tensor", MM(ps_a[:, bo, :], Vh[kb_][:, jb, :], PT[pi][:, mm_, :], jb == jfirst, jb == jlast),
                                    waits=w, sig=S_pe if mm_ == 1 else None)
                        if mm_ == 0:
                            P.emit("tensor", MM(ps_a[:, bl_, :], ones[:], PT[pi][:, 0, :], jb == jfirst, False))
                    k = (2 * ui + m) % 2
                    stt = lstate.setdefault((ui, m), dict(tok={"vector": [None, None]}, cnt={"vector": 0}))
                    eng = "vector"
                    accs_ = LD[k]
                    n_ = stt["cnt"][eng]
                    par = n_ % 2
                    src = PT[pi][:, 1, :]
                    if n_ < 2:
                        tl = P.emit(eng, CP(accs_[par][:], src), waits=[exp_tok[G], l_free[k]], sig=S_dve, chain=False)
                    else:
                        tl = P.emit(eng, TT(accs_[par][:], src, accs_[par][:], ALU.add),
                                    waits=[exp_tok[G], stt["tok"][eng][par]], sig=S_dve, chain=False)
                    stt["tok"][eng][par] = tl
                    stt["cnt"][eng] = n_ + 1
                    pt_free[pi] = [tk, tl]
                    if g == ghi:
                        pending_L.append((ui, m, tk))
                        if m == 1:
                            q_free[qb_] = tk
                            if c == 15:
                                k_free[kb_] = tk

                def do_L():
                    ui, m, tk_o = pending_L.pop(0)
                    h, c = units[ui]
                    k = (2 * ui + m) % 2
                    bo, bl_ = 2 * m, 2 * m + 1
                    stt = lstate.pop((ui, m))
                    parts = []
                    for eng, accs_ in (("vector", LD[k]),):
                        for par in range(2):
                            if stt["cnt"][eng] > par:
                                parts.append((accs_[par], stt["tok"][eng][par]))
                    toks = [t_ for _, t_ in parts]
                    first_ = True
                    if len(parts) == 1:
                        P.emit("vector", CP(Lt[:], parts[0][0][:]), waits=toks)
                    else:
                        P.emit("vector", TT(Lt[:], parts[0][0][:], parts[1][0][:], ALU.add), waits=toks)
                        for (ap_, _) in parts[2:]:
                            P.emit("vector", TT(Lt[:], Lt[:], ap_[:], ALU.add))
                    P.emit("vector", CP(Lhi[:], Lt[:]))
                    td0 = P.emit("vector", TT(Llo[:], Lt[:], Lhi[:], ALU.subtract), sig=S_dve)
                    l_free[k] = td0
                    P.emit("tensor", MM(ps_a[:, bl_, :], ones[:], Lhi[:], False, False), waits=[td0])
                    tk = P.emit("tensor", MM(ps_a[:, bl_, :], ones[:], Llo[:], False, True), sig=S_pe)
                    if m == 0:
                        P.emit("vector", RCP(r1[:], ps_a[:, 1, :]), waits=[tk, tk_o])
                        td = P.emit("vector", TT(o1[:], ps_a[:, 0, :], r1[:], ALU.mult), sig=S_dve)
                        a_free[0] = td; a_free[1] = td
                    else:
                        P.emit("vector", RCP(r1[:], ps_a[:, 3, :]), waits=[tk, tk_o])
                        P.emit("vector", TT(o2[:], ps_a[:, 2, :], r1[:], ALU.mult))
                        P.emit("vector", STT(oo[:], o2[:], neglam, o1[:], ALU.mult, ALU.add), waits=[t_l])
                        td = P.emit("vector", TT(sq[:], oo[:], oo[:], ALU.mult), sig=S_dve)
                        a_free[3] = td
                        a_free[2] = td
                        pending_ss.append((td, ui))

                def do_ss():
                    td, ui = pending_ss.pop(0)
                    h, c = units[ui]
                    qb_ = ui % 2
                    csl = slice(c * 512, (c + 1) * 512)
                    tk = P.emit("tensor", MM(ps_a[:, 2, :], ones[:], sq[:]), waits=[td], sig=S_pe)
                    P.emit("scalar", ACT(rstd[:], ps_a[:, 2, :], AF.Ln, scale=1.0 / 128, bias=epsb[:, 0:1]), waits=[tk])
                    ta = P.emit("scalar", ACT(rstd[:], rstd[:], AF.Exp, scale=-0.5), sig=S_act)
                    a_free[2] = ta
                    si = ui % 2
                    P.emit("vector", STT(oo[:], oo[:], coef[:, 0:1], rstd[:], ALU.mult, ALU.mult), waits=[ta])
                    te = P.emit("vector", TT(stg[si][:], oo[:], sgt[qb_][:], ALU.mult), waits=[st_free[si], loads[("q", ui)]], sig=S_dve)
                    t_st = P.dma("gpsimd", OGc[h * 128:(h + 1) * 128, csl], stg[si][:], waits=[te], sig=S_st[si])
                    st_free[si] = t_st
                    stores.append(t_st)
                    sg_free[qb_] = te

                load_head(0)
                load_unit(0)
                load_unit(1)
                for step in range(NG + 2):
                    if step < NG:
                        ui, m, g = groups[step]
                        h, c = units[ui]
                        do_qk(step)
                        do_exp(step)
                        glo, ghi = urange[ui]
                        if g - glo == min(3, ghi - glo) and pending_L:
                            do_L()
                        if m == 0 and g - glo == min(6, ghi - glo) and pending_ss:
                            do_ss()
                        if m == 0 and g - glo == min(7, ghi - glo):
                            if ui >= 1 and ui + 1 < len(units):
                                load_unit(ui + 1)
                            if c == 8 and h + 1 < 4:
                                load_head(h + 1)
                    if step >= 2:
                        do_av(step - 2)
                while pending_L:
                    do_L()
                while pending_ss:
                    do_ss()
                for e_ in ("sync", "gpsimd", "scalar", "vector", "tensor"):
                    P.wait_sems(e_, S_st)
                P.flush(f"C{l}")

        phase_cast()
        done = False
        for l in range(depth):
            for nm, ph in (("T", phase_T), ("B", phase_B), ("A", phase_A), ("C", phase_C2)):
                ph(l)
                if stop_after == f"{nm}{l}":
                    done = True
                    break
            if done:
                break
        if not done:
            phase_T(depth)
    return nc


_CACHE = {}
_RUN_KW = {}


def kernel(x_prompt, x_sample, norm_g, w_in, w_oa, w_ob, w_oc, w_out, b_sink,
           lam_q1, lam_k1, lam_q2, lam_k2, c_subln_g, final_norm_g):
    f = lambda a: np.ascontiguousarray(np.asarray(a, dtype=np.float32))
    x_prompt = f(x_prompt); x_sample = f(x_sample)
    consts = make_consts()
    seg_p = np.zeros(NT, np.int64)
    seg_s = np.arange(NT) // 2048
    qa_p, ka_p = make_aug(seg_p, False)
    qa_s, ka_s = make_aug(seg_s, True)
    shared = dict(norm_g=f(norm_g), w_in=f(w_in), w_oa=f(w_oa), w_ob=f(w_ob), w_oc=f(w_oc), w_out=f(w_out),
                  b_sink=f(b_sink), lam_q1=f(lam_q1), lam_k1=f(lam_k1), lam_q2=f(lam_q2), lam_k2=f(lam_k2),
                  c_subln_g=f(c_subln_g), final_norm_g=f(final_norm_g).reshape(1, D), **consts)
    PROMPT_CORES = (0, 4)
    SAMPLE_CORES = (1, 2, 5, 6)
    zeros = np.zeros((NT, D), np.float32)
    in_maps = []
    for c in range(NCORES):
        if c in PROMPT_CORES:
            xs = x_prompt[PROMPT_CORES.index(c)]
            qa, ka = qa_p, ka_p
        elif c in SAMPLE_CORES:
            i = SAMPLE_CORES.index(c)
            xs = x_sample[4 * i:4 * i + 4].reshape(NT, D)
            qa, ka = qa_s, ka_s
        else:
            xs = zeros
            qa, ka = qa_p, ka_p
        in_maps.append(dict(x=np.ascontiguousarray(xs), c_qaug=qa, c_kaug=ka, **shared))
    if "nc" not in _CACHE:
        _CACHE["nc"] = build()
    res = run_bass_kernel_spmd(_CACHE["nc"], in_maps, core_ids=list(range(NCORES)), **_RUN_KW)
    _CACHE["res"] = res
    ys = [np.asarray(r["y"], dtype=np.float32) for r in res.results]
    y_prompt = np.stack([ys[c] for c in PROMPT_CORES], 0)
    y_sample = np.concatenate([ys[c].reshape(4, 2048, D) for c in SAMPLE_CORES], 0)
    return (y_prompt, y_sample)
```
